# Optimizing a Trainium2 kernel written in Bass

```python
import jax, jax.numpy as jnp
from jax import lax
import numpy as np

D_MODEL = 4096
BATCH = 4
SEQ = 2048
DEPTH = 1
DEC_BATCH = 1
DEC_SEQ = 8192
PAST_LEN = 128

ATT_PATTERNS = ((128, 1), (512, 4), (2048, 16))
N_GROUPS = 3
ATT_HEAD_DIM = 128
ATT_HEADS = D_MODEL // 512
ATT_WIDTH = ATT_HEADS * ATT_HEAD_DIM
GLA_HEADS = 4
GLA_KEY = D_MODEL // 4
GLA_VAL = D_MODEL // 2
GLA_DK = GLA_KEY // GLA_HEADS
GLA_DV = GLA_VAL // GLA_HEADS
GLA_RANK = 16
GLA_TAU = 16.0
GLA_CHUNK = 64
NORM_EPS = 1e-6
NEG_INF = -1e30

IN_SPLITS = (3 * N_GROUPS * ATT_WIDTH,
             ATT_WIDTH,
             GLA_KEY, GLA_KEY, GLA_VAL,
             GLA_VAL,
             GLA_RANK, GLA_RANK,
             D_MODEL, D_MODEL)
IN_COLS = sum(IN_SPLITS)

kernel_name = "hybrid_dilated_gla_encoder"


def rms_norm(x, g):
    xf = x.astype(jnp.float32)
    y = xf * lax.rsqrt(jnp.mean(xf * xf, axis=-1, keepdims=True) + NORM_EPS)
    return (y * g.astype(jnp.float32)).astype(x.dtype)


def alibi_slopes(n_heads):
    return jnp.exp2(-8.0 * (jnp.arange(n_heads, dtype=jnp.float32) + 1.0) / n_heads)


def dilated_group_attention(q, k, v, window, dilation, slopes):
    B, S, H, hd = q.shape
    R = window // (2 * dilation)
    L = S // dilation
    nb = -(-L // R)
    Lp = nb * R

    def classes(t):
        return t.reshape(B, L, dilation, H, hd).transpose(0, 2, 1, 3, 4)

    qc, kc, vc = classes(q), classes(k), classes(v)
    qb = jnp.pad(qc, ((0, 0), (0, 0), (0, Lp - L), (0, 0), (0, 0))).reshape(B, dilation, nb, R, H, hd)
    kv_pad = ((0, 0), (0, 0), (R, Lp - L + R), (0, 0), (0, 0))
    kp = jnp.pad(kc, kv_pad).reshape(B, dilation, nb + 2, R, H, hd)
    vp = jnp.pad(vc, kv_pad).reshape(B, dilation, nb + 2, R, H, hd)
    kn = jnp.concatenate([kp[:, :, 0:nb], kp[:, :, 1:nb + 1], kp[:, :, 2:nb + 2]], axis=3)
    vn = jnp.concatenate([vp[:, :, 0:nb], vp[:, :, 1:nb + 1], vp[:, :, 2:nb + 2]], axis=3)

    s = jnp.einsum('bdnqhc,bdnkhc->bdnhqk', qb, kn,
                   preferred_element_type=jnp.float32) * (hd ** -0.5)
    qi = jnp.arange(R)[:, None]
    ki = jnp.arange(3 * R)[None, :]
    delta = ki - R - qi
    kpos = jnp.arange(nb)[:, None, None] * R - R + ki[None]
    valid = (jnp.abs(delta) <= R)[None] & (kpos >= 0) & (kpos < L)
    dist = (jnp.abs(delta) * dilation).astype(jnp.float32)
    bias = -slopes[:, None, None] * dist[None]
    s = s + bias[None, None, None]
    s = jnp.where(valid[None, None, :, None], s, NEG_INF)
    lse = jax.nn.logsumexp(s, axis=-1)
    p = jnp.exp(s - lse[..., None])
    o = jnp.einsum('bdnhqk,bdnkhc->bdnqhc', p, vn.astype(jnp.float32))
    o = o.reshape(B, dilation, Lp, H, hd)[:, :, :L].transpose(0, 2, 1, 3, 4).reshape(B, S, H, hd)
    lse = lse.transpose(0, 1, 2, 4, 3).reshape(B, dilation, Lp, H)[:, :, :L]
    lse = lse.transpose(0, 2, 1, 3).reshape(B, S, H)
    return o, lse


def dilated_mixture(qkv, B, S):
    qkv = qkv.reshape(B, S, N_GROUPS, 3, ATT_HEADS, ATT_HEAD_DIM)
    slopes = alibi_slopes(ATT_HEADS)
    outs, lses = [], []
    for g, (window, dilation) in enumerate(ATT_PATTERNS):
        o, l = dilated_group_attention(qkv[:, :, g, 0], qkv[:, :, g, 1], qkv[:, :, g, 2],
                                       window, dilation, slopes)
        outs.append(o)
        lses.append(l)
    o = jnp.stack(outs, axis=0)
    l = jnp.stack(lses, axis=0)
    wts = jax.nn.softmax(l, axis=0)
    return jnp.sum(wts[..., None] * o, axis=0).reshape(B, S, ATT_WIDTH)


def gla_direction(q, k, v, log_a, strict):
    B, S, H, dk = q.shape
    dv = v.shape[-1]
    C = GLA_CHUNK
    n = S // C

    def r(t):
        return t.reshape(B, n, C, H, t.shape[-1])

    q, k, v, log_a = r(q), r(k), r(v), r(log_a)
    b = jnp.cumsum(log_a, axis=2)
    qe = q * jnp.exp(b)
    ke = k * jnp.exp(-b)
    b_last = b[:, :, -1]
    kd = k * jnp.exp(b_last[:, :, None] - b)
    A = jnp.einsum('bnqhc,bnkhc->bnhqk', qe, ke)
    mask = jnp.tril(jnp.ones((C, C), dtype=bool), -1 if strict else 0)
    A = jnp.where(mask, A, 0.0)
    o_intra = jnp.einsum('bnhqk,bnkhv->bnqhv', A, v)

    def step(state, inp):
        qe_c, kd_c, v_c, bl = inp
        o = jnp.einsum('bqhc,bhcv->bqhv', qe_c, state)
        state = jnp.exp(bl)[..., None] * state + jnp.einsum('bkhc,bkhv->bhcv', kd_c, v_c)
        return state, o

    xs = (jnp.moveaxis(qe, 1, 0), jnp.moveaxis(kd, 1, 0), jnp.moveaxis(v, 1, 0),
          jnp.moveaxis(b_last, 1, 0))
    s0 = jnp.zeros((B, H, dk, dv), jnp.float32)
    _, o_inter = lax.scan(step, s0, xs)
    o = o_intra + jnp.moveaxis(o_inter, 0, 1)
    return o.reshape(B, S, H, dv)


def gla_mixer(q, k, v, z_f, z_b, w2_f, b_f, w2_b, b_b, out_g, B, S):
    f32 = jnp.float32
    q = q.astype(f32).reshape(B, S, GLA_HEADS, GLA_DK) * (GLA_DK ** -0.5)
    k = k.astype(f32).reshape(B, S, GLA_HEADS, GLA_DK)
    v = v.astype(f32).reshape(B, S, GLA_HEADS, GLA_DV)
    la_f = jax.nn.log_sigmoid(z_f.astype(f32) @ w2_f.astype(f32) + b_f.astype(f32)) / GLA_TAU
    la_b = jax.nn.log_sigmoid(z_b.astype(f32) @ w2_b.astype(f32) + b_b.astype(f32)) / GLA_TAU
    la_f = la_f.reshape(B, S, GLA_HEADS, GLA_DK)
    la_b = la_b.reshape(B, S, GLA_HEADS, GLA_DK)
    o_f = gla_direction(q, k, v, la_f, False)
    flip = lambda t: jnp.flip(t, axis=1)
    o_b = flip(gla_direction(flip(q), flip(k), flip(v), flip(la_b), True))
    o = o_f + o_b
    o = o * lax.rsqrt(jnp.mean(o * o, axis=-1, keepdims=True) + NORM_EPS) * out_g.astype(f32)
    return o.reshape(B, S, GLA_VAL)


def encoder_layer(x, norm_g, w_in, gla_w2_f, gla_b_f, gla_w2_b, gla_b_b, gla_norm_g,
                  w_up_a, w_up_b, w_o):
    B, S, _ = x.shape
    h = rms_norm(x, norm_g)
    proj = jnp.einsum('bsd,de->bse', h, w_in)
    splits = [int(i) for i in np.cumsum(IN_SPLITS)[:-1]]
    (att_qkv, att_z, gq, gk, gv, gz, lr_f, lr_b, mg_a, mg_b) = jnp.split(proj, splits, axis=-1)
    ya = dilated_mixture(att_qkv, B, S) * jax.nn.silu(att_z.astype(jnp.float32))
    yb = gla_mixer(gq, gk, gv, lr_f, lr_b, gla_w2_f, gla_b_f, gla_w2_b, gla_b_b,
                   gla_norm_g, B, S) * jax.nn.silu(gz.astype(jnp.float32))
    ua = jnp.einsum('bse,ed->bsd', ya.astype(x.dtype), w_up_a)
    ub = jnp.einsum('bse,ed->bsd', yb.astype(x.dtype), w_up_b)
    merged = jax.nn.sigmoid(mg_a) * ua + jax.nn.sigmoid(mg_b) * ub
    return x + jnp.einsum('bsd,de->bse', merged, w_o)


def setup_inputs(seed: int = 0) -> dict:
    key = jax.random.key(seed)
    ks = jax.random.split(key, 16)
    f32 = jnp.float32
    nrm = lambda k, shape, scale: jax.random.normal(k, shape, f32) * scale
    return {
        "x_prompt": jax.random.normal(ks[0], (BATCH, SEQ, D_MODEL), f32),
        "x_sample": jax.random.normal(ks[1], (DEC_BATCH, DEC_SEQ, D_MODEL), f32),
        "norm_g": 1.0 + nrm(ks[2], (DEPTH, D_MODEL), 0.02),
        "w_in": nrm(ks[3], (DEPTH, D_MODEL, IN_COLS), D_MODEL ** -0.5),
        "gla_w2_f": nrm(ks[4], (DEPTH, GLA_RANK, GLA_KEY), GLA_RANK ** -0.5),
        "gla_b_f": nrm(ks[5], (DEPTH, GLA_KEY), 0.1),
        "gla_w2_b": nrm(ks[6], (DEPTH, GLA_RANK, GLA_KEY), GLA_RANK ** -0.5),
        "gla_b_b": nrm(ks[7], (DEPTH, GLA_KEY), 0.1),
        "gla_norm_g": 1.0 + nrm(ks[8], (DEPTH, GLA_DV), 0.02),
        "w_up_a": nrm(ks[9], (DEPTH, ATT_WIDTH, D_MODEL), ATT_WIDTH ** -0.5),
        "w_up_b": nrm(ks[10], (DEPTH, GLA_VAL, D_MODEL), GLA_VAL ** -0.5),
        "w_o": nrm(ks[11], (DEPTH, D_MODEL, D_MODEL), D_MODEL ** -0.5),
        "final_norm_g": 1.0 + nrm(ks[12], (D_MODEL,), 0.02),
    }


def reference(x_prompt, x_sample, norm_g, w_in, gla_w2_f, gla_b_f, gla_w2_b, gla_b_b,
              gla_norm_g, w_up_a, w_up_b, w_o, final_norm_g):
    def run(x):
        for l in range(DEPTH):
            x = encoder_layer(x, norm_g[l], w_in[l], gla_w2_f[l], gla_b_f[l], gla_w2_b[l],
                              gla_b_b[l], gla_norm_g[l], w_up_a[l], w_up_b[l], w_o[l])
        return rms_norm(x, final_norm_g)

    y_prompt = run(x_prompt)
    y_sample = run(x_sample)
    return (y_prompt, y_sample)
```

```python
import numpy as np
import concourse.bass as bass
import concourse.mybir as mybir
from concourse.bass_utils import run_bass_kernel_spmd

F32 = mybir.dt.float32
BF16 = mybir.dt.bfloat16
AF = mybir.ActivationFunctionType
ALU = mybir.AluOpType

D = 4096
T = 2048
NCTX = 6144
EPS = 1e-6
NBLK = 193
DILS = (1, 4, 16)
ARENA0 = 16640
ARENA1 = 229376


def att_blk(hs, g, j):
    return (hs * 3 + g) * 3 + j


B_ATTZ = 72
B_GQ = 80
B_GK = 88
B_GV = 96
B_GZ = 112
B_LR = 128
B_MGA = 129
B_MGB = 161


def block_cols():
    cols = []
    for hs in range(8):
        for g in range(3):
            for j in range(3):
                cols.append((((g * 3 + j) * 8 + hs) * 128, 128))
    for hs in range(8):
        cols.append((9216 + hs * 128, 128))
    for i in range(8):
        cols.append((10240 + i * 128, 128))
    for i in range(8):
        cols.append((11264 + i * 128, 128))
    for i in range(16):
        cols.append((12288 + i * 128, 128))
    for i in range(16):
        cols.append((14336 + i * 128, 128))
    cols.append((16384, 32))
    for i in range(32):
        cols.append((16416 + i * 128, 128))
    for i in range(32):
        cols.append((20512 + i * 128, 128))
    assert len(cols) == NBLK
    return cols


class Sched:
    def __init__(self, nc, n_dma_slots=40, epoch=30000):
        self.nc = nc
        self.names = ["pe", "act", "dve", "pool", "sp"]
        self.prog = {k: [] for k in self.names}
        self.epoch = epoch
        self.esem = {}
        self.ecount = {}
        self.nsem = 0
        for k in self.names:
            self._new_esem(k)
        self.dslots = [[self._sem("dma%d" % i), 0] for i in range(n_dma_slots)]
        self.dnext = 0
        self.seen = {k: {} for k in self.names}
        self.lastw = {}
        self.readers = {}
        self.nops = 0

    def _sem(self, name):
        self.nsem += 1
        return self.nc.alloc_semaphore(name="s_%s_%d" % (name, self.nsem))

    def _new_esem(self, k):
        self.esem[k] = self._sem(k)
        self.ecount[k] = 0

    def _wait(self, e, tok):
        sem, val = tok
        sid = id(sem)
        if self.seen[e].get(sid, 0) >= val:
            return
        self.seen[e][sid] = val
        self.prog[e].append(("wait", sem, val))

    def _deps(self, e, reads, writes):
        for r in reads:
            t = self.lastw.get(r)
            if t is not None:
                self._wait(e, t)
        for w in writes:
            t = self.lastw.get(w)
            if t is not None:
                self._wait(e, t)
            for t in self.readers.get(w, ()):
                self._wait(e, t)

    def _commit(self, tok, reads, writes):
        for w in writes:
            self.lastw[w] = tok
            self.readers[w] = []
        for r in reads:
            if r in writes:
                continue
            self.readers.setdefault(r, []).append(tok)

    def op(self, e, fn, reads=(), writes=()):
        self._deps(e, reads, writes)
        if self.ecount[e] >= self.epoch:
            self._new_esem(e)
        self.ecount[e] += 1
        tok = (self.esem[e], self.ecount[e])
        self.prog[e].append(("op", fn, tok[0], 1))
        self._commit(tok, reads, writes)
        self.nops += 1
        return tok

    def dma(self, e, fn, reads=(), writes=()):
        self._deps(e, reads, writes)
        slot = self.dslots[self.dnext]
        self.dnext = (self.dnext + 1) % len(self.dslots)
        if slot[1] > 0:
            self._wait(e, (slot[0], slot[1]))
        slot[1] += 16
        tok = (slot[0], slot[1])
        self.prog[e].append(("op", fn, tok[0], 16))
        self._commit(tok, reads, writes)
        self.nops += 1
        return tok

    def barrier(self):
        for e in self.names:
            for s in self.dslots:
                if s[1] > 0:
                    self._wait(e, (s[0], s[1]))
            for k in self.names:
                if k != e and self.ecount[k] > 0:
                    self._wait(e, (self.esem[k], self.ecount[k]))
        for e in self.names:
            if self.ecount[e] > 0:
                self._wait(e, (self.esem[e], self.ecount[e]))
        self.lastw = {}
        self.readers = {}

    def emit(self):
        nc = self.nc
        with nc.Block() as block:
            def mk(k):
                def body(engine):
                    for item in self.prog[k]:
                        if item[0] == "wait":
                            engine.wait_ge(item[1], item[2])
                        else:
                            ins = item[1](engine)
                            ins.then_inc(item[2], item[3])
                return body
            block.tensor(mk("pe"))
            block.scalar(mk("act"))
            block.vector(mk("dve"))
            block.gpsimd(mk("pool"))
            block.sync(mk("sp"))


class Arena:
    def __init__(self, nc):
        self.nc = nc
        self.base = ARENA0
        self.ptr = ARENA0
        self.n = 0

    def mark(self):
        return self.ptr

    def reset(self, mark):
        self.ptr = mark

    def alloc(self, shape, dtype, name="t"):
        esz = 4 if dtype == F32 else 2
        per = esz
        for s in shape[1:]:
            per *= s
        off = (self.ptr + 63) // 64 * 64
        assert off + per <= ARENA1, ("SBUF arena overflow", name, off, per)
        self.ptr = off + per
        self.n += 1
        return self.nc.alloc_sbuf_tensor_at("%s_%d" % (name, self.n), list(shape), dtype, offset=off)


def build_program(dbg=False):
    nc = bass.Bass("TRN2", target_bir_lowering=False)
    S = Sched(nc)
    A = Arena(nc)

    def din(name, shape, dt=F32):
        return nc.dram_tensor(name, list(shape), dt, kind="ExternalInput").ap()

    def dscr(name, shape, dt):
        if dbg and name in ("yaT_d", "ybT_d", "gk_d", "gqT_d", "gv_d"):
            return nc.dram_tensor(name, list(shape), dt, kind="ExternalOutput").ap()
        return nc.dram_tensor(name, list(shape), dt).ap()

    x_own = din("x_own", [T, D])
    x_halo = din("x_halo", [T, D])
    x_ctx = din("x_ctx", [NCTX, D])
    w_blk = din("w_blk", [NBLK, 128, 32, 128])
    wua = din("wua", [32, 128, 8, 128])
    wub = din("wub", [32, 128, 16, 128])
    wo = din("wo", [16, 128, 32, 256])
    ng_b = din("ng_b", [128, D])
    fng_b = din("fng_b", [128, D])
    gng_b = din("gng_b", [64, 512])
    w2aug = din("w2aug", [2, 33, 1024])
    cw2aug = din("cw2aug", [3, 33, 1024])
    w_clr = din("w_clr", [3, 128, 32, 16])
    att_bias = din("att_bias", [3, 8, 128, 256])
    att_kb = din("att_kb", [128, 69])
    bnd = din("bnd", [128, 8])
    tri = din("tri", [4, 64, 64])
    msk = din("msk", [2, 64, 64])
    ident_in = din("ident_in", [128, 128])
    y_out = nc.dram_tensor("y_out", [T, D], F32, kind="ExternalOutput").ap()

    hT_d = dscr("hT_d", [20, 128, 32, 512], BF16)
    yaT_d = dscr("yaT_d", [1024, T], BF16)
    ybT_d = dscr("ybT_d", [2048, T], BF16)
    gqT_d = dscr("gqT_d", [1024, T], F32)
    gkT_d = dscr("gkT_d", [1024, T], F32)
    lrT_d = dscr("lrT_d", [32, T], F32)
    gk_d = dscr("gk_d", [T, 1024], F32)
    gv_d = dscr("gv_d", [T, 2048], BF16)
    gzs_d = dscr("gzs_d", [T, 2048], F32)
    ck_d = dscr("ck_d", [NCTX, 1024], F32)
    cv_d = dscr("cv_d", [NCTX, 2048], BF16)
    clrT_d = dscr("clrT_d", [3, 16, T], F32)

    psb = [nc.alloc_psum_tensor("psb%d" % i, [128, 512], F32) for i in range(7)]
    ptb = [nc.alloc_psum_tensor("ptb%d" % i, [128, 1024], BF16) for i in range(1)]
    rot = {"ps": 0, "pt": 0}

    def next_ps(lo=0, hi=7):
        i = lo + rot["ps"] % (hi - lo)
        rot["ps"] += 1
        return i

    ident = A.alloc([128, 128], BF16, "ident")
    ones_b = A.alloc([128, 128], BF16, "ones")
    bnd_t = A.alloc([128, 8], F32, "bnd")
    S.dma("pool", lambda e: e.dma_start(out=ident[:], in_=ident_in), writes=["ident"])
    S.dma("sp", lambda e: e.dma_start(out=bnd_t[:], in_=bnd), writes=["bnd"])
    S.op("dve", lambda e: e.memset(ones_b[:], 1.0), writes=["ones"])
    base_mark = A.mark()

    gb = A.alloc([128, D], F32, "gb")
    S.dma("sp", lambda e: e.dma_start(out=gb[:], in_=ng_b), writes=["gb"])
    xt = [A.alloc([128, D], F32, "xt") for _ in range(2)]
    hb = [A.alloc([128, D], BF16, "hb") for _ in range(2)]
    junk = A.alloc([128, D], BF16, "junk")
    hTt = [A.alloc([128, 32, 512], BF16, "hTt") for _ in range(2)]
    stat = [A.alloc([128, 4], F32, "stat") for _ in range(2)]
    srcs = [(x_own, 4), (x_halo, 4), (x_ctx, 12)]
    tile_id = 0
    it = 0
    for src, nt in srcs:
        for t5 in range(nt):
            hp = tile_id % 2
            for sub in range(4):
                b = it % 2
                r0 = t5 * 512 + sub * 128
                S.dma("sp", lambda e, b=b, r0=r0, src=src: e.dma_start(out=xt[b][:], in_=src[r0:r0 + 128, :]),
                      writes=[("xt", b)])
                S.op("dve", lambda e, b=b: e.memset(stat[b][:, 0:1], 0.0), writes=[("st0", b)])
                S.op("act", lambda e, b=b: e.activation(out=junk[:], in_=xt[b][:], func=AF.Square,
                                                          accum_out=stat[b][:, 0:1]),
                     reads=[("xt", b)], writes=["junk", ("st0", b)])
                S.op("dve", lambda e, b=b: e.tensor_scalar(out=stat[b][:, 1:2], in0=stat[b][:, 0:1],
                                                            scalar1=1.0 / D, scalar2=EPS, op0=ALU.mult, op1=ALU.add),
                     reads=[("st0", b)], writes=[("st1", b)])
                S.op("act", lambda e, b=b: e.activation(out=stat[b][:, 2:3], in_=stat[b][:, 1:2], func=AF.Sqrt),
                     reads=[("st1", b)], writes=[("st2", b)])
                S.op("dve", lambda e, b=b: e.reciprocal(out=stat[b][:, 3:4], in_=stat[b][:, 2:3]),
                     reads=[("st2", b)], writes=[("st3", b)])
                S.op("dve", lambda e, b=b: e.scalar_tensor_tensor(out=hb[b][:], in0=xt[b][:], scalar=stat[b][:, 3:4],
                                                                   in1=gb[:], op0=ALU.mult, op1=ALU.mult),
                     reads=[("xt", b), ("st3", b), "gb"], writes=[("hb", b)])
                for q4 in range(4):
                    pb = 0
                    rot["pt"] += 1

                    def tr(e, b=b, q4=q4, pb=pb):
                        ins = None
                        for k in range(8):
                            kc = q4 * 8 + k
                            ins = e.transpose(out=ptb[pb][:, k * 128:(k + 1) * 128],
                                              in_=hb[b][:, kc * 128:(kc + 1) * 128], identity=ident[:])
                        return ins
                    S.op("pe", tr, reads=[("hb", b), "ident"], writes=[("pt", pb)])
                    eng = "act" if q4 % 2 == 0 else "dve"

                    def ev(e, hp=hp, q4=q4, pb=pb, sub=sub, eng=eng):
                        o = hTt[hp][:, q4 * 8:(q4 + 1) * 8, sub * 128:(sub + 1) * 128]
                        i = ptb[pb][:].rearrange("p (k t) -> p k t", k=8)
                        if eng == "act":
                            return e.copy(out=o, in_=i)
                        return e.tensor_copy(out=o, in_=i)
                    S.op(eng, ev, reads=[("pt", pb)], writes=[("hTt", hp, sub, q4)])
                it += 1
            S.dma("sp", lambda e, hp=hp, tile_id=tile_id: e.dma_start(out=hT_d[tile_id], in_=hTt[hp][:]),
                  reads=[("hTt", hp, s_, q_) for s_ in range(4) for q_ in range(4)], writes=[("hT_d", tile_id)])
            tile_id += 1
    S.barrier()
    A.reset(base_mark)

    def load_h(buf, key, pieces):
        c = 0
        for (tl, c0, n) in pieces:
            S.dma("sp", lambda e, tl=tl, c0=c0, n=n, c=c: e.dma_start(out=buf[:, :, c:c + n],
                                                                      in_=hT_d[tl, :, :, c0:c0 + n]),
                  reads=[("hT_d", tl)], writes=[key])
            c += n
        return c

    def mm_fm(ps_ap, wbuf, hbuf, n, nk=32, wsl=None):
        def f(e):
            ins = None
            for kc in range(nk):
                lw = wbuf[:, kc, :] if wsl is None else wbuf[:, kc, wsl[0]:wsl[1]]
                ins = e.matmul(ps_ap, lhsT=lw, rhs=hbuf[:, kc, 0:n], start=(kc == 0), stop=(kc == nk - 1))
            return ins
        return f

    def mm_tm(ps_ap, hbuf, t0, wbuf, ncols, nk=32):
        def f(e):
            ins = None
            for kc in range(nk):
                ins = e.matmul(ps_ap, lhsT=hbuf[:, kc, t0:t0 + 128], rhs=wbuf[:, kc, 0:ncols],
                               start=(kc == 0), stop=(kc == nk - 1))
            return ins
        return f

    hbuf = [A.alloc([128, 32, 512], BF16, "hbuf") for _ in range(2)]
    wbuf = [[A.alloc([128, 32, 128], BF16, "wbuf") for _ in range(3)] for _ in range(2)]
    qT = A.alloc([128, T], BF16, "qT")
    kT = A.alloc([128, 4096], BF16, "kT")
    vT = A.alloc([128, 4096], BF16, "vT")
    accO = A.alloc([128, T], F32, "accO")
    accR = A.alloc([128, T], F32, "accR")
    tbias = [A.alloc([128, 256], F32, "tbias") for _ in range(2)]
    kb_t = A.alloc([128, 69], F32, "kb")
    sbuf_s = [A.alloc([128, 256], F32, "sb") for _ in range(2)]
    pT = [A.alloc([128, 256], BF16, "pT") for _ in range(2)]
    vt = [A.alloc([128, 128], BF16, "vt") for _ in range(2)]
    zs = [A.alloc([128, 512], F32, "zs") for _ in range(2)]
    rr = [A.alloc([128, 512], F32, "rr") for _ in range(2)]
    yst = [A.alloc([128, 512], BF16, "yst") for _ in range(2)]
    S.dma("sp", lambda e: e.dma_start(out=kb_t[:], in_=att_kb), writes=["kb"])
    SCALE = 128.0 ** -0.5
    hctr = 0
    wset = 0
    kbcol0 = [0, 17, 37]
    for hs in range(8):
        S.op("pool", lambda e: e.memset(accO[:], 0.0), writes=["accO"])
        S.op("pool", lambda e: e.memset(accR[:], 0.0), writes=["accR"])
        for g in range(3):
            d = DILS[g]
            hw = 64 * d
            NT = T + 2 * hw
            tb = tbias[g % 2] if False else tbias[(hs * 3 + g) % 2]
            tbk = ("tbias", (hs * 3 + g) % 2)
            S.dma("sp", lambda e, g=g, hs=hs, tb=tb: e.dma_start(out=tb[:], in_=att_bias[g, hs]), writes=[tbk])
            ws = wset % 2
            wset += 1
            for j in range(3):
                S.dma("pool", lambda e, ws=ws, j=j, hs=hs, g=g: e.dma_start(out=wbuf[ws][j][:],
                                                                           in_=w_blk[att_blk(hs, g, j)]),
                      writes=[("w", ws, j)])
            jobs = []
            if hw == 1024:
                jobs.append(([(4, 0, 512)], 0, 512, False, 0))
                jobs.append(([(5, 0, 512)], 512, 512, False, 0))
            else:
                jobs.append(([(5, 512 - hw, hw)], 0, hw, False, 0))
            for i in range(4):
                jobs.append(([(i, 0, 512)], hw + 512 * i, 512, True, 512 * i))
            if hw == 1024:
                jobs.append(([(6, 0, 512)], hw + T, 512, False, 0))
                jobs.append(([(7, 0, 512)], hw + T + 512, 512, False, 0))
            else:
                jobs.append(([(6, 0, hw)], hw + T, hw, False, 0))
            qkv_keys = []
            for (pieces, edst, n, has_q, t0) in jobs:
                hb_i = hctr % 2
                hctr += 1
                load_h(hbuf[hb_i], ("h", hb_i), pieces)
                for j in ([0, 1, 2] if has_q else [1, 2]):
                    pi = next_ps(0, 3)
                    S.op("pe", mm_fm(psb[pi][:, 0:n], wbuf[ws][j], hbuf[hb_i], n),
                         reads=[("w", ws, j), ("h", hb_i)], writes=[("ps", pi)])
                    if j == 0:
                        dst, dk_ = qT[:, t0:t0 + n], ("qT", t0 // 512)
                    elif j == 1:
                        dst, dk_ = kT[:, edst:edst + n], ("kT", edst // 512 if n == 512 else "h%d" % edst)
                    else:
                        dst, dk_ = vT[:, edst:edst + n], ("vT", edst // 512 if n == 512 else "h%d" % edst)
                    eng = "act" if j != 1 else "dve"
                    qkv_keys.append(dk_)
                    if eng == "act":
                        S.op("act", lambda e, dst=dst, pi=pi, n=n: e.copy(out=dst, in_=psb[pi][:, 0:n]),
                             reads=[("ps", pi)], writes=[dk_])
                    else:
                        S.op("dve", lambda e, dst=dst, pi=pi, n=n: e.tensor_copy(out=dst, in_=psb[pi][:, 0:n]),
                             reads=[("ps", pi)], writes=[dk_])
            L = T // d
            nq = L // 128
            kTv = kT[:, 0:NT].rearrange("p (u r) -> p u r", r=d)
            vTv = vT[:, 0:NT].rearrange("p (u r) -> p u r", r=d)
            qTv = qT[:].rearrange("p (u r) -> p u r", r=d)
            aOv = accO[:].rearrange("p (u r) -> p u r", r=d)
            aRv = accR[:].rearrange("p (u r) -> p u r", r=d)
            ctr = 0
            for r in range(d):
                for jk in range(nq + 1):
                    b2 = ctr % 2
                    ctr += 1
                    qlo = max(jk - 1, 0)
                    qhi = min(jk + 1, nq)
                    nqq = (qhi - qlo) * 128
                    c0 = 0 if jk >= 1 else 128
                    pi = next_ps(0, 3)
                    S.op("pe", lambda e, pi=pi, jk=jk, r=r, qlo=qlo, nqq=nqq, kTv=kTv, qTv=qTv: e.matmul(
                        psb[pi][:, 0:nqq], lhsT=kTv[:, jk * 128:(jk + 1) * 128, r],
                        rhs=qTv[:, qlo * 128:qlo * 128 + nqq, r], start=True, stop=True),
                        reads=qkv_keys, writes=[("ps", pi)])
                    S.op("dve", lambda e, pi=pi, b2=b2, nqq=nqq, c0=c0, tb=tb: e.scalar_tensor_tensor(
                        out=sbuf_s[b2][:, 0:nqq], in0=psb[pi][:, 0:nqq], scalar=SCALE, in1=tb[:, c0:c0 + nqq],
                        op0=ALU.mult, op1=ALU.add), reads=[("ps", pi), tbk], writes=[("sb", b2)])
                    kc_ = kbcol0[g] + r * (nq + 1) + jk
                    S.op("act", lambda e, b2=b2, nqq=nqq, kc_=kc_: e.activation(
                        out=pT[b2][:, 0:nqq], in_=sbuf_s[b2][:, 0:nqq], func=AF.Exp, bias=kb_t[:, kc_:kc_ + 1],
                        scale=1.0), reads=[("sb", b2), "kb"], writes=[("pT", b2)])
                    pb = 0
                    rot["pt"] += 1
                    S.op("pe", lambda e, pb=pb, jk=jk, r=r, vTv=vTv: e.transpose(
                        out=ptb[pb][:, 0:128], in_=vTv[:, jk * 128:(jk + 1) * 128, r], identity=ident[:]),
                        reads=qkv_keys + ["ident"], writes=[("pt", pb)])
                    S.op("act", lambda e, pb=pb, b2=b2: e.copy(out=vt[b2][:], in_=ptb[pb][:, 0:128]),
                         reads=[("pt", pb)], writes=[("vt", b2)])
                    for qi in range(qlo, qhi):
                        col = (qi - qlo) * 128
                        first = (qi == jk)
                        last = (qi == jk - 1)
                        ob = 3 + qi % 2
                        rb = 5 + qi % 2

                        def pv(e, ob=ob, rb=rb, b2=b2, col=col, first=first, last=last):
                            e.matmul(psb[ob][:, 0:128], lhsT=vt[b2][:], rhs=pT[b2][:, col:col + 128],
                                     start=first, stop=last)
                            return e.matmul(psb[rb][:, 0:128], lhsT=ones_b[:], rhs=pT[b2][:, col:col + 128],
                                            start=first, stop=last)
                        S.op("pe", pv, reads=[("vt", b2), ("pT", b2), "ones"], writes=[("ps", ob), ("ps", rb)])
                        if last:
                            S.op("dve", lambda e, ob=ob, qi=qi, r=r, aOv=aOv: e.tensor_tensor(
                                out=aOv[:, qi * 128:(qi + 1) * 128, r], in0=aOv[:, qi * 128:(qi + 1) * 128, r],
                                in1=psb[ob][:, 0:128], op=ALU.add), reads=[("ps", ob), "accO"], writes=["accO"])
                            S.op("dve", lambda e, rb=rb, qi=qi, r=r, aRv=aRv: e.tensor_tensor(
                                out=aRv[:, qi * 128:(qi + 1) * 128, r], in0=aRv[:, qi * 128:(qi + 1) * 128, r],
                                in1=psb[rb][:, 0:128], op=ALU.add), reads=[("ps", rb), "accR"], writes=["accR"])
        ws = wset % 2
        wset += 1
        S.dma("pool", lambda e, ws=ws, hs=hs: e.dma_start(out=wbuf[ws][0][:], in_=w_blk[B_ATTZ + hs]),
              writes=[("w", ws, 0)])
        for i in range(4):
            hb_i = hctr % 2
            hctr += 1
            load_h(hbuf[hb_i], ("h", hb_i), [(i, 0, 512)])
            pi = next_ps(0, 3)
            S.op("pe", mm_fm(psb[pi][:, 0:512], wbuf[ws][0], hbuf[hb_i], 512),
                 reads=[("w", ws, 0), ("h", hb_i)], writes=[("ps", pi)])
            b2 = i % 2
            S.op("act", lambda e, b2=b2, pi=pi: e.activation(out=zs[b2][:], in_=psb[pi][:, 0:512], func=AF.Silu),
                 reads=[("ps", pi)], writes=[("zs", b2)])
            S.op("dve", lambda e, b2=b2, i=i: e.reciprocal(out=rr[b2][:], in_=accR[:, i * 512:(i + 1) * 512]),
                 reads=["accR"], writes=[("rr", b2)])
            S.op("dve", lambda e, b2=b2, i=i: e.tensor_tensor(out=rr[b2][:], in0=rr[b2][:],
                                                             in1=accO[:, i * 512:(i + 1) * 512], op=ALU.mult),
                 reads=["accO", ("rr", b2)], writes=[("rr", b2)])
            S.op("dve", lambda e, b2=b2: e.tensor_tensor(out=yst[b2][:], in0=rr[b2][:], in1=zs[b2][:], op=ALU.mult),
                 reads=[("rr", b2), ("zs", b2)], writes=[("yst", b2)])
            S.dma("sp", lambda e, b2=b2, hs=hs, i=i: e.dma_start(
                out=yaT_d[hs * 128:(hs + 1) * 128, i * 512:(i + 1) * 512], in_=yst[b2][:]),
                reads=[("yst", b2)], writes=[("yaT_d", hs, i)])
    S.barrier()
    A.reset(base_mark)

    hbuf = [A.alloc([128, 32, 512], BF16, "hbuf") for _ in range(2)]
    wfm = [A.alloc([128, 32, 128], BF16, "wfm") for _ in range(4)]
    wtm = [A.alloc([128, 32, 512], BF16, "wtm") for _ in range(2)]
    stg = [A.alloc([128, 512], F32, "stg") for _ in range(4)]
    stgb = [A.alloc([128, 512], BF16, "stgb") for _ in range(4)]
    wclr = A.alloc([128, 32, 16], BF16, "wclr")
    sctr = 0
    hctr = 0
    fm_list = [(B_GQ + i, gqT_d, i * 128, 128, 256.0 ** -0.5) for i in range(8)]
    fm_list += [(B_GK + i, gkT_d, i * 128, 128, 1.0) for i in range(8)]
    fm_list += [(B_LR, lrT_d, 0, 32, 1.0)]
    for c4 in range(0, len(fm_list), 4):
        grp = fm_list[c4:c4 + 4]
        for wi, (blk, dst, r0, m, sc) in enumerate(grp):
            S.dma("pool", lambda e, wi=wi, blk=blk: e.dma_start(out=wfm[wi][:], in_=w_blk[blk]), writes=[("wfm", wi)])
        for i in range(4):
            hb_i = hctr % 2
            hctr += 1
            load_h(hbuf[hb_i], ("h", hb_i), [(i, 0, 512)])
            for wi, (blk, dst, r0, m, sc) in enumerate(grp):
                pi = next_ps()
                S.op("pe", mm_fm(psb[pi][0:m, 0:512], wfm[wi], hbuf[hb_i], 512, wsl=(0, m)),
                     reads=[("wfm", wi), ("h", hb_i)], writes=[("ps", pi)])
                sb_i = sctr % 4
                sctr += 1
                S.op("act", lambda e, sb_i=sb_i, pi=pi, m=m, sc=sc: e.activation(
                    out=stg[sb_i][0:m, :], in_=psb[pi][0:m, 0:512], func=AF.Copy, scale=sc),
                    reads=[("ps", pi)], writes=[("stg", sb_i)])
                S.dma("sp", lambda e, sb_i=sb_i, dst=dst, r0=r0, m=m, i=i: e.dma_start(
                    out=dst[r0:r0 + m, i * 512:(i + 1) * 512], in_=stg[sb_i][0:m, :]),
                    reads=[("stg", sb_i)], writes=[("gfm", id(dst), r0, i)])
    tm_list = []
    for cg in range(2):
        tm_list.append((B_GK + 4 * cg, "k", cg))
    for cg in range(4):
        tm_list.append((B_GV + 4 * cg, "v", cg))
    for cg in range(4):
        tm_list.append((B_GZ + 4 * cg, "z", cg))
    wctr = 0
    for (blk0, kind, cg) in tm_list:
        wi = wctr % 2
        wctr += 1
        for b4 in range(4):
            S.dma("pool", lambda e, wi=wi, blk0=blk0, b4=b4: e.dma_start(
                out=wtm[wi][:, :, b4 * 128:(b4 + 1) * 128], in_=w_blk[blk0 + b4]), writes=[("wtm", wi)])
        tiles = list(range(4)) + ([] if kind == "z" else list(range(8, 20)))
        for tl in tiles:
            hb_i = hctr % 2
            hctr += 1
            load_h(hbuf[hb_i], ("h", hb_i), [(tl, 0, 512)])
            for sub in range(4):
                pi = next_ps()
                S.op("pe", mm_tm(psb[pi][:, 0:512], hbuf[hb_i], sub * 128, wtm[wi], 512),
                     reads=[("wtm", wi), ("h", hb_i)], writes=[("ps", pi)])
                sb_i = sctr % 4
                sctr += 1
                if tl < 4:
                    row0 = tl * 512 + sub * 128
                    dk, dv_, dz = gk_d, gv_d, gzs_d
                else:
                    row0 = (tl - 8) * 512 + sub * 128
                    dk, dv_, dz = ck_d, cv_d, None
                if kind == "k":
                    S.op("act", lambda e, sb_i=sb_i, pi=pi: e.copy(out=stg[sb_i][:], in_=psb[pi][:, 0:512]),
                         reads=[("ps", pi)], writes=[("stg", sb_i)])
                    S.dma("sp", lambda e, sb_i=sb_i, dk=dk, row0=row0, cg=cg: e.dma_start(
                        out=dk[row0:row0 + 128, cg * 512:(cg + 1) * 512], in_=stg[sb_i][:]),
                        reads=[("stg", sb_i)], writes=[("gtm", kind, cg, tl, sub)])
                elif kind == "v":
                    S.op("dve", lambda e, sb_i=sb_i, pi=pi: e.tensor_copy(out=stgb[sb_i][:], in_=psb[pi][:, 0:512]),
                         reads=[("ps", pi)], writes=[("stgb", sb_i)])
                    S.dma("sp", lambda e, sb_i=sb_i, dv_=dv_, row0=row0, cg=cg: e.dma_start(
                        out=dv_[row0:row0 + 128, cg * 512:(cg + 1) * 512], in_=stgb[sb_i][:]),
                        reads=[("stgb", sb_i)], writes=[("gtm", kind, cg, tl, sub)])
                else:
                    S.op("act", lambda e, sb_i=sb_i, pi=pi: e.activation(out=stg[sb_i][:], in_=psb[pi][:, 0:512],
                                                                         func=AF.Silu),
                         reads=[("ps", pi)], writes=[("stg", sb_i)])
                    S.dma("sp", lambda e, sb_i=sb_i, dz=dz, row0=row0, cg=cg: e.dma_start(
                        out=dz[row0:row0 + 128, cg * 512:(cg + 1) * 512], in_=stg[sb_i][:]),
                        reads=[("stg", sb_i)], writes=[("gtm", kind, cg, tl, sub)])
    for sl in range(3):
        S.dma("pool", lambda e, sl=sl: e.dma_start(out=wclr[:], in_=w_clr[sl]), writes=["wclr"])
        for i in range(4):
            hb_i = hctr % 2
            hctr += 1
            load_h(hbuf[hb_i], ("h", hb_i), [(8 + sl * 4 + i, 0, 512)])
            pi = next_ps()
            S.op("pe", mm_fm(psb[pi][0:16, 0:512], wclr, hbuf[hb_i], 512),
                 reads=["wclr", ("h", hb_i)], writes=[("ps", pi)])
            sb_i = sctr % 4
            sctr += 1
            S.op("act", lambda e, sb_i=sb_i, pi=pi: e.copy(out=stg[sb_i][0:16, :], in_=psb[pi][0:16, 0:512]),
                 reads=[("ps", pi)], writes=[("stg", sb_i)])
            S.dma("sp", lambda e, sb_i=sb_i, sl=sl, i=i: e.dma_start(
                out=clrT_d[sl, :, i * 512:(i + 1) * 512], in_=stg[sb_i][0:16, :]),
                reads=[("stg", sb_i)], writes=[("clr", sl, i)])
    S.barrier()
    A.reset(base_mark)

    tri_t = A.alloc([64, 4, 64], F32, "tri")
    msk_t = A.alloc([64, 2, 64], F32, "msk")
    gn_t = A.alloc([64, 512], F32, "gn")
    for i4 in range(4):
        S.dma("sp", lambda e, i4=i4: e.dma_start(out=tri_t[:, i4, :], in_=tri[i4]), writes=["tri"])
    for i2 in range(2):
        S.dma("sp", lambda e, i2=i2: e.dma_start(out=msk_t[:, i2, :], in_=msk[i2]), writes=["msk"])
    S.dma("sp", lambda e: e.dma_start(out=gn_t[:], in_=gng_b), writes=["gn"])
    lra = A.alloc([64, T], F32, "lra")
    w2a = A.alloc([64, 256], F32, "w2a")
    Sst = A.alloc([128, 2, 512], F32, "Sst")
    Sbf = A.alloc([128, 2, 512], BF16, "Sbf")
    Sinf = A.alloc([128, 2, 512], F32, "Sinf")
    e_t = [A.alloc([64, 256], F32, "e_t") for _ in range(2)]
    sp_t = [A.alloc([64, 256], F32, "sp_t") for _ in range(2)]
    eb_t = [A.alloc([128, 2, 64], F32, "eb_t") for _ in range(2)]
    enb_t = [A.alloc([128, 2, 64], F32, "enb_t") for _ in range(2)]
    ekd_t = [A.alloc([64, 256], F32, "ekd_t") for _ in range(2)]
    kd_t = [A.alloc([64, 256], BF16, "kd_t") for _ in range(2)]
    qe_t = [A.alloc([128, 2, 64], BF16, "qe_t") for _ in range(2)]
    ke_t = [A.alloc([128, 2, 64], BF16, "ke_t") for _ in range(2)]
    AT_t = [A.alloc([64, 64], BF16, "AT_t") for _ in range(2)]
    ksc = [A.alloc([64, 8, 256], F32, "ksc") for _ in range(2)]
    vsc = [A.alloc([64, 8, 512], BF16, "vsc") for _ in range(2)]
    qsc = [A.alloc([128, 2, 512], F32, "qsc") for _ in range(2)]
    ktsc = [A.alloc([128, 2, 512], F32, "ktsc") for _ in range(2)]
    zsc = [A.alloc([64, 8, 512], F32, "zsc") for _ in range(2)]
    o_f = A.alloc([64, 32, 512], F32, "o_f")
    o_s = [A.alloc([64, 512], F32, "o_s") for _ in range(2)]
    y_s = [A.alloc([64, 512], BF16, "y_s") for _ in range(2)]
    st2 = [A.alloc([64, 4], F32, "st2") for _ in range(2)]
    junk2 = A.alloc([64, 512], BF16, "junk2")
    ybT_s = [A.alloc([128, 4, 512], BF16, "ybT_s") for _ in range(2)]
    cc = {"n": 0, "sc": 0}

    def setup_lr(src_rows, w2src, h):
        S.op("dve", lambda e: e.memset(lra[:], 0.0), writes=["lra"])
        S.op("dve", lambda e: e.memset(lra[32:33, :], 1.0), writes=["lra"])
        S.dma("sp", lambda e: e.dma_start(out=lra[0:16, :], in_=src_rows), reads=[], writes=["lra"])
        S.op("dve", lambda e: e.memset(w2a[:], 0.0), writes=["w2a"])
        S.dma("sp", lambda e: e.dma_start(out=w2a[0:33, :], in_=w2src[:, h * 256:(h + 1) * 256]), writes=["w2a"])

    def decays(tok0, di, need_fm, b):
        pz = next_ps()
        S.op("pe", lambda e: e.matmul(psb[pz][0:64, 0:256], lhsT=lra[0:33, tok0:tok0 + 64], rhs=w2a[0:33, :],
                                      start=True, stop=True), reads=["lra", "w2a"], writes=[("ps", pz)])
        S.op("act", lambda e: e.activation(out=e_t[b][:], in_=psb[pz][0:64, 0:256], func=AF.Exp, scale=-1.0),
             reads=[("ps", pz)], writes=[("e_t", b)])
        S.op("act", lambda e: e.activation(out=sp_t[b][:], in_=e_t[b][:], func=AF.Ln, bias=1.0, scale=1.0),
             reads=[("e_t", b)], writes=[("sp_t", b)])
        pb_ = next_ps()

        def bfm(e):
            ins = None
            for kk in range(2):
                ins = e.matmul(psb[pb_][:, kk * 64:(kk + 1) * 64], lhsT=sp_t[b][:, kk * 128:(kk + 1) * 128],
                               rhs=tri_t[:, 2 * di, :], start=True, stop=True)
            return ins
        S.op("pe", bfm, reads=[("sp_t", b), "tri"], writes=[("ps", pb_)])
        S.op("act", lambda e: e.activation(out=eb_t[b][:], in_=psb[pb_][:, 0:128].rearrange("p (k t) -> p k t", k=2),
                                           func=AF.Exp), reads=[("ps", pb_)], writes=[("eb_t", b)])
        if need_fm:
            S.op("act", lambda e: e.activation(out=enb_t[b][:],
                                               in_=psb[pb_][:, 0:128].rearrange("p (k t) -> p k t", k=2),
                                               func=AF.Exp, scale=-1.0), reads=[("ps", pb_)], writes=[("enb_t", b)])
        pk = next_ps()
        S.op("pe", lambda e: e.matmul(psb[pk][0:64, 0:256], lhsT=tri_t[:, 2 * di + 1, :], rhs=sp_t[b][:],
                                      start=True, stop=True), reads=[("sp_t", b), "tri"], writes=[("ps", pk)])
        S.op("act", lambda e: e.activation(out=ekd_t[b][:], in_=psb[pk][0:64, 0:256], func=AF.Exp),
             reads=[("ps", pk)], writes=[("ekd_t", b)])

    def state_update(kbuf, vbuf, sl, ch, di, b):
        S.op("dve", lambda e: e.tensor_tensor(out=kd_t[b][:], in0=kbuf[sl][:, ch, :], in1=ekd_t[b][:], op=ALU.mult),
             reads=[("ksc", sl), ("ekd_t", b)], writes=[("kd_t", b)])
        last = 63 if di == 0 else 0
        for kk in range(2):
            pu = next_ps()
            S.op("pe", lambda e, kk=kk, pu=pu: e.matmul(psb[pu][:, 0:512], lhsT=kd_t[b][:, kk * 128:(kk + 1) * 128],
                                                         rhs=vbuf[sl][:, ch, :], start=True, stop=True),
                 reads=[("kd_t", b), ("vsc", sl)], writes=[("ps", pu)])
            S.op("dve", lambda e, kk=kk, pu=pu: e.scalar_tensor_tensor(
                out=Sst[:, kk, :], in0=Sst[:, kk, :], scalar=eb_t[b][:, kk, last:last + 1], in1=psb[pu][:, 0:512],
                op0=ALU.mult, op1=ALU.add), reads=[("ps", pu), ("eb_t", b), ("Sst", kk)], writes=[("Sst", kk)])
            S.op("act", lambda e, kk=kk: e.copy(out=Sbf[:, kk, :], in_=Sst[:, kk, :]),
                 reads=[("Sst", kk)], writes=[("Sbf", kk)])

    def load_kv(kd_src, vd_src, tok0, h, sl):
        S.dma("sp", lambda e: e.dma_start(
            out=ksc[sl][:], in_=kd_src[tok0:tok0 + 512, h * 256:(h + 1) * 256].rearrange("(c t) f -> t c f", t=64)),
            writes=[("ksc", sl)])
        S.dma("sp", lambda e: e.dma_start(
            out=vsc[sl][:], in_=vd_src[tok0:tok0 + 512, h * 512:(h + 1) * 512].rearrange("(c t) f -> t c f", t=64)),
            writes=[("vsc", sl)])

    for h in range(4):
        S.op("dve", lambda e: e.memset(Sst[:], 0.0), writes=[("Sst", 0), ("Sst", 1)])
        S.op("dve", lambda e: e.memset(Sinf[:], 0.0), writes=["Sinf"])

        def boundary(i):
            for kk in range(2):
                S.op("dve", lambda e, kk=kk: e.scalar_tensor_tensor(
                    out=Sinf[:, kk, :], in0=Sst[:, kk, :], scalar=bnd_t[:, i:i + 1], in1=Sinf[:, kk, :],
                    op0=ALU.mult, op1=ALU.add), reads=[("Sst", kk), "Sinf", "bnd"], writes=["Sinf"])
                S.op("dve", lambda e, kk=kk: e.tensor_scalar(
                    out=Sst[:, kk, :], in0=Sst[:, kk, :], scalar1=bnd_t[:, 4 + i:5 + i], scalar2=None, op0=ALU.mult),
                    reads=[("Sst", kk), "bnd"], writes=[("Sst", kk)])
        for sl_ in range(3):
            boundary(sl_)
            setup_lr(clrT_d[sl_], cw2aug[sl_], h)
            for s5 in range(4):
                sl = cc["sc"] % 2
                cc["sc"] += 1
                load_kv(ck_d, cv_d, sl_ * T + s5 * 512, h, sl)
                for ch in range(8):
                    b = cc["n"] % 2
                    cc["n"] += 1
                    decays(s5 * 512 + ch * 64, 0, False, b)
                    state_update(ksc, vsc, sl, ch, 0, b)
        boundary(3)
        Sbk = o_s
        if h == 0:
            Sbk_t = A.alloc([128, 2, 512], F32, "Sbk")
            cc["Sbk"] = Sbk_t
        Sbk_t = cc["Sbk"]
        S.op("dve", lambda e: e.tensor_copy(out=Sbk_t[:], in_=Sst[:]), reads=[("Sst", 0), ("Sst", 1)], writes=["Sbk"])
        for di in range(2):
            src_state = Sinf if di == 0 else Sbk_t
            skey = "Sinf" if di == 0 else "Sbk"
            S.op("dve", lambda e, src_state=src_state: e.tensor_copy(out=Sst[:], in_=src_state[:]),
                 reads=[skey], writes=[("Sst", 0), ("Sst", 1)])
            S.op("act", lambda e, src_state=src_state: e.copy(out=Sbf[:], in_=src_state[:]),
                 reads=[skey], writes=[("Sbf", 0), ("Sbf", 1)])
            setup_lr(lrT_d[16 * di:16 * di + 16, :], w2aug[di], h)
            sc_order = range(4) if di == 0 else range(3, -1, -1)
            for s5 in sc_order:
                sl = cc["sc"] % 2
                cc["sc"] += 1
                tok0 = s5 * 512
                load_kv(gk_d, gv_d, tok0, h, sl)
                S.dma("sp", lambda e, sl=sl, tok0=tok0, h=h: e.dma_start(
                    out=qsc[sl][:], in_=gqT_d[h * 256:(h + 1) * 256, tok0:tok0 + 512].rearrange("(k p) t -> p k t", p=128)),
                    writes=[("qsc", sl)])
                S.dma("sp", lambda e, sl=sl, tok0=tok0, h=h: e.dma_start(
                    out=ktsc[sl][:], in_=gkT_d[h * 256:(h + 1) * 256, tok0:tok0 + 512].rearrange("(k p) t -> p k t", p=128)),
                    writes=[("ktsc", sl)])
                if di == 1:
                    S.dma("sp", lambda e, sl=sl, tok0=tok0, h=h: e.dma_start(
                        out=zsc[sl][:], in_=gzs_d[tok0:tok0 + 512, h * 512:(h + 1) * 512].rearrange("(c t) f -> t c f", t=64)),
                        writes=[("zsc", sl)])
                ch_order = range(8) if di == 0 else range(7, -1, -1)
                for ch in ch_order:
                    b = cc["n"] % 2
                    cc["n"] += 1
                    decays(tok0 + ch * 64, di, True, b)
                    S.op("dve", lambda e, b=b, sl=sl, ch=ch: e.tensor_tensor(
                        out=qe_t[b][:], in0=qsc[sl][:, :, ch * 64:(ch + 1) * 64], in1=eb_t[b][:], op=ALU.mult),
                        reads=[("qsc", sl), ("eb_t", b)], writes=[("qe_t", b)])
                    S.op("dve", lambda e, b=b, sl=sl, ch=ch: e.tensor_tensor(
                        out=ke_t[b][:], in0=ktsc[sl][:, :, ch * 64:(ch + 1) * 64], in1=enb_t[b][:], op=ALU.mult),
                        reads=[("ktsc", sl), ("enb_t", b)], writes=[("ke_t", b)])
                    pa = next_ps()

                    def amm(e, b=b, pa=pa):
                        ins = None
                        for kk in range(2):
                            ins = e.matmul(psb[pa][0:64, 0:64], lhsT=ke_t[b][:, kk, :], rhs=qe_t[b][:, kk, :],
                                           start=(kk == 0), stop=(kk == 1))
                        return ins
                    S.op("pe", amm, reads=[("qe_t", b), ("ke_t", b)], writes=[("ps", pa)])
                    S.op("dve", lambda e, b=b, pa=pa, di=di: e.tensor_tensor(
                        out=AT_t[b][:], in0=psb[pa][0:64, 0:64], in1=msk_t[:, di, :], op=ALU.mult),
                        reads=[("ps", pa), "msk"], writes=[("AT_t", b)])
                    po = next_ps()

                    def omm(e, b=b, po=po, sl=sl, ch=ch):
                        e.matmul(psb[po][0:64, 0:512], lhsT=AT_t[b][:], rhs=vsc[sl][:, ch, :], start=True, stop=False)
                        e.matmul(psb[po][0:64, 0:512], lhsT=qe_t[b][:, 0, :], rhs=Sbf[:, 0, :], start=False, stop=False)
                        return e.matmul(psb[po][0:64, 0:512], lhsT=qe_t[b][:, 1, :], rhs=Sbf[:, 1, :],
                                        start=False, stop=True)
                    S.op("pe", omm, reads=[("AT_t", b), ("vsc", sl), ("qe_t", b), ("Sbf", 0), ("Sbf", 1)],
                         writes=[("ps", po)])
                    cg = s5 * 8 + ch
                    if di == 0:
                        S.op("act", lambda e, po=po, cg=cg: e.copy(out=o_f[:, cg, :], in_=psb[po][0:64, 0:512]),
                             reads=[("ps", po)], writes=[("o_f", cg)])
                    else:
                        S.op("dve", lambda e, b=b, po=po, cg=cg: e.tensor_tensor(
                            out=o_s[b][:], in0=o_f[:, cg, :], in1=psb[po][0:64, 0:512], op=ALU.add),
                            reads=[("ps", po), ("o_f", cg)], writes=[("o_s", b)])
                        S.op("dve", lambda e, b=b: e.memset(st2[b][:, 0:1], 0.0), writes=[("s20", b)])
                        S.op("act", lambda e, b=b: e.activation(out=junk2[:], in_=o_s[b][:], func=AF.Square,
                                                                accum_out=st2[b][:, 0:1]),
                             reads=[("o_s", b)], writes=["junk2", ("s20", b)])
                        S.op("dve", lambda e, b=b: e.tensor_scalar(out=st2[b][:, 1:2], in0=st2[b][:, 0:1],
                                                                    scalar1=1.0 / 512, scalar2=EPS, op0=ALU.mult,
                                                                    op1=ALU.add),
                             reads=[("s20", b)], writes=[("s21", b)])
                        S.op("act", lambda e, b=b: e.activation(out=st2[b][:, 2:3], in_=st2[b][:, 1:2], func=AF.Sqrt),
                             reads=[("s21", b)], writes=[("s22", b)])
                        S.op("dve", lambda e, b=b: e.reciprocal(out=st2[b][:, 3:4], in_=st2[b][:, 2:3]),
                             reads=[("s22", b)], writes=[("s23", b)])
                        S.op("dve", lambda e, b=b: e.scalar_tensor_tensor(
                            out=o_s[b][:], in0=o_s[b][:], scalar=st2[b][:, 3:4], in1=gn_t[:], op0=ALU.mult,
                            op1=ALU.mult), reads=[("o_s", b), ("s23", b), "gn"], writes=[("o_s", b)])
                        S.op("dve", lambda e, b=b, sl=sl, ch=ch: e.tensor_tensor(
                            out=y_s[b][:], in0=o_s[b][:], in1=zsc[sl][:, ch, :], op=ALU.mult),
                            reads=[("o_s", b), ("zsc", sl)], writes=[("y_s", b)])
                        pb = 0
                        rot["pt"] += 1

                        def trf(e, b=b, pb=pb):
                            ins = None
                            for q4 in range(4):
                                ins = e.transpose(out=ptb[pb][:, q4 * 64:(q4 + 1) * 64],
                                                  in_=y_s[b][:, q4 * 128:(q4 + 1) * 128], identity=ident[0:64, 0:64])
                            return ins
                        S.op("pe", trf, reads=[("y_s", b), "ident"], writes=[("pt", pb)])
                        S.op("act", lambda e, pb=pb, sl=sl, ch=ch: e.copy(
                            out=ybT_s[sl][:, :, ch * 64:(ch + 1) * 64],
                            in_=ptb[pb][:, 0:256].rearrange("p (q t) -> p q t", q=4)),
                            reads=[("pt", pb)], writes=[("ybT_s", sl, ch)])
                    state_update(ksc, vsc, sl, ch, di, b)
                if di == 1:
                    S.dma("sp", lambda e, sl=sl, tok0=tok0, h=h: e.dma_start(
                        out=ybT_d[h * 512:(h + 1) * 512, tok0:tok0 + 512].rearrange("(q p) t -> p q t", p=128),
                        in_=ybT_s[sl][:]), reads=[("ybT_s", sl, c_) for c_ in range(8)], writes=[("ybT_d", h, s5)])
    S.barrier()
    A.reset(base_mark)

    mergedT = A.alloc([128, 32, 512], BF16, "mergedT")
    ov_mark = A.mark()
    for tt in range(4):
        A.reset(ov_mark)
        hT1 = A.alloc([128, 32, 512], BF16, "hT1")
        yaT_t = A.alloc([128, 8, 512], BF16, "yaT_t")
        ybT_t = A.alloc([128, 16, 512], BF16, "ybT_t")
        wg = [[A.alloc([128, 32, 128], BF16, "wg") for _ in range(2)] for _ in range(2)]
        wa = [A.alloc([128, 8, 128], BF16, "wa") for _ in range(2)]
        wb_ = [A.alloc([128, 16, 128], BF16, "wb") for _ in range(2)]
        sg = [[A.alloc([128, 512], F32, "sg") for _ in range(2)] for _ in range(2)]
        mt = [A.alloc([128, 512], F32, "mt") for _ in range(2)]
        load_h(hT1, "hT1", [(tt, 0, 512)])
        S.dma("sp", lambda e, tt=tt: e.dma_start(
            out=yaT_t[:], in_=yaT_d[:, tt * 512:(tt + 1) * 512].rearrange("(k p) t -> p k t", p=128)), writes=["yaT_t"])
        S.dma("sp", lambda e, tt=tt: e.dma_start(
            out=ybT_t[:], in_=ybT_d[:, tt * 512:(tt + 1) * 512].rearrange("(k p) t -> p k t", p=128)), writes=["ybT_t"])
        for blk in range(32):
            b = blk % 2
            S.dma("pool", lambda e, b=b, blk=blk: e.dma_start(out=wg[b][0][:], in_=w_blk[B_MGA + blk]),
                  writes=[("wg", b, 0)])
            S.dma("pool", lambda e, b=b, blk=blk: e.dma_start(out=wg[b][1][:], in_=w_blk[B_MGB + blk]),
                  writes=[("wg", b, 1)])
            S.dma("pool", lambda e, b=b, blk=blk: e.dma_start(out=wa[b][:], in_=wua[blk]), writes=[("wa", b)])
            S.dma("pool", lambda e, b=b, blk=blk: e.dma_start(out=wb_[b][:], in_=wub[blk]), writes=[("wb", b)])
            for ab in range(2):
                pi = next_ps()
                S.op("pe", mm_fm(psb[pi][:, 0:512], wg[b][ab], hT1, 512), reads=[("wg", b, ab), "hT1"],
                     writes=[("ps", pi)])
                S.op("act", lambda e, b=b, ab=ab, pi=pi: e.activation(out=sg[b][ab][:], in_=psb[pi][:, 0:512],
                                                                      func=AF.Sigmoid),
                     reads=[("ps", pi)], writes=[("sg", b, ab)])
            pa = next_ps()
            S.op("pe", mm_fm(psb[pa][:, 0:512], wa[b], yaT_t, 512, nk=8), reads=[("wa", b), "yaT_t"],
                 writes=[("ps", pa)])
            S.op("dve", lambda e, b=b, pa=pa: e.tensor_tensor(out=mt[b][:], in0=sg[b][0][:], in1=psb[pa][:, 0:512],
                                                              op=ALU.mult),
                 reads=[("ps", pa), ("sg", b, 0)], writes=[("mt", b)])
            pbb = next_ps()
            S.op("pe", mm_fm(psb[pbb][:, 0:512], wb_[b], ybT_t, 512, nk=16), reads=[("wb", b), "ybT_t"],
                 writes=[("ps", pbb)])
            S.op("dve", lambda e, b=b, pbb=pbb: e.tensor_tensor(out=sg[b][1][:], in0=sg[b][1][:],
                                                                in1=psb[pbb][:, 0:512], op=ALU.mult),
                 reads=[("ps", pbb), ("sg", b, 1)], writes=[("sg", b, 1)])
            S.op("dve", lambda e, b=b, blk=blk: e.tensor_tensor(out=mergedT[:, blk, :], in0=mt[b][:],
                                                                in1=sg[b][1][:], op=ALU.add),
                 reads=[("mt", b), ("sg", b, 1)], writes=[("merged", blk)])
        S.barrier()
        A.reset(ov_mark)
        wo_t = [A.alloc([128, 32, 256], BF16, "wo_t") for _ in range(2)]
        ypre = [A.alloc([128, D], F32, "ypre") for _ in range(4)]
        xs = [A.alloc([128, 256], F32, "xs") for _ in range(4)]
        gs = [A.alloc([128, 512], F32, "gs") for _ in range(2)]
        st3 = A.alloc([128, 8], F32, "st3")
        st4 = A.alloc([128, 16], F32, "st4")
        junk3 = A.alloc([128, 512], BF16, "junk3")
        xc = 0
        for cb in range(16):
            b = cb % 2
            S.dma("pool", lambda e, b=b, cb=cb: e.dma_start(out=wo_t[b][:], in_=wo[cb]), writes=[("wo_t", b)])
            for sub in range(4):
                xi = xc % 4
                xc += 1
                r0 = tt * 512 + sub * 128
                S.dma("sp", lambda e, xi=xi, r0=r0, cb=cb: e.dma_start(out=xs[xi][:],
                                                                      in_=x_own[r0:r0 + 128, cb * 256:(cb + 1) * 256]),
                      writes=[("xs", xi)])
                pi = next_ps()

                def omm2(e, pi=pi, sub=sub, b=b):
                    ins = None
                    for kc in range(32):
                        ins = e.matmul(psb[pi][:, 0:256], lhsT=mergedT[:, kc, sub * 128:(sub + 1) * 128],
                                       rhs=wo_t[b][:, kc, :], start=(kc == 0), stop=(kc == 31))
                    return ins
                S.op("pe", omm2, reads=[("wo_t", b)] + [("merged", k_) for k_ in range(32)], writes=[("ps", pi)])
                S.op("dve", lambda e, pi=pi, sub=sub, cb=cb, xi=xi: e.tensor_tensor(
                    out=ypre[sub][:, cb * 256:(cb + 1) * 256], in0=psb[pi][:, 0:256], in1=xs[xi][:], op=ALU.add),
                    reads=[("ps", pi), ("xs", xi)], writes=[("ypre", sub, cb)])
        for sub in range(4):
            S.op("dve", lambda e: e.memset(st3[:], 0.0), writes=[("st3", c_) for c_ in range(8)])
            for c8 in range(8):
                S.op("act", lambda e, sub=sub, c8=c8: e.activation(
                    out=junk3[:], in_=ypre[sub][:, c8 * 512:(c8 + 1) * 512], func=AF.Square,
                    accum_out=st4[:, sub * 4 + 0:sub * 4 + 1] if False else st3[:, c8:c8 + 1]),
                    reads=[("ypre", sub, 2 * c8), ("ypre", sub, 2 * c8 + 1)], writes=["junk3", ("st3", c8)])
            S.op("dve", lambda e, sub=sub: e.tensor_reduce(out=st4[:, sub * 4:sub * 4 + 1], in_=st3[:, 0:8],
                                                           axis=mybir.AxisListType.X, op=ALU.add),
                 reads=[("st3", c_) for c_ in range(8)], writes=[("st4", sub, 0)])
            S.op("dve", lambda e, sub=sub: e.tensor_scalar(out=st4[:, sub * 4 + 1:sub * 4 + 2],
                                                            in0=st4[:, sub * 4:sub * 4 + 1], scalar1=1.0 / D,
                                                            scalar2=EPS, op0=ALU.mult, op1=ALU.add),
                 reads=[("st4", sub, 0)], writes=[("st4", sub, 1)])
            S.op("act", lambda e, sub=sub: e.activation(out=st4[:, sub * 4 + 2:sub * 4 + 3],
                                                        in_=st4[:, sub * 4 + 1:sub * 4 + 2], func=AF.Sqrt),
                 reads=[("st4", sub, 1)], writes=[("st4", sub, 2)])
            S.op("dve", lambda e, sub=sub: e.reciprocal(out=st4[:, sub * 4 + 3:sub * 4 + 4],
                                                        in_=st4[:, sub * 4 + 2:sub * 4 + 3]),
                 reads=[("st4", sub, 2)], writes=[("st4", sub, 3)])
            for c8 in range(8):
                gi = c8 % 2
                S.dma("sp", lambda e, gi=gi, c8=c8: e.dma_start(out=gs[gi][:], in_=fng_b[:, c8 * 512:(c8 + 1) * 512]),
                      writes=[("gs", gi)])
                S.op("dve", lambda e, sub=sub, c8=c8, gi=gi: e.scalar_tensor_tensor(
                    out=ypre[sub][:, c8 * 512:(c8 + 1) * 512], in0=ypre[sub][:, c8 * 512:(c8 + 1) * 512],
                    scalar=st4[:, sub * 4 + 3:sub * 4 + 4], in1=gs[gi][:], op0=ALU.mult, op1=ALU.mult),
                    reads=[("ypre", sub, 2 * c8), ("ypre", sub, 2 * c8 + 1), ("st4", sub, 3), ("gs", gi)],
                    writes=[("ypre", sub, 2 * c8), ("ypre", sub, 2 * c8 + 1)])
            r0 = tt * 512 + sub * 128
            S.dma("sp", lambda e, sub=sub, r0=r0: e.dma_start(out=y_out[r0:r0 + 128, :], in_=ypre[sub][:]),
                  reads=[("ypre", sub, c_) for c_ in range(16)], writes=[("y_out", r0)])
        S.barrier()
    S.emit()
    return nc


def _consts():
    slopes = np.exp2(-8.0 * (np.arange(8, dtype=np.float32) + 1.0) / 8).astype(np.float32)
    i = np.arange(128)[:, None]
    c = np.arange(256)[None, :]
    delta = i - c + 64
    att_bias = np.zeros((3, 8, 128, 256), np.float32)
    for g, d in enumerate(DILS):
        dist = (np.abs(delta) * d).astype(np.float32)
        for h in range(8):
            b = -slopes[h] * dist
            att_bias[g, h] = np.where(np.abs(delta) <= 64, b, np.float32(-30000.0))
    s = np.arange(64)[:, None]
    t = np.arange(64)[None, :]
    k = np.float32(-1.0 / 16.0)
    tri = np.zeros((4, 64, 64), np.float32)
    tri[0] = np.where(s <= t, k, 0)
    tri[1] = np.where(s > t, k, 0)
    tri[2] = np.where(s >= t, k, 0)
    tri[3] = np.where(s < t, k, 0)
    msk = np.zeros((2, 64, 64), np.float32)
    msk[0] = (s <= t)
    msk[1] = (s > t)
    return att_bias, tri, msk


def _prep_shared(inp):
    w_in = np.asarray(inp["w_in"])[0]
    cols = block_cols()
    idx = np.zeros(NBLK * 128, np.int64)
    valid = np.zeros(NBLK * 128, bool)
    for b, (c0, n) in enumerate(cols):
        idx[b * 128:b * 128 + n] = np.arange(c0, c0 + n)
        valid[b * 128:b * 128 + n] = True
    wsel = np.take(w_in, idx, axis=1)
    wsel[:, ~valid] = 0.0
    w_blk = np.ascontiguousarray(wsel.reshape(32, 128, NBLK, 128).transpose(2, 1, 0, 3))
    wua = np.ascontiguousarray(np.asarray(inp["w_up_a"])[0].reshape(8, 128, 32, 128).transpose(2, 1, 0, 3))
    wub = np.ascontiguousarray(np.asarray(inp["w_up_b"])[0].reshape(16, 128, 32, 128).transpose(2, 1, 0, 3))
    wo = np.ascontiguousarray(np.asarray(inp["w_o"])[0].reshape(32, 128, 16, 256).transpose(2, 1, 0, 3))
    att_bias, tri, msk = _consts()
    w2 = [np.asarray(inp["gla_w2_f"])[0], np.asarray(inp["gla_w2_b"])[0]]
    bb = [np.asarray(inp["gla_b_f"])[0], np.asarray(inp["gla_b_b"])[0]]
    w2aug = np.zeros((2, 33, 1024), np.float32)
    for di in range(2):
        w2aug[di, 0:16] = w2[di]
        w2aug[di, 32] = bb[di]
    wlr = [np.ascontiguousarray(w_in[:, 16384:16400].reshape(32, 128, 16).transpose(1, 0, 2)),
           np.ascontiguousarray(w_in[:, 16400:16416].reshape(32, 128, 16).transpose(1, 0, 2))]
    sh = dict(
        w_blk=w_blk, wua=wua, wub=wub, wo=wo,
        ng_b=np.ascontiguousarray(np.broadcast_to(np.asarray(inp["norm_g"])[0][None, :], (128, D))).astype(np.float32),
        fng_b=np.ascontiguousarray(np.broadcast_to(np.asarray(inp["final_norm_g"])[None, :], (128, D))).astype(np.float32),
        gng_b=np.ascontiguousarray(np.broadcast_to(np.asarray(inp["gla_norm_g"])[0][None, :], (64, 512))).astype(np.float32),
        w2aug=w2aug, att_bias=att_bias, tri=tri, msk=msk, ident_in=np.eye(128, dtype=np.float32))
    return sh, w2aug, wlr


def _core_inputs(sh, w2aug, wlr, seq, seg, nseg):
    x_own = seq[seg * T:(seg + 1) * T]
    x_halo = np.zeros((T, D), np.float32)
    if seg > 0:
        x_halo[0:1024] = seq[seg * T - 1024:seg * T]
    if seg < nseg - 1:
        x_halo[1024:2048] = seq[(seg + 1) * T:(seg + 1) * T + 1024]
    x_ctx = np.zeros((NCTX, D), np.float32)
    cw2aug = np.zeros((3, 33, 1024), np.float32)
    w_clr = np.zeros((3, 128, 32, 16), np.float32)
    slots = []
    for j in range(0, seg):
        slots.append((j, 0))
    for j in range(nseg - 1, seg, -1):
        slots.append((j, 1))
    n_f = seg if nseg > 1 else 0
    for i, (j, di) in enumerate(slots):
        blk = seq[j * T:(j + 1) * T]
        x_ctx[i * T:(i + 1) * T] = blk if di == 0 else blk[::-1]
        cw2aug[i] = w2aug[di]
        w_clr[i] = wlr[di]
    for i in range(len(slots), 3):
        cw2aug[i] = w2aug[0]
        w_clr[i] = wlr[0]
    bnd = np.zeros((128, 8), np.float32)
    bnd[:, n_f] = 1.0
    bnd[:, 4:8] = 1.0 - bnd[:, 0:4]
    kb = np.zeros((128, 69), np.float32)
    col = 0
    for g, d in enumerate(DILS):
        hw = 64 * d
        L = T // d
        nq = L // 128
        for r in range(d):
            for jk in range(nq + 1):
                u = jk * 128 + np.arange(128)
                p = r + d * u
                tglob = seg * T + (p - hw)
                ok = (tglob >= 0) & (tglob < nseg * T)
                kb[:, col] = np.where(ok, 0.0, -30000.0)
                col += 1
    d_ = dict(sh)
    d_.update(x_own=np.ascontiguousarray(x_own), x_halo=x_halo, x_ctx=x_ctx, cw2aug=cw2aug, w_clr=w_clr,
              att_kb=kb, bnd=bnd)
    return d_


_NC_CACHE = {}


def kernel(x_prompt, x_sample, norm_g, w_in, gla_w2_f, gla_b_f, gla_w2_b, gla_b_b, gla_norm_g,
           w_up_a, w_up_b, w_o, final_norm_g):
    inp = dict(w_in=w_in, w_up_a=w_up_a, w_up_b=w_up_b, w_o=w_o, norm_g=norm_g, final_norm_g=final_norm_g,
               gla_norm_g=gla_norm_g, gla_w2_f=gla_w2_f, gla_w2_b=gla_w2_b, gla_b_f=gla_b_f, gla_b_b=gla_b_b)
    sh, w2aug, wlr = _prep_shared(inp)
    xp = np.asarray(x_prompt, dtype=np.float32)
    xs = np.asarray(x_sample, dtype=np.float32)
    in_maps = []
    for c in range(4):
        in_maps.append(_core_inputs(sh, w2aug, wlr, xp[c], 0, 1))
    for c in range(4):
        in_maps.append(_core_inputs(sh, w2aug, wlr, xs[0], c, 4))
    if "nc" not in _NC_CACHE:
        _NC_CACHE["nc"] = build_program()
    nc = _NC_CACHE["nc"]
    res = run_bass_kernel_spmd(nc, in_maps, core_ids=list(range(8)))
    outs = [np.asarray(r["y_out"], dtype=np.float32) for r in res.results]
    y_prompt = np.stack(outs[0:4], axis=0)
    y_sample = np.concatenate(outs[4:8], axis=0)[None]
    return (y_prompt, y_sample)
```

```python
import numpy as np
import concourse.bass as bass
import concourse.mybir as mybir
from concourse.bass_utils import run_bass_kernel_spmd

F32 = mybir.dt.float32
BF16 = mybir.dt.bfloat16
AF = mybir.ActivationFunctionType
ALU = mybir.AluOpType

D = 4096
T = 2048
NCTX = 6144
EPS = 1e-6
NBLK = 193
DILS = (1, 4, 16)
ARENA0 = 16640
ARENA1 = 229376


def att_blk(hs, g, j):
    return (hs * 3 + g) * 3 + j


B_ATTZ = 72
B_GQ = 80
B_GK = 88
B_GV = 96
B_GZ = 112
B_LR = 128
B_MGA = 129
B_MGB = 161


def block_cols():
    cols = []
    for hs in range(8):
        for g in range(3):
            for j in range(3):
                cols.append((((g * 3 + j) * 8 + hs) * 128, 128))
    for hs in range(8):
        cols.append((9216 + hs * 128, 128))
    for i in range(8):
        cols.append((10240 + i * 128, 128))
    for i in range(8):
        cols.append((11264 + i * 128, 128))
    for i in range(16):
        cols.append((12288 + i * 128, 128))
    for i in range(16):
        cols.append((14336 + i * 128, 128))
    cols.append((16384, 32))
    for i in range(32):
        cols.append((16416 + i * 128, 128))
    for i in range(32):
        cols.append((20512 + i * 128, 128))
    assert len(cols) == NBLK
    return cols


class Sched:
    def __init__(self, nc, n_dma_slots=40, epoch=30000):
        self.nc = nc
        self.names = ["pe", "act", "dve", "pool", "sp"]
        self.prog = {k: [] for k in self.names}
        self.epoch = epoch
        self.esem = {}
        self.ecount = {}
        self.nsem = 0
        for k in self.names:
            self._new_esem(k)
        self.dslots = [[self._sem("dma%d" % i), 0] for i in range(n_dma_slots)]
        self.dnext = 0
        self.seen = {k: {} for k in self.names}
        self.lastw = {}
        self.readers = {}
        self.nops = 0

    def _sem(self, name):
        self.nsem += 1
        return self.nc.alloc_semaphore(name="s_%s_%d" % (name, self.nsem))

    def _new_esem(self, k):
        self.esem[k] = self._sem(k)
        self.ecount[k] = 0

    def _wait(self, e, tok):
        sem, val = tok
        sid = id(sem)
        if self.seen[e].get(sid, 0) >= val:
            return
        self.seen[e][sid] = val
        self.prog[e].append(("wait", sem, val))

    def _deps(self, e, reads, writes):
        for r in reads:
            t = self.lastw.get(r)
            if t is not None:
                self._wait(e, t)
        for w in writes:
            t = self.lastw.get(w)
            if t is not None:
                self._wait(e, t)
            for t in self.readers.get(w, ()):
                self._wait(e, t)

    def _commit(self, tok, reads, writes):
        for w in writes:
            self.lastw[w] = tok
            self.readers[w] = []
        for r in reads:
            if r in writes:
                continue
            self.readers.setdefault(r, []).append(tok)

    def rec_begin(self):
        self.rec = []

    def rec_end(self):
        r = self.rec
        self.rec = None
        return r

    def replay(self, items):
        for (kind, e, fn, reads, writes) in items:
            if kind == "op":
                self.op(e, fn, reads, writes)
            else:
                self.dma(e, fn, reads, writes)

    def merge(self, lists, weights=None):
        weights = weights or [1] * len(lists)
        pos = [0] * len(lists)
        alive = True
        while alive:
            alive = False
            for i, l in enumerate(lists):
                n = min(weights[i], len(l) - pos[i])
                if n > 0:
                    self.replay(l[pos[i]:pos[i] + n])
                    pos[i] += n
                    alive = True

    def op(self, e, fn, reads=(), writes=()):
        if getattr(self, "rec", None) is not None:
            self.rec.append(("op", e, fn, tuple(reads), tuple(writes)))
            return None
        self._deps(e, reads, writes)
        if self.ecount[e] >= self.epoch:
            self._new_esem(e)
        self.ecount[e] += 1
        tok = (self.esem[e], self.ecount[e])
        self.prog[e].append(("op", fn, tok[0], 1))
        self._commit(tok, reads, writes)
        self.nops += 1
        return tok

    def dma(self, e, fn, reads=(), writes=()):
        if getattr(self, "rec", None) is not None:
            self.rec.append(("dma", e, fn, tuple(reads), tuple(writes)))
            return None
        self._deps(e, reads, writes)
        slot = self.dslots[self.dnext]
        self.dnext = (self.dnext + 1) % len(self.dslots)
        if slot[1] > 0:
            self._wait(e, (slot[0], slot[1]))
        slot[1] += 16
        tok = (slot[0], slot[1])
        self.prog[e].append(("op", fn, tok[0], 16))
        self._commit(tok, reads, writes)
        self.nops += 1
        return tok

    def barrier(self):
        for e in self.names:
            for s in self.dslots:
                if s[1] > 0:
                    self._wait(e, (s[0], s[1]))
            for k in self.names:
                if k != e and self.ecount[k] > 0:
                    self._wait(e, (self.esem[k], self.ecount[k]))
        for e in self.names:
            if self.ecount[e] > 0:
                self._wait(e, (self.esem[e], self.ecount[e]))
        self.lastw = {}
        self.readers = {}

    def emit(self):
        nc = self.nc
        with nc.Block() as block:
            def mk(k):
                def body(engine):
                    for item in self.prog[k]:
                        if item[0] == "wait":
                            engine.wait_ge(item[1], item[2])
                        else:
                            ins = item[1](engine)
                            ins.then_inc(item[2], item[3])
                return body
            block.tensor(mk("pe"))
            block.scalar(mk("act"))
            block.vector(mk("dve"))
            block.gpsimd(mk("pool"))
            block.sync(mk("sp"))


class Arena:
    def __init__(self, nc):
        self.nc = nc
        self.base = ARENA0
        self.ptr = ARENA0
        self.n = 0

    def mark(self):
        return self.ptr

    def reset(self, mark):
        self.ptr = mark

    def alloc(self, shape, dtype, name="t"):
        esz = 4 if dtype == F32 else 2
        per = esz
        for s in shape[1:]:
            per *= s
        off = (self.ptr + 63) // 64 * 64
        assert off + per <= ARENA1, ("SBUF arena overflow", name, off, per)
        self.ptr = off + per
        self.n += 1
        return self.nc.alloc_sbuf_tensor_at("%s_%d" % (name, self.n), list(shape), dtype, offset=off)


def build_program(dbg=False):
    nc = bass.Bass("TRN2", target_bir_lowering=False)
    S = Sched(nc)
    A = Arena(nc)

    def din(name, shape, dt=F32):
        return nc.dram_tensor(name, list(shape), dt, kind="ExternalInput").ap()

    def dscr(name, shape, dt):
        if dbg and name in ("yaT_d", "ybT_d", "gk_d", "gqT_d", "gv_d"):
            return nc.dram_tensor(name, list(shape), dt, kind="ExternalOutput").ap()
        return nc.dram_tensor(name, list(shape), dt).ap()

    x_own = din("x_own", [T, D])
    x_halo = din("x_halo", [T, D])
    x_ctx = din("x_ctx", [NCTX, D])
    w_blk = din("w_blk", [NBLK, 128, 32, 128])
    wua = din("wua", [32, 128, 8, 128])
    wub = din("wub", [32, 128, 16, 128])
    wo = din("wo", [16, 128, 32, 256])
    ng_b = din("ng_b", [128, D])
    fng_b = din("fng_b", [128, D])
    gng_b = din("gng_b", [128, 512])
    w2aug = din("w2aug", [2, 33, 1024])
    cw2aug = din("cw2aug", [3, 33, 1024])
    w_clr = din("w_clr", [3, 128, 32, 16])
    att_bias = din("att_bias", [3, 8, 128, 256])
    att_kb = din("att_kb", [128, 69])
    bnd = din("bnd", [128, 8])
    tri = din("tri", [4, 128, 128])
    msk = din("msk", [2, 128, 128])
    ident_in = din("ident_in", [128, 128])
    y_out = nc.dram_tensor("y_out", [T, D], F32, kind="ExternalOutput").ap()

    hT_d = dscr("hT_d", [20, 128, 32, 512], BF16)
    yaT_d = dscr("yaT_d", [1024, T], BF16)
    ybT_d = dscr("ybT_d", [2048, T], BF16)
    gqT_d = dscr("gqT_d", [1024, T], F32)
    gkT_d = dscr("gkT_d", [1024, T], F32)
    lrT_d = dscr("lrT_d", [32, T], F32)
    gk_d = dscr("gk_d", [T, 1024], F32)
    gv_d = dscr("gv_d", [T, 2048], BF16)
    gzs_d = dscr("gzs_d", [T, 2048], F32)
    ck_d = dscr("ck_d", [NCTX, 1024], F32)
    cv_d = dscr("cv_d", [NCTX, 2048], BF16)
    clrT_d = dscr("clrT_d", [3, 16, T], F32)
    o_d = dscr("o_d", [2, T, 2048], F32)

    psb = [nc.alloc_psum_tensor("psb%d" % i, [128, 512], F32) for i in range(8)]
    ptb = [psb[7][:].bitcast(BF16)]
    rot = {"ps": 0, "pt": 0}

    def next_ps(lo=0, hi=7):
        i = lo + rot["ps"] % (hi - lo)
        rot["ps"] += 1
        return i

    ident = A.alloc([128, 128], BF16, "ident")
    ones_b = A.alloc([128, 128], BF16, "ones")
    bnd_t = A.alloc([128, 8], F32, "bnd")
    S.dma("pool", lambda e: e.dma_start(out=ident[:], in_=ident_in), writes=["ident"])
    S.dma("sp", lambda e: e.dma_start(out=bnd_t[:], in_=bnd), writes=["bnd"])
    S.op("dve", lambda e: e.memset(ones_b[:], 1.0), writes=["ones"])
    base_mark = A.mark()

    gb = A.alloc([128, D], F32, "gb")
    S.dma("sp", lambda e: e.dma_start(out=gb[:], in_=ng_b), writes=["gb"])
    xt = [A.alloc([128, D], F32, "xt") for _ in range(2)]
    hb = [A.alloc([128, D], BF16, "hb") for _ in range(2)]
    junk = A.alloc([128, D], BF16, "junk")
    hTt = [A.alloc([128, 32, 512], BF16, "hTt") for _ in range(2)]
    stat = [A.alloc([128, 4], F32, "stat") for _ in range(2)]
    srcs = [(x_own, 4), (x_halo, 4), (x_ctx, 12)]
    tile_id = 0
    it = 0
    for src, nt in srcs:
        for t5 in range(nt):
            hp = tile_id % 2
            for sub in range(4):
                b = it % 2
                r0 = t5 * 512 + sub * 128
                S.dma("sp", lambda e, b=b, r0=r0, src=src: e.dma_start(out=xt[b][:], in_=src[r0:r0 + 128, :]),
                      writes=[("xt", b)])
                S.op("dve", lambda e, b=b: e.memset(stat[b][:, 0:1], 0.0), writes=[("st0", b)])
                S.op("act", lambda e, b=b: e.activation(out=junk[:], in_=xt[b][:], func=AF.Square,
                                                          accum_out=stat[b][:, 0:1]),
                     reads=[("xt", b)], writes=["junk", ("st0", b)])
                S.op("dve", lambda e, b=b: e.tensor_scalar(out=stat[b][:, 1:2], in0=stat[b][:, 0:1],
                                                            scalar1=1.0 / D, scalar2=EPS, op0=ALU.mult, op1=ALU.add),
                     reads=[("st0", b)], writes=[("st1", b)])
                S.op("act", lambda e, b=b: e.activation(out=stat[b][:, 2:3], in_=stat[b][:, 1:2], func=AF.Sqrt),
                     reads=[("st1", b)], writes=[("st2", b)])
                S.op("dve", lambda e, b=b: e.reciprocal(out=stat[b][:, 3:4], in_=stat[b][:, 2:3]),
                     reads=[("st2", b)], writes=[("st3", b)])
                S.op("dve", lambda e, b=b: e.scalar_tensor_tensor(out=hb[b][:], in0=xt[b][:], scalar=stat[b][:, 3:4],
                                                                   in1=gb[:], op0=ALU.mult, op1=ALU.mult),
                     reads=[("xt", b), ("st3", b), "gb"], writes=[("hb", b)])
                for q4 in range(4):
                    pb = 0
                    rot["pt"] += 1

                    def tr(e, b=b, q4=q4, pb=pb):
                        ins = None
                        for k in range(8):
                            kc = q4 * 8 + k
                            ins = e.transpose(out=ptb[pb][:, k * 128:(k + 1) * 128],
                                              in_=hb[b][:, kc * 128:(kc + 1) * 128], identity=ident[:])
                        return ins
                    S.op("pe", tr, reads=[("hb", b), "ident"], writes=[("pt", pb)])
                    eng = "act" if q4 % 2 == 0 else "dve"

                    def ev(e, hp=hp, q4=q4, pb=pb, sub=sub, eng=eng):
                        o = hTt[hp][:, q4 * 8:(q4 + 1) * 8, sub * 128:(sub + 1) * 128]
                        i = ptb[pb][:].rearrange("p (k t) -> p k t", k=8)
                        if eng == "act":
                            return e.copy(out=o, in_=i)
                        return e.tensor_copy(out=o, in_=i)
                    S.op(eng, ev, reads=[("pt", pb)], writes=[("hTt", hp, sub, q4)])
                it += 1
            S.dma("sp", lambda e, hp=hp, tile_id=tile_id: e.dma_start(out=hT_d[tile_id], in_=hTt[hp][:]),
                  reads=[("hTt", hp, s_, q_) for s_ in range(4) for q_ in range(4)], writes=[("hT_d", tile_id)])
            tile_id += 1
    S.barrier()
    A.reset(base_mark)

    def load_h(buf, key, pieces):
        c = 0
        for (tl, c0, n) in pieces:
            S.dma("sp", lambda e, tl=tl, c0=c0, n=n, c=c: e.dma_start(out=buf[:, :, c:c + n],
                                                                      in_=hT_d[tl, :, :, c0:c0 + n]),
                  reads=[("hT_d", tl)], writes=[key])
            c += n
        return c

    def mm_fm(ps_ap, wbuf, hbuf, n, nk=32, wsl=None):
        def f(e):
            ins = None
            for kc in range(nk):
                lw = wbuf[:, kc, :] if wsl is None else wbuf[:, kc, wsl[0]:wsl[1]]
                ins = e.matmul(ps_ap, lhsT=lw, rhs=hbuf[:, kc, 0:n], start=(kc == 0), stop=(kc == nk - 1))
            return ins
        return f

    def mm_tm(ps_ap, hbuf, t0, wbuf, ncols, nk=32):
        def f(e):
            ins = None
            for kc in range(nk):
                ins = e.matmul(ps_ap, lhsT=hbuf[:, kc, t0:t0 + 128], rhs=wbuf[:, kc, 0:ncols],
                               start=(kc == 0), stop=(kc == nk - 1))
            return ins
        return f

    hbuf = [A.alloc([128, 32, 512], BF16, "hbuf") for _ in range(2)]
    wbuf = [[A.alloc([128, 32, 128], BF16, "wbuf") for _ in range(3)] for _ in range(2)]
    qT = A.alloc([128, T], BF16, "qT")
    kT = A.alloc([128, 4096], BF16, "kT")
    vT = A.alloc([128, 4096], BF16, "vT")
    accO = A.alloc([128, T], F32, "accO")
    accR = A.alloc([128, T], F32, "accR")
    tbias = [A.alloc([128, 256], F32, "tbias") for _ in range(2)]
    kb_t = A.alloc([128, 69], F32, "kb")
    sbuf_s = [A.alloc([128, 256], F32, "sb") for _ in range(2)]
    pT = [A.alloc([128, 256], BF16, "pT") for _ in range(2)]
    vt = [A.alloc([128, 128], BF16, "vt") for _ in range(2)]
    zs = [A.alloc([128, 512], F32, "zs") for _ in range(2)]
    rr = [A.alloc([128, 512], F32, "rr") for _ in range(2)]
    yst = [A.alloc([128, 512], BF16, "yst") for _ in range(2)]
    S.dma("sp", lambda e: e.dma_start(out=kb_t[:], in_=att_kb), writes=["kb"])
    SCALE = 128.0 ** -0.5
    hctr = 0
    wset = 0
    kbcol0 = [0, 17, 37]
    for hs in range(8):
        S.op("pool", lambda e: e.memset(accO[:], 0.0), writes=["accO"])
        S.op("pool", lambda e: e.memset(accR[:], 0.0), writes=["accR"])
        for g in range(3):
            d = DILS[g]
            hw = 64 * d
            NT = T + 2 * hw
            tb = tbias[g % 2] if False else tbias[(hs * 3 + g) % 2]
            tbk = ("tbias", (hs * 3 + g) % 2)
            S.dma("sp", lambda e, g=g, hs=hs, tb=tb: e.dma_start(out=tb[:], in_=att_bias[g, hs]), writes=[tbk])
            ws = wset % 2
            wset += 1
            for j in range(3):
                S.dma("pool", lambda e, ws=ws, j=j, hs=hs, g=g: e.dma_start(out=wbuf[ws][j][:],
                                                                           in_=w_blk[att_blk(hs, g, j)]),
                      writes=[("w", ws, j)])
            jobs = []
            if hw == 1024:
                jobs.append(([(4, 0, 512)], 0, 512, False, 0))
                jobs.append(([(5, 0, 512)], 512, 512, False, 0))
            else:
                jobs.append(([(5, 512 - hw, hw)], 0, hw, False, 0))
            for i in range(4):
                jobs.append(([(i, 0, 512)], hw + 512 * i, 512, True, 512 * i))
            if hw == 1024:
                jobs.append(([(6, 0, 512)], hw + T, 512, False, 0))
                jobs.append(([(7, 0, 512)], hw + T + 512, 512, False, 0))
            else:
                jobs.append(([(6, 0, hw)], hw + T, hw, False, 0))
            qkv_keys = []
            for (pieces, edst, n, has_q, t0) in jobs:
                hb_i = hctr % 2
                hctr += 1
                load_h(hbuf[hb_i], ("h", hb_i), pieces)
                for j in ([0, 1, 2] if has_q else [1, 2]):
                    pi = next_ps(0, 3)
                    S.op("pe", mm_fm(psb[pi][:, 0:n], wbuf[ws][j], hbuf[hb_i], n),
                         reads=[("w", ws, j), ("h", hb_i)], writes=[("ps", pi)])
                    if j == 0:
                        dst, dk_ = qT[:, t0:t0 + n], ("qT", t0 // 512)
                    elif j == 1:
                        dst, dk_ = kT[:, edst:edst + n], ("kT", edst // 512 if n == 512 else "h%d" % edst)
                    else:
                        dst, dk_ = vT[:, edst:edst + n], ("vT", edst // 512 if n == 512 else "h%d" % edst)
                    eng = "act" if j != 1 else "dve"
                    qkv_keys.append(dk_)
                    if eng == "act":
                        S.op("act", lambda e, dst=dst, pi=pi, n=n: e.copy(out=dst, in_=psb[pi][:, 0:n]),
                             reads=[("ps", pi)], writes=[dk_])
                    else:
                        S.op("dve", lambda e, dst=dst, pi=pi, n=n: e.tensor_copy(out=dst, in_=psb[pi][:, 0:n]),
                             reads=[("ps", pi)], writes=[dk_])
            L = T // d
            nq = L // 128
            kTv = kT[:, 0:NT].rearrange("p (u r) -> p u r", r=d)
            vTv = vT[:, 0:NT].rearrange("p (u r) -> p u r", r=d)
            qTv = qT[:].rearrange("p (u r) -> p u r", r=d)
            aOv = accO[:].rearrange("p (u r) -> p u r", r=d)
            aRv = accR[:].rearrange("p (u r) -> p u r", r=d)
            ctr = 0
            for r in range(d):
                for jk in range(nq + 1):
                    b2 = ctr % 2
                    ctr += 1
                    qlo = max(jk - 1, 0)
                    qhi = min(jk + 1, nq)
                    nqq = (qhi - qlo) * 128
                    c0 = 0 if jk >= 1 else 128
                    pi = next_ps(0, 3)
                    S.op("pe", lambda e, pi=pi, jk=jk, r=r, qlo=qlo, nqq=nqq, kTv=kTv, qTv=qTv: e.matmul(
                        psb[pi][:, 0:nqq], lhsT=kTv[:, jk * 128:(jk + 1) * 128, r],
                        rhs=qTv[:, qlo * 128:qlo * 128 + nqq, r], start=True, stop=True),
                        reads=qkv_keys, writes=[("ps", pi)])
                    S.op("dve", lambda e, pi=pi, b2=b2, nqq=nqq, c0=c0, tb=tb: e.scalar_tensor_tensor(
                        out=sbuf_s[b2][:, 0:nqq], in0=psb[pi][:, 0:nqq], scalar=SCALE, in1=tb[:, c0:c0 + nqq],
                        op0=ALU.mult, op1=ALU.add), reads=[("ps", pi), tbk], writes=[("sb", b2)])
                    kc_ = kbcol0[g] + r * (nq + 1) + jk
                    S.op("act", lambda e, b2=b2, nqq=nqq, kc_=kc_: e.activation(
                        out=pT[b2][:, 0:nqq], in_=sbuf_s[b2][:, 0:nqq], func=AF.Exp, bias=kb_t[:, kc_:kc_ + 1],
                        scale=1.0), reads=[("sb", b2), "kb"], writes=[("pT", b2)])
                    pb = 0
                    rot["pt"] += 1
                    S.op("pe", lambda e, pb=pb, jk=jk, r=r, vTv=vTv: e.transpose(
                        out=ptb[pb][:, 0:128], in_=vTv[:, jk * 128:(jk + 1) * 128, r], identity=ident[:]),
                        reads=qkv_keys + ["ident"], writes=[("pt", pb)])
                    S.op("act", lambda e, pb=pb, b2=b2: e.copy(out=vt[b2][:], in_=ptb[pb][:, 0:128]),
                         reads=[("pt", pb)], writes=[("vt", b2)])
                    for qi in range(qlo, qhi):
                        col = (qi - qlo) * 128
                        first = (qi == jk)
                        last = (qi == jk - 1)
                        ob = 3 + qi % 2
                        rb = 5 + qi % 2

                        def pv(e, ob=ob, rb=rb, b2=b2, col=col, first=first, last=last):
                            e.matmul(psb[ob][:, 0:128], lhsT=vt[b2][:], rhs=pT[b2][:, col:col + 128],
                                     start=first, stop=last)
                            return e.matmul(psb[rb][:, 0:128], lhsT=ones_b[:], rhs=pT[b2][:, col:col + 128],
                                            start=first, stop=last)
                        S.op("pe", pv, reads=[("vt", b2), ("pT", b2), "ones"], writes=[("ps", ob), ("ps", rb)])
                        if last:
                            S.op("dve", lambda e, ob=ob, qi=qi, r=r, aOv=aOv: e.tensor_tensor(
                                out=aOv[:, qi * 128:(qi + 1) * 128, r], in0=aOv[:, qi * 128:(qi + 1) * 128, r],
                                in1=psb[ob][:, 0:128], op=ALU.add), reads=[("ps", ob), "accO"], writes=["accO"])
                            S.op("dve", lambda e, rb=rb, qi=qi, r=r, aRv=aRv: e.tensor_tensor(
                                out=aRv[:, qi * 128:(qi + 1) * 128, r], in0=aRv[:, qi * 128:(qi + 1) * 128, r],
                                in1=psb[rb][:, 0:128], op=ALU.add), reads=[("ps", rb), "accR"], writes=["accR"])
        ws = wset % 2
        wset += 1
        S.dma("pool", lambda e, ws=ws, hs=hs: e.dma_start(out=wbuf[ws][0][:], in_=w_blk[B_ATTZ + hs]),
              writes=[("w", ws, 0)])
        for i in range(4):
            hb_i = hctr % 2
            hctr += 1
            load_h(hbuf[hb_i], ("h", hb_i), [(i, 0, 512)])
            pi = next_ps(0, 3)
            S.op("pe", mm_fm(psb[pi][:, 0:512], wbuf[ws][0], hbuf[hb_i], 512),
                 reads=[("w", ws, 0), ("h", hb_i)], writes=[("ps", pi)])
            b2 = i % 2
            S.op("act", lambda e, b2=b2, pi=pi: e.activation(out=zs[b2][:], in_=psb[pi][:, 0:512], func=AF.Silu),
                 reads=[("ps", pi)], writes=[("zs", b2)])
            S.op("dve", lambda e, b2=b2, i=i: e.reciprocal(out=rr[b2][:], in_=accR[:, i * 512:(i + 1) * 512]),
                 reads=["accR"], writes=[("rr", b2)])
            S.op("dve", lambda e, b2=b2, i=i: e.tensor_tensor(out=rr[b2][:], in0=rr[b2][:],
                                                             in1=accO[:, i * 512:(i + 1) * 512], op=ALU.mult),
                 reads=["accO", ("rr", b2)], writes=[("rr", b2)])
            S.op("dve", lambda e, b2=b2: e.tensor_tensor(out=yst[b2][:], in0=rr[b2][:], in1=zs[b2][:], op=ALU.mult),
                 reads=[("rr", b2), ("zs", b2)], writes=[("yst", b2)])
            S.dma("sp", lambda e, b2=b2, hs=hs, i=i: e.dma_start(
                out=yaT_d[hs * 128:(hs + 1) * 128, i * 512:(i + 1) * 512], in_=yst[b2][:]),
                reads=[("yst", b2)], writes=[("yaT_d", hs, i)])
    S.barrier()
    A.reset(base_mark)

    hbuf = [A.alloc([128, 32, 512], BF16, "hbuf") for _ in range(2)]
    wfm = [A.alloc([128, 32, 128], BF16, "wfm") for _ in range(4)]
    wtm = [A.alloc([128, 32, 512], BF16, "wtm") for _ in range(2)]
    stg = [A.alloc([128, 512], F32, "stg") for _ in range(4)]
    stgb = [A.alloc([128, 512], BF16, "stgb") for _ in range(4)]
    wclr = A.alloc([128, 32, 16], BF16, "wclr")
    sctr = 0
    hctr = 0
    fm_list = [(B_GQ + i, gqT_d, i * 128, 128, 256.0 ** -0.5) for i in range(8)]
    fm_list += [(B_GK + i, gkT_d, i * 128, 128, 1.0) for i in range(8)]
    fm_list += [(B_LR, lrT_d, 0, 32, 1.0)]
    for c4 in range(0, len(fm_list), 4):
        grp = fm_list[c4:c4 + 4]
        for wi, (blk, dst, r0, m, sc) in enumerate(grp):
            S.dma("pool", lambda e, wi=wi, blk=blk: e.dma_start(out=wfm[wi][:], in_=w_blk[blk]), writes=[("wfm", wi)])
        for i in range(4):
            hb_i = hctr % 2
            hctr += 1
            load_h(hbuf[hb_i], ("h", hb_i), [(i, 0, 512)])
            for wi, (blk, dst, r0, m, sc) in enumerate(grp):
                pi = next_ps()
                S.op("pe", mm_fm(psb[pi][0:m, 0:512], wfm[wi], hbuf[hb_i], 512, wsl=(0, m)),
                     reads=[("wfm", wi), ("h", hb_i)], writes=[("ps", pi)])
                sb_i = sctr % 4
                sctr += 1
                S.op("act", lambda e, sb_i=sb_i, pi=pi, m=m, sc=sc: e.activation(
                    out=stg[sb_i][0:m, :], in_=psb[pi][0:m, 0:512], func=AF.Copy, scale=sc),
                    reads=[("ps", pi)], writes=[("stg", sb_i)])
                S.dma("sp", lambda e, sb_i=sb_i, dst=dst, r0=r0, m=m, i=i: e.dma_start(
                    out=dst[r0:r0 + m, i * 512:(i + 1) * 512], in_=stg[sb_i][0:m, :]),
                    reads=[("stg", sb_i)], writes=[("gfm", id(dst), r0, i)])
    tm_list = []
    for cg in range(2):
        tm_list.append((B_GK + 4 * cg, "k", cg))
    for cg in range(4):
        tm_list.append((B_GV + 4 * cg, "v", cg))
    for cg in range(4):
        tm_list.append((B_GZ + 4 * cg, "z", cg))
    wctr = 0
    for (blk0, kind, cg) in tm_list:
        wi = wctr % 2
        wctr += 1
        for b4 in range(4):
            S.dma("pool", lambda e, wi=wi, blk0=blk0, b4=b4: e.dma_start(
                out=wtm[wi][:, :, b4 * 128:(b4 + 1) * 128], in_=w_blk[blk0 + b4]), writes=[("wtm", wi)])
        tiles = list(range(4)) + ([] if kind == "z" else list(range(8, 20)))
        for tl in tiles:
            hb_i = hctr % 2
            hctr += 1
            load_h(hbuf[hb_i], ("h", hb_i), [(tl, 0, 512)])
            for sub in range(4):
                pi = next_ps()
                S.op("pe", mm_tm(psb[pi][:, 0:512], hbuf[hb_i], sub * 128, wtm[wi], 512),
                     reads=[("wtm", wi), ("h", hb_i)], writes=[("ps", pi)])
                sb_i = sctr % 4
                sctr += 1
                if tl < 4:
                    row0 = tl * 512 + sub * 128
                    dk, dv_, dz = gk_d, gv_d, gzs_d
                else:
                    row0 = (tl - 8) * 512 + sub * 128
                    dk, dv_, dz = ck_d, cv_d, None
                if kind == "k":
                    S.op("act", lambda e, sb_i=sb_i, pi=pi: e.copy(out=stg[sb_i][:], in_=psb[pi][:, 0:512]),
                         reads=[("ps", pi)], writes=[("stg", sb_i)])
                    S.dma("sp", lambda e, sb_i=sb_i, dk=dk, row0=row0, cg=cg: e.dma_start(
                        out=dk[row0:row0 + 128, cg * 512:(cg + 1) * 512], in_=stg[sb_i][:]),
                        reads=[("stg", sb_i)], writes=[("gtm", kind, cg, tl, sub)])
                elif kind == "v":
                    S.op("dve", lambda e, sb_i=sb_i, pi=pi: e.tensor_copy(out=stgb[sb_i][:], in_=psb[pi][:, 0:512]),
                         reads=[("ps", pi)], writes=[("stgb", sb_i)])
                    S.dma("sp", lambda e, sb_i=sb_i, dv_=dv_, row0=row0, cg=cg: e.dma_start(
                        out=dv_[row0:row0 + 128, cg * 512:(cg + 1) * 512], in_=stgb[sb_i][:]),
                        reads=[("stgb", sb_i)], writes=[("gtm", kind, cg, tl, sub)])
                else:
                    S.op("act", lambda e, sb_i=sb_i, pi=pi: e.activation(out=stg[sb_i][:], in_=psb[pi][:, 0:512],
                                                                         func=AF.Silu),
                         reads=[("ps", pi)], writes=[("stg", sb_i)])
                    S.dma("sp", lambda e, sb_i=sb_i, dz=dz, row0=row0, cg=cg: e.dma_start(
                        out=dz[row0:row0 + 128, cg * 512:(cg + 1) * 512], in_=stg[sb_i][:]),
                        reads=[("stg", sb_i)], writes=[("gtm", kind, cg, tl, sub)])
    for sl in range(3):
        S.dma("pool", lambda e, sl=sl: e.dma_start(out=wclr[:], in_=w_clr[sl]), writes=["wclr"])
        for i in range(4):
            hb_i = hctr % 2
            hctr += 1
            load_h(hbuf[hb_i], ("h", hb_i), [(8 + sl * 4 + i, 0, 512)])
            pi = next_ps()
            S.op("pe", mm_fm(psb[pi][0:16, 0:512], wclr, hbuf[hb_i], 512),
                 reads=["wclr", ("h", hb_i)], writes=[("ps", pi)])
            sb_i = sctr % 4
            sctr += 1
            S.op("act", lambda e, sb_i=sb_i, pi=pi: e.copy(out=stg[sb_i][0:16, :], in_=psb[pi][0:16, 0:512]),
                 reads=[("ps", pi)], writes=[("stg", sb_i)])
            S.dma("sp", lambda e, sb_i=sb_i, sl=sl, i=i: e.dma_start(
                out=clrT_d[sl, :, i * 512:(i + 1) * 512], in_=stg[sb_i][0:16, :]),
                reads=[("stg", sb_i)], writes=[("clr", sl, i)])
    S.barrier()
    A.reset(base_mark)

    tri_t = A.alloc([128, 4, 128], F32, "tri")
    msk_t = A.alloc([128, 2, 128], F32, "msk")
    for i4 in range(4):
        S.dma("sp", lambda e, i4=i4: e.dma_start(out=tri_t[:, i4, :], in_=tri[i4]), writes=["tri"])
    for i2 in range(2):
        S.dma("sp", lambda e, i2=i2: e.dma_start(out=msk_t[:, i2, :], in_=msk[i2]), writes=["msk"])

    def gla_thread(h):
        X = psb[2 * h]
        Y = psb[2 * h + 1]
        kx = ("ps", 2 * h)
        ky = ("ps", 2 * h + 1)
        K_ = lambda name, *a: (name, h) + a
        lra = [A.alloc([64, 128], F32, "lra") for _ in range(3)]
        w2a = A.alloc([64, 256], F32, "w2a")
        SA = A.alloc([128, 2, 512], F32, "SA")
        SB = A.alloc([128, 2, 512], F32, "SB")
        Sbf = A.alloc([128, 2, 512], BF16, "Sbf")
        e_t = [A.alloc([128, 256], F32, "e_t") for _ in range(2)]
        sp_t = [A.alloc([128, 256], F32, "sp_t") for _ in range(2)]
        eb_t = [A.alloc([128, 2, 128], F32, "eb_t") for _ in range(2)]
        enb_t = [A.alloc([128, 2, 128], F32, "enb_t") for _ in range(2)]
        ekd_t = [A.alloc([128, 256], F32, "ekd_t") for _ in range(2)]
        kd_t = [A.alloc([128, 256], BF16, "kd_t") for _ in range(2)]
        qe_t = [A.alloc([128, 2, 128], BF16, "qe_t") for _ in range(2)]
        ke_t = [A.alloc([128, 2, 128], BF16, "ke_t") for _ in range(2)]
        AT_t = [A.alloc([128, 128], BF16, "AT_t") for _ in range(2)]
        ksc = [A.alloc([128, 256], F32, "ksc") for _ in range(3)]
        vsc = [A.alloc([128, 512], BF16, "vsc") for _ in range(3)]
        qsc = [A.alloc([128, 2, 128], F32, "qsc") for _ in range(3)]
        ktsc = [A.alloc([128, 2, 128], F32, "ktsc") for _ in range(3)]
        o_s = [A.alloc([128, 512], F32, "o_s") for _ in range(3)]
        cc = {"n": 0, "sc": 0, "o": 0}

        for b in range(3):
            S.op("dve", lambda e, b=b: e.memset(lra[b][:], 0.0), writes=[K_("lra", b)])
            S.op("dve", lambda e, b=b: e.memset(lra[b][32:33, :], 1.0), writes=[K_("lra", b)])
        S.op("dve", lambda e: e.memset(w2a[:], 0.0), writes=[K_("w2a")])

        def set_w2(w2src):
            S.dma("sp", lambda e: e.dma_start(out=w2a[0:33, :], in_=w2src[:, h * 256:(h + 1) * 256]),
                  writes=[K_("w2a")])

        def load_sc(lr_rows, kd_src, vd_src, tok0, sl):
            S.dma("sp", lambda e: e.dma_start(out=lra[sl][0:16, :], in_=lr_rows), writes=[K_("lra", sl)])
            S.dma("sp", lambda e: e.dma_start(out=ksc[sl][:], in_=kd_src[tok0:tok0 + 128, h * 256:(h + 1) * 256]),
                  writes=[K_("ksc", sl)])
            S.dma("sp", lambda e: e.dma_start(out=vsc[sl][:], in_=vd_src[tok0:tok0 + 128, h * 512:(h + 1) * 512]),
                  writes=[K_("vsc", sl)])

        def decays(sl, di, need_fm, b):
            S.op("pe", lambda e: e.matmul(X[:, 0:256], lhsT=lra[sl][0:33, :], rhs=w2a[0:33, :],
                                          start=True, stop=True), reads=[K_("lra", sl), K_("w2a")], writes=[kx])
            S.op("act", lambda e: e.activation(out=e_t[b][:], in_=X[:, 0:256], func=AF.Exp, scale=-1.0),
                 reads=[kx], writes=[K_("e_t", b)])
            S.op("act", lambda e: e.activation(out=sp_t[b][:], in_=e_t[b][:], func=AF.Ln, bias=1.0, scale=1.0),
                 reads=[K_("e_t", b)], writes=[K_("sp_t", b)])

            def bfm(e):
                ins = None
                for kk in range(2):
                    ins = e.matmul(X[:, kk * 128:(kk + 1) * 128], lhsT=sp_t[b][:, kk * 128:(kk + 1) * 128],
                                   rhs=tri_t[:, 2 * di, :], start=True, stop=True)
                return ins
            S.op("pe", bfm, reads=[K_("sp_t", b), "tri"], writes=[kx])
            S.op("act", lambda e: e.activation(out=eb_t[b][:], in_=X[:, 0:256].rearrange("p (k t) -> p k t", k=2),
                                               func=AF.Exp), reads=[kx], writes=[K_("eb_t", b)])
            if need_fm:
                S.op("act", lambda e: e.activation(out=enb_t[b][:], in_=X[:, 0:256].rearrange("p (k t) -> p k t", k=2),
                                                   func=AF.Exp, scale=-1.0), reads=[kx], writes=[K_("enb_t", b)])
            S.op("pe", lambda e: e.matmul(X[:, 256:512], lhsT=tri_t[:, 2 * di + 1, :], rhs=sp_t[b][:],
                                          start=True, stop=True), reads=[K_("sp_t", b), "tri"], writes=[kx])
            S.op("act", lambda e: e.activation(out=ekd_t[b][:], in_=X[:, 256:512], func=AF.Exp),
                 reads=[kx], writes=[K_("ekd_t", b)])
            S.op("dve", lambda e: e.tensor_tensor(out=kd_t[b][:], in0=ksc[sl][:], in1=ekd_t[b][:], op=ALU.mult),
                 reads=[K_("ksc", sl), K_("ekd_t", b)], writes=[K_("kd_t", b)])

        def state_update(St, skey, sl, di, b, cast):
            last = 127 if di == 0 else 0
            for kk in range(2):
                S.op("pe", lambda e, kk=kk: e.matmul(Y[:, 0:512], lhsT=kd_t[b][:, kk * 128:(kk + 1) * 128],
                                                      rhs=vsc[sl][:], start=True, stop=True),
                     reads=[K_("kd_t", b), K_("vsc", sl)], writes=[ky])
                S.op("dve", lambda e, kk=kk: e.scalar_tensor_tensor(
                    out=St[:, kk, :], in0=St[:, kk, :], scalar=eb_t[b][:, kk, last:last + 1], in1=Y[:, 0:512],
                    op0=ALU.mult, op1=ALU.add), reads=[ky, K_("eb_t", b), K_(skey, kk)], writes=[K_(skey, kk)])
            if cast:
                S.op("act", lambda e: e.copy(out=Sbf[:], in_=St[:]), reads=[K_(skey, 0), K_(skey, 1)],
                     writes=[K_("Sbf")])

        S.op("dve", lambda e: e.memset(SB[:], 0.0), writes=[K_("SB", 0), K_("SB", 1)])
        S.op("dve", lambda e: e.memset(SA[:], 0.0), writes=[K_("SA", 0), K_("SA", 1)])

        def boundary(i):
            for kk in range(2):
                S.op("dve", lambda e, kk=kk: e.scalar_tensor_tensor(
                    out=SA[:, kk, :], in0=SB[:, kk, :], scalar=bnd_t[:, i:i + 1], in1=SA[:, kk, :],
                    op0=ALU.mult, op1=ALU.add), reads=[K_("SB", kk), K_("SA", kk), "bnd"], writes=[K_("SA", kk)])
                S.op("dve", lambda e, kk=kk: e.tensor_scalar(
                    out=SB[:, kk, :], in0=SB[:, kk, :], scalar1=bnd_t[:, 4 + i:5 + i], scalar2=None, op0=ALU.mult),
                    reads=[K_("SB", kk), "bnd"], writes=[K_("SB", kk)])
        for sl_ in range(3):
            boundary(sl_)
            set_w2(cw2aug[sl_])
            for s1 in range(16):
                sl = cc["sc"] % 3
                cc["sc"] += 1
                t0 = s1 * 128
                load_sc(clrT_d[sl_, :, t0:t0 + 128], ck_d, cv_d, sl_ * T + t0, sl)
                b = cc["n"] % 2
                cc["n"] += 1
                decays(sl, 0, False, b)
                state_update(SB, "SB", sl, 0, b, False)
        boundary(3)
        for di in range(2):
            St = SA if di == 0 else SB
            skey = "SA" if di == 0 else "SB"
            S.op("act", lambda e, St=St: e.copy(out=Sbf[:], in_=St[:]), reads=[K_(skey, 0), K_(skey, 1)],
                 writes=[K_("Sbf")])
            set_w2(w2aug[di])
            for s1 in (range(16) if di == 0 else range(15, -1, -1)):
                sl = cc["sc"] % 3
                cc["sc"] += 1
                tok0 = s1 * 128
                load_sc(lrT_d[16 * di:16 * di + 16, tok0:tok0 + 128], gk_d, gv_d, tok0, sl)
                S.dma("sp", lambda e, sl=sl, tok0=tok0: e.dma_start(
                    out=qsc[sl][:], in_=gqT_d[h * 256:(h + 1) * 256, tok0:tok0 + 128].rearrange("(k p) t -> p k t", p=128)),
                    writes=[K_("qsc", sl)])
                S.dma("sp", lambda e, sl=sl, tok0=tok0: e.dma_start(
                    out=ktsc[sl][:], in_=gkT_d[h * 256:(h + 1) * 256, tok0:tok0 + 128].rearrange("(k p) t -> p k t", p=128)),
                    writes=[K_("ktsc", sl)])
                b = cc["n"] % 2
                cc["n"] += 1
                decays(sl, di, True, b)
                S.op("dve", lambda e, b=b, sl=sl: e.tensor_tensor(out=qe_t[b][:], in0=qsc[sl][:], in1=eb_t[b][:],
                                                                  op=ALU.mult),
                     reads=[K_("qsc", sl), K_("eb_t", b)], writes=[K_("qe_t", b)])
                S.op("dve", lambda e, b=b, sl=sl: e.tensor_tensor(out=ke_t[b][:], in0=ktsc[sl][:], in1=enb_t[b][:],
                                                                  op=ALU.mult),
                     reads=[K_("ktsc", sl), K_("enb_t", b)], writes=[K_("ke_t", b)])

                def amm(e, b=b):
                    ins = None
                    for kk in range(2):
                        ins = e.matmul(X[:, 0:128], lhsT=ke_t[b][:, kk, :], rhs=qe_t[b][:, kk, :],
                                       start=(kk == 0), stop=(kk == 1))
                    return ins
                S.op("pe", amm, reads=[K_("qe_t", b), K_("ke_t", b)], writes=[kx])
                S.op("dve", lambda e, b=b, di=di: e.tensor_tensor(
                    out=AT_t[b][:], in0=X[:, 0:128], in1=msk_t[:, di, :], op=ALU.mult),
                    reads=[kx, "msk"], writes=[K_("AT_t", b)])

                def omm(e, b=b, sl=sl):
                    e.matmul(Y[:, 0:512], lhsT=AT_t[b][:], rhs=vsc[sl][:], start=True, stop=False)
                    e.matmul(Y[:, 0:512], lhsT=qe_t[b][:, 0, :], rhs=Sbf[:, 0, :], start=False, stop=False)
                    return e.matmul(Y[:, 0:512], lhsT=qe_t[b][:, 1, :], rhs=Sbf[:, 1, :], start=False, stop=True)
                S.op("pe", omm, reads=[K_("AT_t", b), K_("vsc", sl), K_("qe_t", b), K_("Sbf")], writes=[ky])
                ob_ = cc["o"] % 3
                cc["o"] += 1
                S.op("act", lambda e, ob_=ob_: e.copy(out=o_s[ob_][:], in_=Y[:, 0:512]), reads=[ky],
                     writes=[K_("o_s", ob_)])
                S.dma("pool", lambda e, ob_=ob_, tok0=tok0, di=di: e.dma_start(
                    out=o_d[di, tok0:tok0 + 128, h * 512:(h + 1) * 512], in_=o_s[ob_][:]),
                    reads=[K_("o_s", ob_)], writes=[("o_d", di, h, tok0)])
                state_update(St, skey, sl, di, b, True)

    threads = []
    for h in range(4):
        S.rec_begin()
        gla_thread(h)
        threads.append(S.rec_end())
    S.merge(threads)
    S.barrier()
    A.reset(base_mark)
    gn_t = A.alloc([128, 512], F32, "gn")
    S.dma("sp", lambda e: e.dma_start(out=gn_t[:], in_=gng_b), writes=["gn"])
    of_t = [A.alloc([128, 2048], F32, "of_t") for _ in range(2)]
    ob_t = [A.alloc([128, 2048], F32, "ob_t") for _ in range(2)]
    gz_t = [A.alloc([128, 2048], F32, "gz_t") for _ in range(2)]
    yb_t = [A.alloc([128, 2048], BF16, "yb_t") for _ in range(2)]
    junk2 = A.alloc([128, 512], BF16, "junk2")
    st2 = [A.alloc([128, 16], F32, "st2") for _ in range(2)]
    ybT_s = [A.alloc([128, 16, 512], BF16, "ybT_s") for _ in range(2)]
    for gi in range(4):
        ys = gi % 2
        for sub in range(4):
            b = (gi * 4 + sub) % 2
            r0 = gi * 512 + sub * 128
            S.dma("sp", lambda e, b=b, r0=r0: e.dma_start(out=of_t[b][:], in_=o_d[0, r0:r0 + 128, :]), writes=[("of", b)])
            S.dma("sp", lambda e, b=b, r0=r0: e.dma_start(out=ob_t[b][:], in_=o_d[1, r0:r0 + 128, :]), writes=[("ob", b)])
            S.dma("sp", lambda e, b=b, r0=r0: e.dma_start(out=gz_t[b][:], in_=gzs_d[r0:r0 + 128, :]), writes=[("gz", b)])
            S.op("pool", lambda e, b=b: e.tensor_tensor(out=of_t[b][:], in0=of_t[b][:], in1=ob_t[b][:], op=ALU.add),
                 reads=[("of", b), ("ob", b)], writes=[("of", b)])
            S.op("dve", lambda e, b=b: e.memset(st2[b][:], 0.0), writes=[("st2", b)])
            for hh in range(4):
                S.op("act", lambda e, b=b, hh=hh: e.activation(out=junk2[:], in_=of_t[b][:, hh * 512:(hh + 1) * 512],
                                                                func=AF.Square, accum_out=st2[b][:, hh:hh + 1]),
                     reads=[("of", b), ("st2", b)], writes=["junk2", ("st2", b)])
            S.op("dve", lambda e, b=b: e.tensor_scalar(out=st2[b][:, 4:8], in0=st2[b][:, 0:4], scalar1=1.0 / 512,
                                                        scalar2=EPS, op0=ALU.mult, op1=ALU.add),
                 reads=[("st2", b)], writes=[("st2", b)])
            S.op("act", lambda e, b=b: e.activation(out=st2[b][:, 8:12], in_=st2[b][:, 4:8], func=AF.Sqrt),
                 reads=[("st2", b)], writes=[("st2", b)])
            S.op("dve", lambda e, b=b: e.reciprocal(out=st2[b][:, 12:16], in_=st2[b][:, 8:12]),
                 reads=[("st2", b)], writes=[("st2", b)])
            for hh in range(4):
                S.op("dve", lambda e, b=b, hh=hh: e.scalar_tensor_tensor(
                    out=of_t[b][:, hh * 512:(hh + 1) * 512], in0=of_t[b][:, hh * 512:(hh + 1) * 512],
                    scalar=st2[b][:, 12 + hh:13 + hh], in1=gn_t[:], op0=ALU.mult, op1=ALU.mult),
                    reads=[("of", b), ("st2", b), "gn"], writes=[("of", b)])
            S.op("dve", lambda e, b=b: e.tensor_tensor(out=yb_t[b][:], in0=of_t[b][:], in1=gz_t[b][:], op=ALU.mult),
                 reads=[("of", b), ("gz", b)], writes=[("yb", b)])
            for q2 in range(2):
                def trf(e, b=b, q2=q2):
                    ins = None
                    for k in range(8):
                        kc = q2 * 8 + k
                        ins = e.transpose(out=ptb[0][:, k * 128:(k + 1) * 128], in_=yb_t[b][:, kc * 128:(kc + 1) * 128],
                                          identity=ident[:])
                    return ins
                S.op("pe", trf, reads=[("yb", b), "ident"], writes=[("pt", 0)])
                S.op("act", lambda e, ys=ys, q2=q2, sub=sub: e.copy(
                    out=ybT_s[ys][:, q2 * 8:(q2 + 1) * 8, sub * 128:(sub + 1) * 128],
                    in_=ptb[0][:].rearrange("p (k t) -> p k t", k=8)), reads=[("pt", 0)], writes=[("ybT_s", ys, sub, q2)])
        S.dma("sp", lambda e, ys=ys, gi=gi: e.dma_start(
            out=ybT_d[:, gi * 512:(gi + 1) * 512].rearrange("(k p) t -> p k t", p=128), in_=ybT_s[ys][:]),
            reads=[("ybT_s", ys, s_, q_) for s_ in range(4) for q_ in range(2)], writes=[("ybT_d", gi)])
    S.barrier()
    A.reset(base_mark)

    mergedT = A.alloc([128, 32, 512], BF16, "mergedT")
    ov_mark = A.mark()
    for tt in range(4):
        A.reset(ov_mark)
        hT1 = A.alloc([128, 32, 512], BF16, "hT1")
        yaT_t = A.alloc([128, 8, 512], BF16, "yaT_t")
        ybT_t = A.alloc([128, 16, 512], BF16, "ybT_t")
        wg = [[A.alloc([128, 32, 128], BF16, "wg") for _ in range(2)] for _ in range(2)]
        wa = [A.alloc([128, 8, 128], BF16, "wa") for _ in range(2)]
        wb_ = [A.alloc([128, 16, 128], BF16, "wb") for _ in range(2)]
        sg = [[A.alloc([128, 512], F32, "sg") for _ in range(2)] for _ in range(2)]
        mt = [A.alloc([128, 512], F32, "mt") for _ in range(2)]
        load_h(hT1, "hT1", [(tt, 0, 512)])
        S.dma("sp", lambda e, tt=tt: e.dma_start(
            out=yaT_t[:], in_=yaT_d[:, tt * 512:(tt + 1) * 512].rearrange("(k p) t -> p k t", p=128)), writes=["yaT_t"])
        S.dma("sp", lambda e, tt=tt: e.dma_start(
            out=ybT_t[:], in_=ybT_d[:, tt * 512:(tt + 1) * 512].rearrange("(k p) t -> p k t", p=128)), writes=["ybT_t"])
        for blk in range(32):
            b = blk % 2
            S.dma("pool", lambda e, b=b, blk=blk: e.dma_start(out=wg[b][0][:], in_=w_blk[B_MGA + blk]),
                  writes=[("wg", b, 0)])
            S.dma("pool", lambda e, b=b, blk=blk: e.dma_start(out=wg[b][1][:], in_=w_blk[B_MGB + blk]),
                  writes=[("wg", b, 1)])
            S.dma("pool", lambda e, b=b, blk=blk: e.dma_start(out=wa[b][:], in_=wua[blk]), writes=[("wa", b)])
            S.dma("pool", lambda e, b=b, blk=blk: e.dma_start(out=wb_[b][:], in_=wub[blk]), writes=[("wb", b)])
            for ab in range(2):
                pi = next_ps()
                S.op("pe", mm_fm(psb[pi][:, 0:512], wg[b][ab], hT1, 512), reads=[("wg", b, ab), "hT1"],
                     writes=[("ps", pi)])
                S.op("act", lambda e, b=b, ab=ab, pi=pi: e.activation(out=sg[b][ab][:], in_=psb[pi][:, 0:512],
                                                                      func=AF.Sigmoid),
                     reads=[("ps", pi)], writes=[("sg", b, ab)])
            pa = next_ps()
            S.op("pe", mm_fm(psb[pa][:, 0:512], wa[b], yaT_t, 512, nk=8), reads=[("wa", b), "yaT_t"],
                 writes=[("ps", pa)])
            S.op("dve", lambda e, b=b, pa=pa: e.tensor_tensor(out=mt[b][:], in0=sg[b][0][:], in1=psb[pa][:, 0:512],
                                                              op=ALU.mult),
                 reads=[("ps", pa), ("sg", b, 0)], writes=[("mt", b)])
            pbb = next_ps()
            S.op("pe", mm_fm(psb[pbb][:, 0:512], wb_[b], ybT_t, 512, nk=16), reads=[("wb", b), "ybT_t"],
                 writes=[("ps", pbb)])
            S.op("dve", lambda e, b=b, pbb=pbb: e.tensor_tensor(out=sg[b][1][:], in0=sg[b][1][:],
                                                                in1=psb[pbb][:, 0:512], op=ALU.mult),
                 reads=[("ps", pbb), ("sg", b, 1)], writes=[("sg", b, 1)])
            S.op("dve", lambda e, b=b, blk=blk: e.tensor_tensor(out=mergedT[:, blk, :], in0=mt[b][:],
                                                                in1=sg[b][1][:], op=ALU.add),
                 reads=[("mt", b), ("sg", b, 1)], writes=[("merged", blk)])
        S.barrier()
        A.reset(ov_mark)
        wo_t = [A.alloc([128, 32, 256], BF16, "wo_t") for _ in range(2)]
        ypre = [A.alloc([128, D], F32, "ypre") for _ in range(4)]
        xs = [A.alloc([128, 256], F32, "xs") for _ in range(4)]
        gs = [A.alloc([128, 512], F32, "gs") for _ in range(2)]
        st3 = A.alloc([128, 8], F32, "st3")
        st4 = A.alloc([128, 16], F32, "st4")
        junk3 = A.alloc([128, 512], BF16, "junk3")
        xc = 0
        for cb in range(16):
            b = cb % 2
            S.dma("pool", lambda e, b=b, cb=cb: e.dma_start(out=wo_t[b][:], in_=wo[cb]), writes=[("wo_t", b)])
            for sub in range(4):
                xi = xc % 4
                xc += 1
                r0 = tt * 512 + sub * 128
                S.dma("sp", lambda e, xi=xi, r0=r0, cb=cb: e.dma_start(out=xs[xi][:],
                                                                      in_=x_own[r0:r0 + 128, cb * 256:(cb + 1) * 256]),
                      writes=[("xs", xi)])
                pi = next_ps()

                def omm2(e, pi=pi, sub=sub, b=b):
                    ins = None
                    for kc in range(32):
                        ins = e.matmul(psb[pi][:, 0:256], lhsT=mergedT[:, kc, sub * 128:(sub + 1) * 128],
                                       rhs=wo_t[b][:, kc, :], start=(kc == 0), stop=(kc == 31))
                    return ins
                S.op("pe", omm2, reads=[("wo_t", b)] + [("merged", k_) for k_ in range(32)], writes=[("ps", pi)])
                S.op("dve", lambda e, pi=pi, sub=sub, cb=cb, xi=xi: e.tensor_tensor(
                    out=ypre[sub][:, cb * 256:(cb + 1) * 256], in0=psb[pi][:, 0:256], in1=xs[xi][:], op=ALU.add),
                    reads=[("ps", pi), ("xs", xi)], writes=[("ypre", sub, cb)])
        for sub in range(4):
            S.op("dve", lambda e: e.memset(st3[:], 0.0), writes=[("st3", c_) for c_ in range(8)])
            for c8 in range(8):
                S.op("act", lambda e, sub=sub, c8=c8: e.activation(
                    out=junk3[:], in_=ypre[sub][:, c8 * 512:(c8 + 1) * 512], func=AF.Square,
                    accum_out=st4[:, sub * 4 + 0:sub * 4 + 1] if False else st3[:, c8:c8 + 1]),
                    reads=[("ypre", sub, 2 * c8), ("ypre", sub, 2 * c8 + 1)], writes=["junk3", ("st3", c8)])
            S.op("dve", lambda e, sub=sub: e.tensor_reduce(out=st4[:, sub * 4:sub * 4 + 1], in_=st3[:, 0:8],
                                                           axis=mybir.AxisListType.X, op=ALU.add),
                 reads=[("st3", c_) for c_ in range(8)], writes=[("st4", sub, 0)])
            S.op("dve", lambda e, sub=sub: e.tensor_scalar(out=st4[:, sub * 4 + 1:sub * 4 + 2],
                                                            in0=st4[:, sub * 4:sub * 4 + 1], scalar1=1.0 / D,
                                                            scalar2=EPS, op0=ALU.mult, op1=ALU.add),
                 reads=[("st4", sub, 0)], writes=[("st4", sub, 1)])
            S.op("act", lambda e, sub=sub: e.activation(out=st4[:, sub * 4 + 2:sub * 4 + 3],
                                                        in_=st4[:, sub * 4 + 1:sub * 4 + 2], func=AF.Sqrt),
                 reads=[("st4", sub, 1)], writes=[("st4", sub, 2)])
            S.op("dve", lambda e, sub=sub: e.reciprocal(out=st4[:, sub * 4 + 3:sub * 4 + 4],
                                                        in_=st4[:, sub * 4 + 2:sub * 4 + 3]),
                 reads=[("st4", sub, 2)], writes=[("st4", sub, 3)])
            for c8 in range(8):
                gi = c8 % 2
                S.dma("sp", lambda e, gi=gi, c8=c8: e.dma_start(out=gs[gi][:], in_=fng_b[:, c8 * 512:(c8 + 1) * 512]),
                      writes=[("gs", gi)])
                S.op("dve", lambda e, sub=sub, c8=c8, gi=gi: e.scalar_tensor_tensor(
                    out=ypre[sub][:, c8 * 512:(c8 + 1) * 512], in0=ypre[sub][:, c8 * 512:(c8 + 1) * 512],
                    scalar=st4[:, sub * 4 + 3:sub * 4 + 4], in1=gs[gi][:], op0=ALU.mult, op1=ALU.mult),
                    reads=[("ypre", sub, 2 * c8), ("ypre", sub, 2 * c8 + 1), ("st4", sub, 3), ("gs", gi)],
                    writes=[("ypre", sub, 2 * c8), ("ypre", sub, 2 * c8 + 1)])
            r0 = tt * 512 + sub * 128
            S.dma("sp", lambda e, sub=sub, r0=r0: e.dma_start(out=y_out[r0:r0 + 128, :], in_=ypre[sub][:]),
                  reads=[("ypre", sub, c_) for c_ in range(16)], writes=[("y_out", r0)])
        S.barrier()
    S.emit()
    return nc


def _consts():
    slopes = np.exp2(-8.0 * (np.arange(8, dtype=np.float32) + 1.0) / 8).astype(np.float32)
    i = np.arange(128)[:, None]
    c = np.arange(256)[None, :]
    delta = i - c + 64
    att_bias = np.zeros((3, 8, 128, 256), np.float32)
    for g, d in enumerate(DILS):
        dist = (np.abs(delta) * d).astype(np.float32)
        for h in range(8):
            b = -slopes[h] * dist
            att_bias[g, h] = np.where(np.abs(delta) <= 64, b, np.float32(-30000.0))
    s = np.arange(128)[:, None]
    t = np.arange(128)[None, :]
    k = np.float32(-1.0 / 16.0)
    tri = np.zeros((4, 128, 128), np.float32)
    tri[0] = np.where(s <= t, k, 0)
    tri[1] = np.where(s > t, k, 0)
    tri[2] = np.where(s >= t, k, 0)
    tri[3] = np.where(s < t, k, 0)
    msk = np.zeros((2, 128, 128), np.float32)
    msk[0] = (s <= t)
    msk[1] = (s > t)
    return att_bias, tri, msk


def _prep_shared(inp):
    w_in = np.asarray(inp["w_in"])[0]
    cols = block_cols()
    idx = np.zeros(NBLK * 128, np.int64)
    valid = np.zeros(NBLK * 128, bool)
    for b, (c0, n) in enumerate(cols):
        idx[b * 128:b * 128 + n] = np.arange(c0, c0 + n)
        valid[b * 128:b * 128 + n] = True
    wsel = np.take(w_in, idx, axis=1)
    wsel[:, ~valid] = 0.0
    w_blk = np.ascontiguousarray(wsel.reshape(32, 128, NBLK, 128).transpose(2, 1, 0, 3))
    wua = np.ascontiguousarray(np.asarray(inp["w_up_a"])[0].reshape(8, 128, 32, 128).transpose(2, 1, 0, 3))
    wub = np.ascontiguousarray(np.asarray(inp["w_up_b"])[0].reshape(16, 128, 32, 128).transpose(2, 1, 0, 3))
    wo = np.ascontiguousarray(np.asarray(inp["w_o"])[0].reshape(32, 128, 16, 256).transpose(2, 1, 0, 3))
    att_bias, tri, msk = _consts()
    w2 = [np.asarray(inp["gla_w2_f"])[0], np.asarray(inp["gla_w2_b"])[0]]
    bb = [np.asarray(inp["gla_b_f"])[0], np.asarray(inp["gla_b_b"])[0]]
    w2aug = np.zeros((2, 33, 1024), np.float32)
    for di in range(2):
        w2aug[di, 0:16] = w2[di]
        w2aug[di, 32] = bb[di]
    wlr = [np.ascontiguousarray(w_in[:, 16384:16400].reshape(32, 128, 16).transpose(1, 0, 2)),
           np.ascontiguousarray(w_in[:, 16400:16416].reshape(32, 128, 16).transpose(1, 0, 2))]
    sh = dict(
        w_blk=w_blk, wua=wua, wub=wub, wo=wo,
        ng_b=np.ascontiguousarray(np.broadcast_to(np.asarray(inp["norm_g"])[0][None, :], (128, D))).astype(np.float32),
        fng_b=np.ascontiguousarray(np.broadcast_to(np.asarray(inp["final_norm_g"])[None, :], (128, D))).astype(np.float32),
        gng_b=np.ascontiguousarray(np.broadcast_to(np.asarray(inp["gla_norm_g"])[0][None, :], (128, 512))).astype(np.float32),
        w2aug=w2aug, att_bias=att_bias, tri=tri, msk=msk, ident_in=np.eye(128, dtype=np.float32))
    return sh, w2aug, wlr


def _core_inputs(sh, w2aug, wlr, seq, seg, nseg):
    x_own = seq[seg * T:(seg + 1) * T]
    x_halo = np.zeros((T, D), np.float32)
    if seg > 0:
        x_halo[0:1024] = seq[seg * T - 1024:seg * T]
    if seg < nseg - 1:
        x_halo[1024:2048] = seq[(seg + 1) * T:(seg + 1) * T + 1024]
    x_ctx = np.zeros((NCTX, D), np.float32)
    cw2aug = np.zeros((3, 33, 1024), np.float32)
    w_clr = np.zeros((3, 128, 32, 16), np.float32)
    slots = []
    for j in range(0, seg):
        slots.append((j, 0))
    for j in range(nseg - 1, seg, -1):
        slots.append((j, 1))
    n_f = seg if nseg > 1 else 0
    for i, (j, di) in enumerate(slots):
        blk = seq[j * T:(j + 1) * T]
        x_ctx[i * T:(i + 1) * T] = blk if di == 0 else blk[::-1]
        cw2aug[i] = w2aug[di]
        w_clr[i] = wlr[di]
    for i in range(len(slots), 3):
        cw2aug[i] = w2aug[0]
        w_clr[i] = wlr[0]
    bnd = np.zeros((128, 8), np.float32)
    bnd[:, n_f] = 1.0
    bnd[:, 4:8] = 1.0 - bnd[:, 0:4]
    kb = np.zeros((128, 69), np.float32)
    col = 0
    for g, d in enumerate(DILS):
        hw = 64 * d
        L = T // d
        nq = L // 128
        for r in range(d):
            for jk in range(nq + 1):
                u = jk * 128 + np.arange(128)
                p = r + d * u
                tglob = seg * T + (p - hw)
                ok = (tglob >= 0) & (tglob < nseg * T)
                kb[:, col] = np.where(ok, 0.0, -30000.0)
                col += 1
    d_ = dict(sh)
    d_.update(x_own=np.ascontiguousarray(x_own), x_halo=x_halo, x_ctx=x_ctx, cw2aug=cw2aug, w_clr=w_clr,
              att_kb=kb, bnd=bnd)
    return d_


_NC_CACHE = {}


def kernel(x_prompt, x_sample, norm_g, w_in, gla_w2_f, gla_b_f, gla_w2_b, gla_b_b, gla_norm_g,
           w_up_a, w_up_b, w_o, final_norm_g):
    inp = dict(w_in=w_in, w_up_a=w_up_a, w_up_b=w_up_b, w_o=w_o, norm_g=norm_g, final_norm_g=final_norm_g,
               gla_norm_g=gla_norm_g, gla_w2_f=gla_w2_f, gla_w2_b=gla_w2_b, gla_b_f=gla_b_f, gla_b_b=gla_b_b)
    sh, w2aug, wlr = _prep_shared(inp)
    xp = np.asarray(x_prompt, dtype=np.float32)
    xs = np.asarray(x_sample, dtype=np.float32)
    in_maps = []
    for c in range(4):
        in_maps.append(_core_inputs(sh, w2aug, wlr, xp[c], 0, 1))
    for c in range(4):
        in_maps.append(_core_inputs(sh, w2aug, wlr, xs[0], c, 4))
    if "nc" not in _NC_CACHE:
        _NC_CACHE["nc"] = build_program()
    nc = _NC_CACHE["nc"]
    res = run_bass_kernel_spmd(nc, in_maps, core_ids=list(range(8)))
    outs = [np.asarray(r["y_out"], dtype=np.float32) for r in res.results]
    y_prompt = np.stack(outs[0:4], axis=0)
    y_sample = np.concatenate(outs[4:8], axis=0)[None]
    return (y_prompt, y_sample)
```

```python
import numpy as np
import concourse.bass as bass
import concourse.mybir as mybir
from concourse.bass_utils import run_bass_kernel_spmd

F32 = mybir.dt.float32
BF16 = mybir.dt.bfloat16
AF = mybir.ActivationFunctionType
ALU = mybir.AluOpType

D = 4096
T = 2048
NCTX = 6144
EPS = 1e-6
NBLK = 193
DILS = (1, 4, 16)
ARENA0 = 16640
ARENA1 = 229376


def att_blk(hs, g, j):
    return (hs * 3 + g) * 3 + j


B_ATTZ = 72
B_GQ = 80
B_GK = 88
B_GV = 96
B_GZ = 112
B_LR = 128
B_MGA = 129
B_MGB = 161


def block_cols():
    cols = []
    for hs in range(8):
        for g in range(3):
            for j in range(3):
                cols.append((((g * 3 + j) * 8 + hs) * 128, 128))
    for hs in range(8):
        cols.append((9216 + hs * 128, 128))
    for i in range(8):
        cols.append((10240 + i * 128, 128))
    for i in range(8):
        cols.append((11264 + i * 128, 128))
    for i in range(16):
        cols.append((12288 + i * 128, 128))
    for i in range(16):
        cols.append((14336 + i * 128, 128))
    cols.append((16384, 32))
    for i in range(32):
        cols.append((16416 + i * 128, 128))
    for i in range(32):
        cols.append((20512 + i * 128, 128))
    assert len(cols) == NBLK
    return cols


class Sched:
    def __init__(self, nc, n_dma_slots=40, epoch=30000):
        self.nc = nc
        self.names = ["pe", "act", "dve", "pool", "sp"]
        self.prog = {k: [] for k in self.names}
        self.epoch = epoch
        self.esem = {}
        self.ecount = {}
        self.nsem = 0
        for k in self.names:
            self._new_esem(k)
        self.dslots = [[self._sem("dma%d" % i), 0] for i in range(n_dma_slots)]
        self.dnext = 0
        self.seen = {k: {} for k in self.names}
        self.lastw = {}
        self.readers = {}
        self.nops = 0

    def _sem(self, name):
        self.nsem += 1
        return self.nc.alloc_semaphore(name="s_%s_%d" % (name, self.nsem))

    def _new_esem(self, k):
        self.esem[k] = self._sem(k)
        self.ecount[k] = 0

    def _wait(self, e, tok):
        sem, val = tok
        sid = id(sem)
        if self.seen[e].get(sid, 0) >= val:
            return
        self.seen[e][sid] = val
        self.prog[e].append(("wait", sem, val))

    def _deps(self, e, reads, writes):
        for r in reads:
            t = self.lastw.get(r)
            if t is not None:
                self._wait(e, t)
        for w in writes:
            t = self.lastw.get(w)
            if t is not None:
                self._wait(e, t)
            for t in self.readers.get(w, ()):
                self._wait(e, t)

    def _commit(self, tok, reads, writes):
        for w in writes:
            self.lastw[w] = tok
            self.readers[w] = []
        for r in reads:
            if r in writes:
                continue
            self.readers.setdefault(r, []).append(tok)

    def rec_begin(self):
        self.rec = []

    def rec_end(self):
        r = self.rec
        self.rec = None
        return r

    def replay(self, items):
        for (kind, e, fn, reads, writes) in items:
            if kind == "op":
                self.op(e, fn, reads, writes)
            else:
                self.dma(e, fn, reads, writes)

    def merge(self, lists, weights=None):
        weights = weights or [1] * len(lists)
        pos = [0] * len(lists)
        alive = True
        while alive:
            alive = False
            for i, l in enumerate(lists):
                n = min(weights[i], len(l) - pos[i])
                if n > 0:
                    self.replay(l[pos[i]:pos[i] + n])
                    pos[i] += n
                    alive = True

    def op(self, e, fn, reads=(), writes=()):
        if getattr(self, "rec", None) is not None:
            self.rec.append(("op", e, fn, tuple(reads), tuple(writes)))
            return None
        self._deps(e, reads, writes)
        if self.ecount[e] >= self.epoch:
            self._new_esem(e)
        self.ecount[e] += 1
        tok = (self.esem[e], self.ecount[e])
        self.prog[e].append(("op", fn, tok[0], 1))
        self._commit(tok, reads, writes)
        self.nops += 1
        return tok

    def dma(self, e, fn, reads=(), writes=()):
        if getattr(self, "rec", None) is not None:
            self.rec.append(("dma", e, fn, tuple(reads), tuple(writes)))
            return None
        self._deps(e, reads, writes)
        slot = self.dslots[self.dnext]
        self.dnext = (self.dnext + 1) % len(self.dslots)
        if slot[1] > 0:
            self._wait(e, (slot[0], slot[1]))
        slot[1] += 16
        tok = (slot[0], slot[1])
        self.prog[e].append(("op", fn, tok[0], 16))
        self._commit(tok, reads, writes)
        self.nops += 1
        return tok

    def barrier(self):
        for e in self.names:
            for s in self.dslots:
                if s[1] > 0:
                    self._wait(e, (s[0], s[1]))
            for k in self.names:
                if k != e and self.ecount[k] > 0:
                    self._wait(e, (self.esem[k], self.ecount[k]))
        for e in self.names:
            if self.ecount[e] > 0:
                self._wait(e, (self.esem[e], self.ecount[e]))
        self.lastw = {}
        self.readers = {}

    def emit(self):
        nc = self.nc
        with nc.Block() as block:
            def mk(k):
                def body(engine):
                    for item in self.prog[k]:
                        if item[0] == "wait":
                            engine.wait_ge(item[1], item[2])
                        else:
                            ins = item[1](engine)
                            ins.then_inc(item[2], item[3])
                return body
            block.tensor(mk("pe"))
            block.scalar(mk("act"))
            block.vector(mk("dve"))
            block.gpsimd(mk("pool"))
            block.sync(mk("sp"))


class Arena:
    def __init__(self, nc):
        self.nc = nc
        self.base = ARENA0
        self.ptr = ARENA0
        self.n = 0

    def mark(self):
        return self.ptr

    def reset(self, mark):
        self.ptr = mark

    def alloc(self, shape, dtype, name="t"):
        esz = 4 if dtype == F32 else 2
        per = esz
        for s in shape[1:]:
            per *= s
        off = (self.ptr + 63) // 64 * 64
        assert off + per <= ARENA1, ("SBUF arena overflow", name, off, per)
        self.ptr = off + per
        self.n += 1
        return self.nc.alloc_sbuf_tensor_at("%s_%d" % (name, self.n), list(shape), dtype, offset=off)


def build_program(dbg=False):
    nc = bass.Bass("TRN2", target_bir_lowering=False)
    S = Sched(nc)
    A = Arena(nc)

    def din(name, shape, dt=F32):
        return nc.dram_tensor(name, list(shape), dt, kind="ExternalInput").ap()

    def dscr(name, shape, dt):
        if dbg and name in ("yaT_d", "ybT_d", "gk_d", "gqT_d", "gv_d"):
            return nc.dram_tensor(name, list(shape), dt, kind="ExternalOutput").ap()
        return nc.dram_tensor(name, list(shape), dt).ap()

    x_own = din("x_own", [T, D])
    x_halo = din("x_halo", [T, D])
    x_ctx = din("x_ctx", [NCTX, D])
    w_blk = din("w_blk", [NBLK, 128, 32, 128])
    wua = din("wua", [32, 128, 8, 128])
    wub = din("wub", [32, 128, 16, 128])
    wo = din("wo", [16, 128, 32, 256])
    ng_b = din("ng_b", [128, D])
    fng_b = din("fng_b", [128, D])
    gng_b = din("gng_b", [128, 512])
    w2aug = din("w2aug", [2, 33, 1024])
    cw2aug = din("cw2aug", [3, 33, 1024])
    w_clr = din("w_clr", [3, 128, 32, 16])
    att_bias = din("att_bias", [3, 8, 128, 256])
    att_kb = din("att_kb", [128, 69])
    bnd = din("bnd", [128, 8])
    tri = din("tri", [4, 128, 128])
    msk = din("msk", [2, 128, 128])
    ident_in = din("ident_in", [128, 128])
    y_out = nc.dram_tensor("y_out", [T, D], F32, kind="ExternalOutput").ap()

    hT_d = dscr("hT_d", [20, 128, 32, 512], BF16)
    yaT_d = dscr("yaT_d", [1024, T], BF16)
    ybT_d = dscr("ybT_d", [2048, T], BF16)
    gqT_d = dscr("gqT_d", [1024, T], F32)
    gkT_d = dscr("gkT_d", [1024, T], F32)
    lrT_d = dscr("lrT_d", [32, T], F32)
    gk_d = dscr("gk_d", [T, 1024], F32)
    gv_d = dscr("gv_d", [T, 2048], BF16)
    gzs_d = dscr("gzs_d", [T, 2048], F32)
    ck_d = dscr("ck_d", [NCTX, 1024], F32)
    cv_d = dscr("cv_d", [NCTX, 2048], BF16)
    clrT_d = dscr("clrT_d", [3, 16, T], F32)
    o_d = dscr("o_d", [2, T, 2048], F32)

    psb = [nc.alloc_psum_tensor("psb%d" % i, [128, 512], F32) for i in range(8)]
    ptb = [psb[7][:].bitcast(BF16)]
    rot = {"ps": 0, "pt": 0}

    def next_ps(lo=0, hi=7):
        i = lo + rot["ps"] % (hi - lo)
        rot["ps"] += 1
        return i

    ident = A.alloc([128, 128], BF16, "ident")
    ones_b = A.alloc([128, 128], BF16, "ones")
    bnd_t = A.alloc([128, 8], F32, "bnd")
    S.dma("pool", lambda e: e.dma_start(out=ident[:], in_=ident_in), writes=["ident"])
    S.dma("sp", lambda e: e.dma_start(out=bnd_t[:], in_=bnd), writes=["bnd"])
    S.op("dve", lambda e: e.memset(ones_b[:], 1.0), writes=["ones"])
    base_mark = A.mark()

    gb = A.alloc([128, D], F32, "gb")
    S.dma("sp", lambda e: e.dma_start(out=gb[:], in_=ng_b), writes=["gb"])
    xt = [A.alloc([128, D], F32, "xt") for _ in range(2)]
    hb = [A.alloc([128, D], BF16, "hb") for _ in range(2)]
    junk = A.alloc([128, D], BF16, "junk")
    hTt = [A.alloc([128, 32, 512], BF16, "hTt") for _ in range(2)]
    stat = [A.alloc([128, 4], F32, "stat") for _ in range(2)]
    srcs = [(x_own, 4), (x_halo, 4), (x_ctx, 12)]
    tile_id = 0
    it = 0
    for src, nt in srcs:
        for t5 in range(nt):
            hp = tile_id % 2
            for sub in range(4):
                b = it % 2
                r0 = t5 * 512 + sub * 128
                S.dma("sp", lambda e, b=b, r0=r0, src=src: e.dma_start(out=xt[b][:], in_=src[r0:r0 + 128, :]),
                      writes=[("xt", b)])
                S.op("dve", lambda e, b=b: e.memset(stat[b][:, 0:1], 0.0), writes=[("st0", b)])
                S.op("act", lambda e, b=b: e.activation(out=junk[:], in_=xt[b][:], func=AF.Square,
                                                          accum_out=stat[b][:, 0:1]),
                     reads=[("xt", b)], writes=["junk", ("st0", b)])
                S.op("dve", lambda e, b=b: e.tensor_scalar(out=stat[b][:, 1:2], in0=stat[b][:, 0:1],
                                                            scalar1=1.0 / D, scalar2=EPS, op0=ALU.mult, op1=ALU.add),
                     reads=[("st0", b)], writes=[("st1", b)])
                S.op("act", lambda e, b=b: e.activation(out=stat[b][:, 2:3], in_=stat[b][:, 1:2], func=AF.Sqrt),
                     reads=[("st1", b)], writes=[("st2", b)])
                S.op("dve", lambda e, b=b: e.reciprocal(out=stat[b][:, 3:4], in_=stat[b][:, 2:3]),
                     reads=[("st2", b)], writes=[("st3", b)])
                S.op("dve", lambda e, b=b: e.scalar_tensor_tensor(out=hb[b][:], in0=xt[b][:], scalar=stat[b][:, 3:4],
                                                                   in1=gb[:], op0=ALU.mult, op1=ALU.mult),
                     reads=[("xt", b), ("st3", b), "gb"], writes=[("hb", b)])
                for q4 in range(4):
                    pb = 0
                    rot["pt"] += 1

                    def tr(e, b=b, q4=q4, pb=pb):
                        ins = None
                        for k in range(8):
                            kc = q4 * 8 + k
                            ins = e.transpose(out=ptb[pb][:, k * 128:(k + 1) * 128],
                                              in_=hb[b][:, kc * 128:(kc + 1) * 128], identity=ident[:])
                        return ins
                    S.op("pe", tr, reads=[("hb", b), "ident"], writes=[("pt", pb)])
                    eng = "act" if q4 % 2 == 0 else "dve"

                    def ev(e, hp=hp, q4=q4, pb=pb, sub=sub, eng=eng):
                        o = hTt[hp][:, q4 * 8:(q4 + 1) * 8, sub * 128:(sub + 1) * 128]
                        i = ptb[pb][:].rearrange("p (k t) -> p k t", k=8)
                        if eng == "act":
                            return e.copy(out=o, in_=i)
                        return e.tensor_copy(out=o, in_=i)
                    S.op(eng, ev, reads=[("pt", pb)], writes=[("hTt", hp, sub, q4)])
                it += 1
            S.dma("sp", lambda e, hp=hp, tile_id=tile_id: e.dma_start(out=hT_d[tile_id], in_=hTt[hp][:]),
                  reads=[("hTt", hp, s_, q_) for s_ in range(4) for q_ in range(4)], writes=[("hT_d", tile_id)])
            tile_id += 1
    S.barrier()
    A.reset(base_mark)

    def load_h(buf, key, pieces):
        c = 0
        for (tl, c0, n) in pieces:
            S.dma("sp", lambda e, tl=tl, c0=c0, n=n, c=c: e.dma_start(out=buf[:, :, c:c + n],
                                                                      in_=hT_d[tl, :, :, c0:c0 + n]),
                  reads=[("hT_d", tl)], writes=[key])
            c += n
        return c

    def mm_fm(ps_ap, wbuf, hbuf, n, nk=32, wsl=None):
        def f(e):
            ins = None
            for kc in range(nk):
                lw = wbuf[:, kc, :] if wsl is None else wbuf[:, kc, wsl[0]:wsl[1]]
                ins = e.matmul(ps_ap, lhsT=lw, rhs=hbuf[:, kc, 0:n], start=(kc == 0), stop=(kc == nk - 1))
            return ins
        return f

    def mm_tm(ps_ap, hbuf, t0, wbuf, ncols, nk=32):
        def f(e):
            ins = None
            for kc in range(nk):
                ins = e.matmul(ps_ap, lhsT=hbuf[:, kc, t0:t0 + 128], rhs=wbuf[:, kc, 0:ncols],
                               start=(kc == 0), stop=(kc == nk - 1))
            return ins
        return f

    hbuf = [A.alloc([128, 32, 512], BF16, "hbuf") for _ in range(2)]
    wbuf = [[A.alloc([128, 32, 128], BF16, "wbuf") for _ in range(3)] for _ in range(2)]
    wz = A.alloc([128, 32, 128], BF16, "wz")
    qT2 = [A.alloc([128, T], BF16, "qT") for _ in range(2)]
    kT2 = [A.alloc([128, 4096], BF16, "kT") for _ in range(2)]
    vT2 = [A.alloc([128, 4096], BF16, "vT") for _ in range(2)]
    zs2 = [A.alloc([128, T], F32, "zs") for _ in range(2)]
    accO = A.alloc([128, T], F32, "accO")
    accR = A.alloc([128, T], F32, "accR")
    tbias = [A.alloc([128, 256], F32, "tbias") for _ in range(2)]
    kb_t = A.alloc([128, 69], F32, "kb")
    sbuf_s = [A.alloc([128, 256], F32, "sb") for _ in range(2)]
    pT = [A.alloc([128, 256], BF16, "pT") for _ in range(2)]
    vt = [A.alloc([128, 128], BF16, "vt") for _ in range(2)]
    rr = [A.alloc([128, 512], F32, "rr") for _ in range(2)]
    yst = [A.alloc([128, 512], BF16, "yst") for _ in range(2)]
    S.dma("sp", lambda e: e.dma_start(out=kb_t[:], in_=att_kb), writes=["kb"])
    SCALE = 128.0 ** -0.5
    kbcol0 = [0, 17, 37]
    gst = {"h": 0, "ps": 0}

    def g_ps():
        i = gst["ps"] % 2
        gst["ps"] += 1
        return i

    def gemm_group(hs, g, par):
        d = DILS[g]
        hw = 64 * d
        qT, kT, vT = qT2[par], kT2[par], vT2[par]
        keys = []
        if g == 0:
            zp = hs % 2
            S.dma("pool", lambda e: e.dma_start(out=wz[:], in_=w_blk[B_ATTZ + hs]), writes=["wz"])
            for i in range(4):
                hb_i = gst["h"] % 2
                gst["h"] += 1
                load_h(hbuf[hb_i], ("h", hb_i), [(i, 0, 512)])
                pi = g_ps()
                S.op("pe", mm_fm(psb[pi][:, 0:512], wz, hbuf[hb_i], 512), reads=["wz", ("h", hb_i)],
                     writes=[("ps", pi)])
                S.op("act", lambda e, pi=pi, i=i, zp=zp: e.activation(out=zs2[zp][:, i * 512:(i + 1) * 512],
                                                                      in_=psb[pi][:, 0:512], func=AF.Silu),
                     reads=[("ps", pi)], writes=[("zs", zp, i)])
        ws = par
        for j in range(3):
            S.dma("pool", lambda e, j=j: e.dma_start(out=wbuf[ws][j][:], in_=w_blk[att_blk(hs, g, j)]),
                  writes=[("w", ws, j)])
        jobs = []
        if hw == 1024:
            jobs.append(([(4, 0, 512)], 0, 512, False, 0))
            jobs.append(([(5, 0, 512)], 512, 512, False, 0))
        else:
            jobs.append(([(5, 512 - hw, hw)], 0, hw, False, 0))
        for i in range(4):
            jobs.append(([(i, 0, 512)], hw + 512 * i, 512, True, 512 * i))
        if hw == 1024:
            jobs.append(([(6, 0, 512)], hw + T, 512, False, 0))
            jobs.append(([(7, 0, 512)], hw + T + 512, 512, False, 0))
        else:
            jobs.append(([(6, 0, hw)], hw + T, hw, False, 0))
        for (pieces, edst, n, has_q, t0) in jobs:
            hb_i = gst["h"] % 2
            gst["h"] += 1
            load_h(hbuf[hb_i], ("h", hb_i), pieces)
            for j in ([0, 1, 2] if has_q else [1, 2]):
                pi = g_ps()
                S.op("pe", mm_fm(psb[pi][:, 0:n], wbuf[ws][j], hbuf[hb_i], n),
                     reads=[("w", ws, j), ("h", hb_i)], writes=[("ps", pi)])
                if j == 0:
                    dst, dk_ = qT[:, t0:t0 + n], ("qT", par, t0)
                elif j == 1:
                    dst, dk_ = kT[:, edst:edst + n], ("kT", par, edst)
                else:
                    dst, dk_ = vT[:, edst:edst + n], ("vT", par, edst)
                keys.append(dk_)
                if j != 1:
                    S.op("act", lambda e, dst=dst, pi=pi, n=n: e.copy(out=dst, in_=psb[pi][:, 0:n]),
                         reads=[("ps", pi)], writes=[dk_])
                else:
                    S.op("dve", lambda e, dst=dst, pi=pi, n=n: e.tensor_copy(out=dst, in_=psb[pi][:, 0:n]),
                         reads=[("ps", pi)], writes=[dk_])
        return keys

    def attn_group(hs, g, par, qkv_keys):
        d = DILS[g]
        hw = 64 * d
        NT = T + 2 * hw
        qT, kT, vT = qT2[par], kT2[par], vT2[par]
        if g == 0:
            S.op("pool", lambda e: e.memset(accO[:], 0.0), writes=["accO"])
            S.op("pool", lambda e: e.memset(accR[:], 0.0), writes=["accR"])
        tb = tbias[par]
        tbk = ("tbias", par)
        S.dma("sp", lambda e: e.dma_start(out=tb[:], in_=att_bias[g, hs]), writes=[tbk])
        L = T // d
        nq = L // 128
        kTv = kT[:, 0:NT].rearrange("p (u r) -> p u r", r=d)
        vTv = vT[:, 0:NT].rearrange("p (u r) -> p u r", r=d)
        qTv = qT[:].rearrange("p (u r) -> p u r", r=d)
        aOv = accO[:].rearrange("p (u r) -> p u r", r=d)
        aRv = accR[:].rearrange("p (u r) -> p u r", r=d)
        ctr = 0
        for r in range(d):
            for jk in range(nq + 1):
                b2 = ctr % 2
                ctr += 1
                qlo = max(jk - 1, 0)
                qhi = min(jk + 1, nq)
                nqq = (qhi - qlo) * 128
                c0 = 0 if jk >= 1 else 128
                S.op("pe", lambda e, jk=jk, r=r, qlo=qlo, nqq=nqq: e.matmul(
                    psb[2][:, 0:nqq], lhsT=kTv[:, jk * 128:(jk + 1) * 128, r],
                    rhs=qTv[:, qlo * 128:qlo * 128 + nqq, r], start=True, stop=True),
                    reads=qkv_keys, writes=[("ps", 2)])
                S.op("dve", lambda e, b2=b2, nqq=nqq, c0=c0: e.scalar_tensor_tensor(
                    out=sbuf_s[b2][:, 0:nqq], in0=psb[2][:, 0:nqq], scalar=SCALE, in1=tb[:, c0:c0 + nqq],
                    op0=ALU.mult, op1=ALU.add), reads=[("ps", 2), tbk], writes=[("sb", b2)])
                kc_ = kbcol0[g] + r * (nq + 1) + jk
                S.op("act", lambda e, b2=b2, nqq=nqq, kc_=kc_: e.activation(
                    out=pT[b2][:, 0:nqq], in_=sbuf_s[b2][:, 0:nqq], func=AF.Exp, bias=kb_t[:, kc_:kc_ + 1],
                    scale=1.0), reads=[("sb", b2), "kb"], writes=[("pT", b2)])
                S.op("pe", lambda e, jk=jk, r=r: e.transpose(
                    out=ptb[0][:, 0:128], in_=vTv[:, jk * 128:(jk + 1) * 128, r], identity=ident[:]),
                    reads=qkv_keys + ["ident"], writes=[("pt", 0)])
                S.op("act", lambda e, b2=b2: e.copy(out=vt[b2][:], in_=ptb[0][:, 0:128]),
                     reads=[("pt", 0)], writes=[("vt", b2)])
                for qi in range(qlo, qhi):
                    col = (qi - qlo) * 128
                    first = (qi == jk)
                    last = (qi == jk - 1)
                    ob = 3 + qi % 2
                    rb = 5 + qi % 2

                    def pv(e, ob=ob, rb=rb, b2=b2, col=col, first=first, last=last):
                        e.matmul(psb[ob][:, 0:128], lhsT=vt[b2][:], rhs=pT[b2][:, col:col + 128],
                                 start=first, stop=last)
                        return e.matmul(psb[rb][:, 0:128], lhsT=ones_b[:], rhs=pT[b2][:, col:col + 128],
                                        start=first, stop=last)
                    S.op("pe", pv, reads=[("vt", b2), ("pT", b2), "ones"], writes=[("ps", ob), ("ps", rb)])
                    if last:
                        S.op("dve", lambda e, ob=ob, qi=qi, r=r: e.tensor_tensor(
                            out=aOv[:, qi * 128:(qi + 1) * 128, r], in0=aOv[:, qi * 128:(qi + 1) * 128, r],
                            in1=psb[ob][:, 0:128], op=ALU.add), reads=[("ps", ob), "accO"], writes=["accO"])
                        S.op("dve", lambda e, rb=rb, qi=qi, r=r: e.tensor_tensor(
                            out=aRv[:, qi * 128:(qi + 1) * 128, r], in0=aRv[:, qi * 128:(qi + 1) * 128, r],
                            in1=psb[rb][:, 0:128], op=ALU.add), reads=[("ps", rb), "accR"], writes=["accR"])
        if g == 2:
            zp = hs % 2
            for i in range(4):
                b2 = i % 2
                S.op("dve", lambda e, b2=b2, i=i: e.reciprocal(out=rr[b2][:], in_=accR[:, i * 512:(i + 1) * 512]),
                     reads=["accR"], writes=[("rr", b2)])
                S.op("dve", lambda e, b2=b2, i=i: e.tensor_tensor(out=rr[b2][:], in0=rr[b2][:],
                                                                 in1=accO[:, i * 512:(i + 1) * 512], op=ALU.mult),
                     reads=["accO", ("rr", b2)], writes=[("rr", b2)])
                S.op("dve", lambda e, b2=b2, i=i, zp=zp: e.tensor_tensor(
                    out=yst[b2][:], in0=rr[b2][:], in1=zs2[zp][:, i * 512:(i + 1) * 512], op=ALU.mult),
                    reads=[("rr", b2), ("zs", zp, i)], writes=[("yst", b2)])
                S.dma("sp", lambda e, b2=b2, i=i: e.dma_start(
                    out=yaT_d[hs * 128:(hs + 1) * 128, i * 512:(i + 1) * 512], in_=yst[b2][:]),
                    reads=[("yst", b2)], writes=[("yaT_d", hs, i)])

    def merge_prop(a, b):
        la, lb = len(a), len(b)
        ia = ib = 0
        while ia < la or ib < lb:
            if ib >= lb or (ia < la and ia * lb <= ib * la):
                S.replay(a[ia:ia + 1])
                ia += 1
            else:
                S.replay(b[ib:ib + 1])
                ib += 1

    groups = [(hs, g) for hs in range(8) for g in range(3)]
    g_lists, a_lists, g_keys = [], [], []
    for gi, (hs, g) in enumerate(groups):
        S.rec_begin()
        g_keys.append(gemm_group(hs, g, gi % 2))
        g_lists.append(S.rec_end())
    for gi, (hs, g) in enumerate(groups):
        S.rec_begin()
        attn_group(hs, g, gi % 2, g_keys[gi])
        a_lists.append(S.rec_end())
    for step in range(len(groups) + 1):
        ga = g_lists[step] if step < len(groups) else []
        aa = a_lists[step - 1] if step >= 1 else []
        merge_prop(ga, aa)
    S.barrier()
    A.reset(base_mark)

    hbuf = [A.alloc([128, 32, 512], BF16, "hbuf") for _ in range(2)]
    wfm = [A.alloc([128, 32, 128], BF16, "wfm") for _ in range(4)]
    wtm = [A.alloc([128, 32, 512], BF16, "wtm") for _ in range(2)]
    stg = [A.alloc([128, 512], F32, "stg") for _ in range(4)]
    stgb = [A.alloc([128, 512], BF16, "stgb") for _ in range(4)]
    wclr = A.alloc([128, 32, 16], BF16, "wclr")
    sctr = 0
    hctr = 0
    fm_list = [(B_GQ + i, gqT_d, i * 128, 128, 256.0 ** -0.5) for i in range(8)]
    fm_list += [(B_GK + i, gkT_d, i * 128, 128, 1.0) for i in range(8)]
    fm_list += [(B_LR, lrT_d, 0, 32, 1.0)]
    for c4 in range(0, len(fm_list), 4):
        grp = fm_list[c4:c4 + 4]
        for wi, (blk, dst, r0, m, sc) in enumerate(grp):
            S.dma("pool", lambda e, wi=wi, blk=blk: e.dma_start(out=wfm[wi][:], in_=w_blk[blk]), writes=[("wfm", wi)])
        for i in range(4):
            hb_i = hctr % 2
            hctr += 1
            load_h(hbuf[hb_i], ("h", hb_i), [(i, 0, 512)])
            for wi, (blk, dst, r0, m, sc) in enumerate(grp):
                pi = next_ps()
                S.op("pe", mm_fm(psb[pi][0:m, 0:512], wfm[wi], hbuf[hb_i], 512, wsl=(0, m)),
                     reads=[("wfm", wi), ("h", hb_i)], writes=[("ps", pi)])
                sb_i = sctr % 4
                sctr += 1
                S.op("act", lambda e, sb_i=sb_i, pi=pi, m=m, sc=sc: e.activation(
                    out=stg[sb_i][0:m, :], in_=psb[pi][0:m, 0:512], func=AF.Copy, scale=sc),
                    reads=[("ps", pi)], writes=[("stg", sb_i)])
                S.dma("sp", lambda e, sb_i=sb_i, dst=dst, r0=r0, m=m, i=i: e.dma_start(
                    out=dst[r0:r0 + m, i * 512:(i + 1) * 512], in_=stg[sb_i][0:m, :]),
                    reads=[("stg", sb_i)], writes=[("gfm", id(dst), r0, i)])
    tm_list = []
    for cg in range(2):
        tm_list.append((B_GK + 4 * cg, "k", cg))
    for cg in range(4):
        tm_list.append((B_GV + 4 * cg, "v", cg))
    for cg in range(4):
        tm_list.append((B_GZ + 4 * cg, "z", cg))
    wctr = 0
    for (blk0, kind, cg) in tm_list:
        wi = wctr % 2
        wctr += 1
        for b4 in range(4):
            S.dma("pool", lambda e, wi=wi, blk0=blk0, b4=b4: e.dma_start(
                out=wtm[wi][:, :, b4 * 128:(b4 + 1) * 128], in_=w_blk[blk0 + b4]), writes=[("wtm", wi)])
        tiles = list(range(4)) + ([] if kind == "z" else list(range(8, 20)))
        for tl in tiles:
            hb_i = hctr % 2
            hctr += 1
            load_h(hbuf[hb_i], ("h", hb_i), [(tl, 0, 512)])
            for sub in range(4):
                pi = next_ps()
                S.op("pe", mm_tm(psb[pi][:, 0:512], hbuf[hb_i], sub * 128, wtm[wi], 512),
                     reads=[("wtm", wi), ("h", hb_i)], writes=[("ps", pi)])
                sb_i = sctr % 4
                sctr += 1
                if tl < 4:
                    row0 = tl * 512 + sub * 128
                    dk, dv_, dz = gk_d, gv_d, gzs_d
                else:
                    row0 = (tl - 8) * 512 + sub * 128
                    dk, dv_, dz = ck_d, cv_d, None
                if kind == "k":
                    S.op("act", lambda e, sb_i=sb_i, pi=pi: e.copy(out=stg[sb_i][:], in_=psb[pi][:, 0:512]),
                         reads=[("ps", pi)], writes=[("stg", sb_i)])
                    S.dma("sp", lambda e, sb_i=sb_i, dk=dk, row0=row0, cg=cg: e.dma_start(
                        out=dk[row0:row0 + 128, cg * 512:(cg + 1) * 512], in_=stg[sb_i][:]),
                        reads=[("stg", sb_i)], writes=[("gtm", kind, cg, tl, sub)])
                elif kind == "v":
                    S.op("dve", lambda e, sb_i=sb_i, pi=pi: e.tensor_copy(out=stgb[sb_i][:], in_=psb[pi][:, 0:512]),
                         reads=[("ps", pi)], writes=[("stgb", sb_i)])
                    S.dma("sp", lambda e, sb_i=sb_i, dv_=dv_, row0=row0, cg=cg: e.dma_start(
                        out=dv_[row0:row0 + 128, cg * 512:(cg + 1) * 512], in_=stgb[sb_i][:]),
                        reads=[("stgb", sb_i)], writes=[("gtm", kind, cg, tl, sub)])
                else:
                    S.op("act", lambda e, sb_i=sb_i, pi=pi: e.activation(out=stg[sb_i][:], in_=psb[pi][:, 0:512],
                                                                         func=AF.Silu),
                         reads=[("ps", pi)], writes=[("stg", sb_i)])
                    S.dma("sp", lambda e, sb_i=sb_i, dz=dz, row0=row0, cg=cg: e.dma_start(
                        out=dz[row0:row0 + 128, cg * 512:(cg + 1) * 512], in_=stg[sb_i][:]),
                        reads=[("stg", sb_i)], writes=[("gtm", kind, cg, tl, sub)])
    for sl in range(3):
        S.dma("pool", lambda e, sl=sl: e.dma_start(out=wclr[:], in_=w_clr[sl]), writes=["wclr"])
        for i in range(4):
            hb_i = hctr % 2
            hctr += 1
            load_h(hbuf[hb_i], ("h", hb_i), [(8 + sl * 4 + i, 0, 512)])
            pi = next_ps()
            S.op("pe", mm_fm(psb[pi][0:16, 0:512], wclr, hbuf[hb_i], 512),
                 reads=["wclr", ("h", hb_i)], writes=[("ps", pi)])
            sb_i = sctr % 4
            sctr += 1
            S.op("act", lambda e, sb_i=sb_i, pi=pi: e.copy(out=stg[sb_i][0:16, :], in_=psb[pi][0:16, 0:512]),
                 reads=[("ps", pi)], writes=[("stg", sb_i)])
            S.dma("sp", lambda e, sb_i=sb_i, sl=sl, i=i: e.dma_start(
                out=clrT_d[sl, :, i * 512:(i + 1) * 512], in_=stg[sb_i][0:16, :]),
                reads=[("stg", sb_i)], writes=[("clr", sl, i)])
    S.barrier()
    A.reset(base_mark)

    tri_t = A.alloc([128, 4, 128], F32, "tri")
    msk_t = A.alloc([128, 2, 128], F32, "msk")
    for i4 in range(4):
        S.dma("sp", lambda e, i4=i4: e.dma_start(out=tri_t[:, i4, :], in_=tri[i4]), writes=["tri"])
    for i2 in range(2):
        S.dma("sp", lambda e, i2=i2: e.dma_start(out=msk_t[:, i2, :], in_=msk[i2]), writes=["msk"])

    def gla_thread(h):
        X = psb[2 * h]
        Y = psb[2 * h + 1]
        kx = ("ps", 2 * h)
        ky = ("ps", 2 * h + 1)
        K_ = lambda name, *a: (name, h) + a
        lra = [A.alloc([64, 128], F32, "lra") for _ in range(3)]
        w2a = A.alloc([64, 256], F32, "w2a")
        SA = A.alloc([128, 2, 512], F32, "SA")
        SB = A.alloc([128, 2, 512], F32, "SB")
        Sbf = A.alloc([128, 2, 512], BF16, "Sbf")
        e_t = [A.alloc([128, 256], F32, "e_t") for _ in range(2)]
        sp_t = [A.alloc([128, 256], F32, "sp_t") for _ in range(2)]
        eb_t = [A.alloc([128, 2, 128], F32, "eb_t") for _ in range(2)]
        enb_t = [A.alloc([128, 2, 128], F32, "enb_t") for _ in range(2)]
        ekd_t = [A.alloc([128, 256], F32, "ekd_t") for _ in range(2)]
        kd_t = [A.alloc([128, 256], BF16, "kd_t") for _ in range(2)]
        qe_t = [A.alloc([128, 2, 128], BF16, "qe_t") for _ in range(2)]
        ke_t = [A.alloc([128, 2, 128], BF16, "ke_t") for _ in range(2)]
        AT_t = [A.alloc([128, 128], BF16, "AT_t") for _ in range(2)]
        ksc = [A.alloc([128, 256], F32, "ksc") for _ in range(3)]
        vsc = [A.alloc([128, 512], BF16, "vsc") for _ in range(3)]
        qsc = [A.alloc([128, 2, 128], F32, "qsc") for _ in range(3)]
        ktsc = [A.alloc([128, 2, 128], F32, "ktsc") for _ in range(3)]
        o_s = [A.alloc([128, 512], F32, "o_s") for _ in range(3)]
        cc = {"n": 0, "sc": 0, "o": 0}

        for b in range(3):
            S.op("dve", lambda e, b=b: e.memset(lra[b][:], 0.0), writes=[K_("lra", b)])
            S.op("dve", lambda e, b=b: e.memset(lra[b][32:33, :], 1.0), writes=[K_("lra", b)])
        S.op("dve", lambda e: e.memset(w2a[:], 0.0), writes=[K_("w2a")])

        def set_w2(w2src):
            S.dma("sp", lambda e: e.dma_start(out=w2a[0:33, :], in_=w2src[:, h * 256:(h + 1) * 256]),
                  writes=[K_("w2a")])

        def load_sc(lr_rows, kd_src, vd_src, tok0, sl):
            S.dma("sp", lambda e: e.dma_start(out=lra[sl][0:16, :], in_=lr_rows), writes=[K_("lra", sl)])
            S.dma("sp", lambda e: e.dma_start(out=ksc[sl][:], in_=kd_src[tok0:tok0 + 128, h * 256:(h + 1) * 256]),
                  writes=[K_("ksc", sl)])
            S.dma("sp", lambda e: e.dma_start(out=vsc[sl][:], in_=vd_src[tok0:tok0 + 128, h * 512:(h + 1) * 512]),
                  writes=[K_("vsc", sl)])

        def decays(sl, di, need_fm, b):
            S.op("pe", lambda e: e.matmul(X[:, 0:256], lhsT=lra[sl][0:33, :], rhs=w2a[0:33, :],
                                          start=True, stop=True), reads=[K_("lra", sl), K_("w2a")], writes=[kx])
            S.op("act", lambda e: e.activation(out=e_t[b][:], in_=X[:, 0:256], func=AF.Exp, scale=-1.0),
                 reads=[kx], writes=[K_("e_t", b)])
            S.op("act", lambda e: e.activation(out=sp_t[b][:], in_=e_t[b][:], func=AF.Ln, bias=1.0, scale=1.0),
                 reads=[K_("e_t", b)], writes=[K_("sp_t", b)])

            def bfm(e):
                ins = None
                for kk in range(2):
                    ins = e.matmul(X[:, kk * 128:(kk + 1) * 128], lhsT=sp_t[b][:, kk * 128:(kk + 1) * 128],
                                   rhs=tri_t[:, 2 * di, :], start=True, stop=True)
                return ins
            S.op("pe", bfm, reads=[K_("sp_t", b), "tri"], writes=[kx])
            S.op("act", lambda e: e.activation(out=eb_t[b][:], in_=X[:, 0:256].rearrange("p (k t) -> p k t", k=2),
                                               func=AF.Exp), reads=[kx], writes=[K_("eb_t", b)])
            if need_fm:
                S.op("act", lambda e: e.activation(out=enb_t[b][:], in_=X[:, 0:256].rearrange("p (k t) -> p k t", k=2),
                                                   func=AF.Exp, scale=-1.0), reads=[kx], writes=[K_("enb_t", b)])
            S.op("pe", lambda e: e.matmul(X[:, 256:512], lhsT=tri_t[:, 2 * di + 1, :], rhs=sp_t[b][:],
                                          start=True, stop=True), reads=[K_("sp_t", b), "tri"], writes=[kx])
            S.op("act", lambda e: e.activation(out=ekd_t[b][:], in_=X[:, 256:512], func=AF.Exp),
                 reads=[kx], writes=[K_("ekd_t", b)])
            S.op("dve", lambda e: e.tensor_tensor(out=kd_t[b][:], in0=ksc[sl][:], in1=ekd_t[b][:], op=ALU.mult),
                 reads=[K_("ksc", sl), K_("ekd_t", b)], writes=[K_("kd_t", b)])

        def state_update(St, skey, sl, di, b, cast):
            last = 127 if di == 0 else 0
            for kk in range(2):
                S.op("pe", lambda e, kk=kk: e.matmul(Y[:, 0:512], lhsT=kd_t[b][:, kk * 128:(kk + 1) * 128],
                                                      rhs=vsc[sl][:], start=True, stop=True),
                     reads=[K_("kd_t", b), K_("vsc", sl)], writes=[ky])
                S.op("dve", lambda e, kk=kk: e.scalar_tensor_tensor(
                    out=St[:, kk, :], in0=St[:, kk, :], scalar=eb_t[b][:, kk, last:last + 1], in1=Y[:, 0:512],
                    op0=ALU.mult, op1=ALU.add), reads=[ky, K_("eb_t", b), K_(skey, kk)], writes=[K_(skey, kk)])
            if cast:
                S.op("act", lambda e: e.copy(out=Sbf[:], in_=St[:]), reads=[K_(skey, 0), K_(skey, 1)],
                     writes=[K_("Sbf")])

        S.op("dve", lambda e: e.memset(SB[:], 0.0), writes=[K_("SB", 0), K_("SB", 1)])
        S.op("dve", lambda e: e.memset(SA[:], 0.0), writes=[K_("SA", 0), K_("SA", 1)])

        def boundary(i):
            for kk in range(2):
                S.op("dve", lambda e, kk=kk: e.scalar_tensor_tensor(
                    out=SA[:, kk, :], in0=SB[:, kk, :], scalar=bnd_t[:, i:i + 1], in1=SA[:, kk, :],
                    op0=ALU.mult, op1=ALU.add), reads=[K_("SB", kk), K_("SA", kk), "bnd"], writes=[K_("SA", kk)])
                S.op("dve", lambda e, kk=kk: e.tensor_scalar(
                    out=SB[:, kk, :], in0=SB[:, kk, :], scalar1=bnd_t[:, 4 + i:5 + i], scalar2=None, op0=ALU.mult),
                    reads=[K_("SB", kk), "bnd"], writes=[K_("SB", kk)])
        for sl_ in range(3):
            boundary(sl_)
            set_w2(cw2aug[sl_])
            for s1 in range(16):
                sl = cc["sc"] % 3
                cc["sc"] += 1
                t0 = s1 * 128
                load_sc(clrT_d[sl_, :, t0:t0 + 128], ck_d, cv_d, sl_ * T + t0, sl)
                b = cc["n"] % 2
                cc["n"] += 1
                decays(sl, 0, False, b)
                state_update(SB, "SB", sl, 0, b, False)
        boundary(3)
        for di in range(2):
            St = SA if di == 0 else SB
            skey = "SA" if di == 0 else "SB"
            S.op("act", lambda e, St=St: e.copy(out=Sbf[:], in_=St[:]), reads=[K_(skey, 0), K_(skey, 1)],
                 writes=[K_("Sbf")])
            set_w2(w2aug[di])
            for s1 in (range(16) if di == 0 else range(15, -1, -1)):
                sl = cc["sc"] % 3
                cc["sc"] += 1
                tok0 = s1 * 128
                load_sc(lrT_d[16 * di:16 * di + 16, tok0:tok0 + 128], gk_d, gv_d, tok0, sl)
                S.dma("sp", lambda e, sl=sl, tok0=tok0: e.dma_start(
                    out=qsc[sl][:], in_=gqT_d[h * 256:(h + 1) * 256, tok0:tok0 + 128].rearrange("(k p) t -> p k t", p=128)),
                    writes=[K_("qsc", sl)])
                S.dma("sp", lambda e, sl=sl, tok0=tok0: e.dma_start(
                    out=ktsc[sl][:], in_=gkT_d[h * 256:(h + 1) * 256, tok0:tok0 + 128].rearrange("(k p) t -> p k t", p=128)),
                    writes=[K_("ktsc", sl)])
                b = cc["n"] % 2
                cc["n"] += 1
                decays(sl, di, True, b)
                S.op("dve", lambda e, b=b, sl=sl: e.tensor_tensor(out=qe_t[b][:], in0=qsc[sl][:], in1=eb_t[b][:],
                                                                  op=ALU.mult),
                     reads=[K_("qsc", sl), K_("eb_t", b)], writes=[K_("qe_t", b)])
                S.op("dve", lambda e, b=b, sl=sl: e.tensor_tensor(out=ke_t[b][:], in0=ktsc[sl][:], in1=enb_t[b][:],
                                                                  op=ALU.mult),
                     reads=[K_("ktsc", sl), K_("enb_t", b)], writes=[K_("ke_t", b)])

                def amm(e, b=b):
                    ins = None
                    for kk in range(2):
                        ins = e.matmul(X[:, 0:128], lhsT=ke_t[b][:, kk, :], rhs=qe_t[b][:, kk, :],
                                       start=(kk == 0), stop=(kk == 1))
                    return ins
                S.op("pe", amm, reads=[K_("qe_t", b), K_("ke_t", b)], writes=[kx])
                S.op("dve", lambda e, b=b, di=di: e.tensor_tensor(
                    out=AT_t[b][:], in0=X[:, 0:128], in1=msk_t[:, di, :], op=ALU.mult),
                    reads=[kx, "msk"], writes=[K_("AT_t", b)])

                def omm(e, b=b, sl=sl):
                    e.matmul(Y[:, 0:512], lhsT=AT_t[b][:], rhs=vsc[sl][:], start=True, stop=False)
                    e.matmul(Y[:, 0:512], lhsT=qe_t[b][:, 0, :], rhs=Sbf[:, 0, :], start=False, stop=False)
                    return e.matmul(Y[:, 0:512], lhsT=qe_t[b][:, 1, :], rhs=Sbf[:, 1, :], start=False, stop=True)
                S.op("pe", omm, reads=[K_("AT_t", b), K_("vsc", sl), K_("qe_t", b), K_("Sbf")], writes=[ky])
                ob_ = cc["o"] % 3
                cc["o"] += 1
                S.op("act", lambda e, ob_=ob_: e.copy(out=o_s[ob_][:], in_=Y[:, 0:512]), reads=[ky],
                     writes=[K_("o_s", ob_)])
                S.dma("pool", lambda e, ob_=ob_, tok0=tok0, di=di: e.dma_start(
                    out=o_d[di, tok0:tok0 + 128, h * 512:(h + 1) * 512], in_=o_s[ob_][:]),
                    reads=[K_("o_s", ob_)], writes=[("o_d", di, h, tok0)])
                state_update(St, skey, sl, di, b, True)

    threads = []
    for h in range(4):
        S.rec_begin()
        gla_thread(h)
        threads.append(S.rec_end())
    S.merge(threads)
    S.barrier()
    A.reset(base_mark)
    gn_t = A.alloc([128, 512], F32, "gn")
    S.dma("sp", lambda e: e.dma_start(out=gn_t[:], in_=gng_b), writes=["gn"])
    of_t = [A.alloc([128, 2048], F32, "of_t") for _ in range(2)]
    ob_t = [A.alloc([128, 2048], F32, "ob_t") for _ in range(2)]
    gz_t = [A.alloc([128, 2048], F32, "gz_t") for _ in range(2)]
    yb_t = [A.alloc([128, 2048], BF16, "yb_t") for _ in range(2)]
    junk2 = A.alloc([128, 512], BF16, "junk2")
    st2 = [A.alloc([128, 16], F32, "st2") for _ in range(2)]
    ybT_s = [A.alloc([128, 16, 512], BF16, "ybT_s") for _ in range(2)]
    for gi in range(4):
        ys = gi % 2
        for sub in range(4):
            b = (gi * 4 + sub) % 2
            r0 = gi * 512 + sub * 128
            S.dma("sp", lambda e, b=b, r0=r0: e.dma_start(out=of_t[b][:], in_=o_d[0, r0:r0 + 128, :]), writes=[("of", b)])
            S.dma("sp", lambda e, b=b, r0=r0: e.dma_start(out=ob_t[b][:], in_=o_d[1, r0:r0 + 128, :]), writes=[("ob", b)])
            S.dma("sp", lambda e, b=b, r0=r0: e.dma_start(out=gz_t[b][:], in_=gzs_d[r0:r0 + 128, :]), writes=[("gz", b)])
            S.op("pool", lambda e, b=b: e.tensor_tensor(out=of_t[b][:], in0=of_t[b][:], in1=ob_t[b][:], op=ALU.add),
                 reads=[("of", b), ("ob", b)], writes=[("of", b)])
            S.op("dve", lambda e, b=b: e.memset(st2[b][:], 0.0), writes=[("st2", b)])
            for hh in range(4):
                S.op("act", lambda e, b=b, hh=hh: e.activation(out=junk2[:], in_=of_t[b][:, hh * 512:(hh + 1) * 512],
                                                                func=AF.Square, accum_out=st2[b][:, hh:hh + 1]),
                     reads=[("of", b), ("st2", b)], writes=["junk2", ("st2", b)])
            S.op("dve", lambda e, b=b: e.tensor_scalar(out=st2[b][:, 4:8], in0=st2[b][:, 0:4], scalar1=1.0 / 512,
                                                        scalar2=EPS, op0=ALU.mult, op1=ALU.add),
                 reads=[("st2", b)], writes=[("st2", b)])
            S.op("act", lambda e, b=b: e.activation(out=st2[b][:, 8:12], in_=st2[b][:, 4:8], func=AF.Sqrt),
                 reads=[("st2", b)], writes=[("st2", b)])
            S.op("dve", lambda e, b=b: e.reciprocal(out=st2[b][:, 12:16], in_=st2[b][:, 8:12]),
                 reads=[("st2", b)], writes=[("st2", b)])
            for hh in range(4):
                S.op("dve", lambda e, b=b, hh=hh: e.scalar_tensor_tensor(
                    out=of_t[b][:, hh * 512:(hh + 1) * 512], in0=of_t[b][:, hh * 512:(hh + 1) * 512],
                    scalar=st2[b][:, 12 + hh:13 + hh], in1=gn_t[:], op0=ALU.mult, op1=ALU.mult),
                    reads=[("of", b), ("st2", b), "gn"], writes=[("of", b)])
            S.op("dve", lambda e, b=b: e.tensor_tensor(out=yb_t[b][:], in0=of_t[b][:], in1=gz_t[b][:], op=ALU.mult),
                 reads=[("of", b), ("gz", b)], writes=[("yb", b)])
            for q2 in range(2):
                def trf(e, b=b, q2=q2):
                    ins = None
                    for k in range(8):
                        kc = q2 * 8 + k
                        ins = e.transpose(out=ptb[0][:, k * 128:(k + 1) * 128], in_=yb_t[b][:, kc * 128:(kc + 1) * 128],
                                          identity=ident[:])
                    return ins
                S.op("pe", trf, reads=[("yb", b), "ident"], writes=[("pt", 0)])
                S.op("act", lambda e, ys=ys, q2=q2, sub=sub: e.copy(
                    out=ybT_s[ys][:, q2 * 8:(q2 + 1) * 8, sub * 128:(sub + 1) * 128],
                    in_=ptb[0][:].rearrange("p (k t) -> p k t", k=8)), reads=[("pt", 0)], writes=[("ybT_s", ys, sub, q2)])
        S.dma("sp", lambda e, ys=ys, gi=gi: e.dma_start(
            out=ybT_d[:, gi * 512:(gi + 1) * 512].rearrange("(k p) t -> p k t", p=128), in_=ybT_s[ys][:]),
            reads=[("ybT_s", ys, s_, q_) for s_ in range(4) for q_ in range(2)], writes=[("ybT_d", gi)])
    S.barrier()
    A.reset(base_mark)

    mergedT = A.alloc([128, 32, 512], BF16, "mergedT")
    ov_mark = A.mark()
    for tt in range(4):
        A.reset(ov_mark)
        hT1 = A.alloc([128, 32, 512], BF16, "hT1")
        yaT_t = A.alloc([128, 8, 512], BF16, "yaT_t")
        ybT_t = A.alloc([128, 16, 512], BF16, "ybT_t")
        wg = [[A.alloc([128, 32, 128], BF16, "wg") for _ in range(2)] for _ in range(2)]
        wa = [A.alloc([128, 8, 128], BF16, "wa") for _ in range(2)]
        wb_ = [A.alloc([128, 16, 128], BF16, "wb") for _ in range(2)]
        sg = [[A.alloc([128, 512], F32, "sg") for _ in range(2)] for _ in range(2)]
        mt = [A.alloc([128, 512], F32, "mt") for _ in range(2)]
        load_h(hT1, "hT1", [(tt, 0, 512)])
        S.dma("sp", lambda e, tt=tt: e.dma_start(
            out=yaT_t[:], in_=yaT_d[:, tt * 512:(tt + 1) * 512].rearrange("(k p) t -> p k t", p=128)), writes=["yaT_t"])
        S.dma("sp", lambda e, tt=tt: e.dma_start(
            out=ybT_t[:], in_=ybT_d[:, tt * 512:(tt + 1) * 512].rearrange("(k p) t -> p k t", p=128)), writes=["ybT_t"])
        for blk in range(32):
            b = blk % 2
            S.dma("pool", lambda e, b=b, blk=blk: e.dma_start(out=wg[b][0][:], in_=w_blk[B_MGA + blk]),
                  writes=[("wg", b, 0)])
            S.dma("pool", lambda e, b=b, blk=blk: e.dma_start(out=wg[b][1][:], in_=w_blk[B_MGB + blk]),
                  writes=[("wg", b, 1)])
            S.dma("pool", lambda e, b=b, blk=blk: e.dma_start(out=wa[b][:], in_=wua[blk]), writes=[("wa", b)])
            S.dma("pool", lambda e, b=b, blk=blk: e.dma_start(out=wb_[b][:], in_=wub[blk]), writes=[("wb", b)])
            for ab in range(2):
                pi = next_ps()
                S.op("pe", mm_fm(psb[pi][:, 0:512], wg[b][ab], hT1, 512), reads=[("wg", b, ab), "hT1"],
                     writes=[("ps", pi)])
                S.op("act", lambda e, b=b, ab=ab, pi=pi: e.activation(out=sg[b][ab][:], in_=psb[pi][:, 0:512],
                                                                      func=AF.Sigmoid),
                     reads=[("ps", pi)], writes=[("sg", b, ab)])
            pa = next_ps()
            S.op("pe", mm_fm(psb[pa][:, 0:512], wa[b], yaT_t, 512, nk=8), reads=[("wa", b), "yaT_t"],
                 writes=[("ps", pa)])
            S.op("dve", lambda e, b=b, pa=pa: e.tensor_tensor(out=mt[b][:], in0=sg[b][0][:], in1=psb[pa][:, 0:512],
                                                              op=ALU.mult),
                 reads=[("ps", pa), ("sg", b, 0)], writes=[("mt", b)])
            pbb = next_ps()
            S.op("pe", mm_fm(psb[pbb][:, 0:512], wb_[b], ybT_t, 512, nk=16), reads=[("wb", b), "ybT_t"],
                 writes=[("ps", pbb)])
            S.op("dve", lambda e, b=b, pbb=pbb: e.tensor_tensor(out=sg[b][1][:], in0=sg[b][1][:],
                                                                in1=psb[pbb][:, 0:512], op=ALU.mult),
                 reads=[("ps", pbb), ("sg", b, 1)], writes=[("sg", b, 1)])
            S.op("dve", lambda e, b=b, blk=blk: e.tensor_tensor(out=mergedT[:, blk, :], in0=mt[b][:],
                                                                in1=sg[b][1][:], op=ALU.add),
                 reads=[("mt", b), ("sg", b, 1)], writes=[("merged", blk)])
        S.barrier()
        A.reset(ov_mark)
        wo_t = [A.alloc([128, 32, 256], BF16, "wo_t") for _ in range(2)]
        ypre = [A.alloc([128, D], F32, "ypre") for _ in range(4)]
        xs = [A.alloc([128, 256], F32, "xs") for _ in range(4)]
        gs = [A.alloc([128, 512], F32, "gs") for _ in range(2)]
        st3 = A.alloc([128, 8], F32, "st3")
        st4 = A.alloc([128, 16], F32, "st4")
        junk3 = A.alloc([128, 512], BF16, "junk3")
        xc = 0
        for cb in range(16):
            b = cb % 2
            S.dma("pool", lambda e, b=b, cb=cb: e.dma_start(out=wo_t[b][:], in_=wo[cb]), writes=[("wo_t", b)])
            for sub in range(4):
                xi = xc % 4
                xc += 1
                r0 = tt * 512 + sub * 128
                S.dma("sp", lambda e, xi=xi, r0=r0, cb=cb: e.dma_start(out=xs[xi][:],
                                                                      in_=x_own[r0:r0 + 128, cb * 256:(cb + 1) * 256]),
                      writes=[("xs", xi)])
                pi = next_ps()

                def omm2(e, pi=pi, sub=sub, b=b):
                    ins = None
                    for kc in range(32):
                        ins = e.matmul(psb[pi][:, 0:256], lhsT=mergedT[:, kc, sub * 128:(sub + 1) * 128],
                                       rhs=wo_t[b][:, kc, :], start=(kc == 0), stop=(kc == 31))
                    return ins
                S.op("pe", omm2, reads=[("wo_t", b)] + [("merged", k_) for k_ in range(32)], writes=[("ps", pi)])
                S.op("dve", lambda e, pi=pi, sub=sub, cb=cb, xi=xi: e.tensor_tensor(
                    out=ypre[sub][:, cb * 256:(cb + 1) * 256], in0=psb[pi][:, 0:256], in1=xs[xi][:], op=ALU.add),
                    reads=[("ps", pi), ("xs", xi)], writes=[("ypre", sub, cb)])
        for sub in range(4):
            S.op("dve", lambda e: e.memset(st3[:], 0.0), writes=[("st3", c_) for c_ in range(8)])
            for c8 in range(8):
                S.op("act", lambda e, sub=sub, c8=c8: e.activation(
                    out=junk3[:], in_=ypre[sub][:, c8 * 512:(c8 + 1) * 512], func=AF.Square,
                    accum_out=st4[:, sub * 4 + 0:sub * 4 + 1] if False else st3[:, c8:c8 + 1]),
                    reads=[("ypre", sub, 2 * c8), ("ypre", sub, 2 * c8 + 1)], writes=["junk3", ("st3", c8)])
            S.op("dve", lambda e, sub=sub: e.tensor_reduce(out=st4[:, sub * 4:sub * 4 + 1], in_=st3[:, 0:8],
                                                           axis=mybir.AxisListType.X, op=ALU.add),
                 reads=[("st3", c_) for c_ in range(8)], writes=[("st4", sub, 0)])
            S.op("dve", lambda e, sub=sub: e.tensor_scalar(out=st4[:, sub * 4 + 1:sub * 4 + 2],
                                                            in0=st4[:, sub * 4:sub * 4 + 1], scalar1=1.0 / D,
                                                            scalar2=EPS, op0=ALU.mult, op1=ALU.add),
                 reads=[("st4", sub, 0)], writes=[("st4", sub, 1)])
            S.op("act", lambda e, sub=sub: e.activation(out=st4[:, sub * 4 + 2:sub * 4 + 3],
                                                        in_=st4[:, sub * 4 + 1:sub * 4 + 2], func=AF.Sqrt),
                 reads=[("st4", sub, 1)], writes=[("st4", sub, 2)])
            S.op("dve", lambda e, sub=sub: e.reciprocal(out=st4[:, sub * 4 + 3:sub * 4 + 4],
                                                        in_=st4[:, sub * 4 + 2:sub * 4 + 3]),
                 reads=[("st4", sub, 2)], writes=[("st4", sub, 3)])
            for c8 in range(8):
                gi = c8 % 2
                S.dma("sp", lambda e, gi=gi, c8=c8: e.dma_start(out=gs[gi][:], in_=fng_b[:, c8 * 512:(c8 + 1) * 512]),
                      writes=[("gs", gi)])
                S.op("dve", lambda e, sub=sub, c8=c8, gi=gi: e.scalar_tensor_tensor(
                    out=ypre[sub][:, c8 * 512:(c8 + 1) * 512], in0=ypre[sub][:, c8 * 512:(c8 + 1) * 512],
                    scalar=st4[:, sub * 4 + 3:sub * 4 + 4], in1=gs[gi][:], op0=ALU.mult, op1=ALU.mult),
                    reads=[("ypre", sub, 2 * c8), ("ypre", sub, 2 * c8 + 1), ("st4", sub, 3), ("gs", gi)],
                    writes=[("ypre", sub, 2 * c8), ("ypre", sub, 2 * c8 + 1)])
            r0 = tt * 512 + sub * 128
            S.dma("sp", lambda e, sub=sub, r0=r0: e.dma_start(out=y_out[r0:r0 + 128, :], in_=ypre[sub][:]),
                  reads=[("ypre", sub, c_) for c_ in range(16)], writes=[("y_out", r0)])
        S.barrier()
    S.emit()
    return nc


def _consts():
    slopes = np.exp2(-8.0 * (np.arange(8, dtype=np.float32) + 1.0) / 8).astype(np.float32)
    i = np.arange(128)[:, None]
    c = np.arange(256)[None, :]
    delta = i - c + 64
    att_bias = np.zeros((3, 8, 128, 256), np.float32)
    for g, d in enumerate(DILS):
        dist = (np.abs(delta) * d).astype(np.float32)
        for h in range(8):
            b = -slopes[h] * dist
            att_bias[g, h] = np.where(np.abs(delta) <= 64, b, np.float32(-30000.0))
    s = np.arange(128)[:, None]
    t = np.arange(128)[None, :]
    k = np.float32(-1.0 / 16.0)
    tri = np.zeros((4, 128, 128), np.float32)
    tri[0] = np.where(s <= t, k, 0)
    tri[1] = np.where(s > t, k, 0)
    tri[2] = np.where(s >= t, k, 0)
    tri[3] = np.where(s < t, k, 0)
    msk = np.zeros((2, 128, 128), np.float32)
    msk[0] = (s <= t)
    msk[1] = (s > t)
    return att_bias, tri, msk


def _prep_shared(inp):
    w_in = np.asarray(inp["w_in"])[0]
    cols = block_cols()
    idx = np.zeros(NBLK * 128, np.int64)
    valid = np.zeros(NBLK * 128, bool)
    for b, (c0, n) in enumerate(cols):
        idx[b * 128:b * 128 + n] = np.arange(c0, c0 + n)
        valid[b * 128:b * 128 + n] = True
    wsel = np.take(w_in, idx, axis=1)
    wsel[:, ~valid] = 0.0
    w_blk = np.ascontiguousarray(wsel.reshape(32, 128, NBLK, 128).transpose(2, 1, 0, 3))
    wua = np.ascontiguousarray(np.asarray(inp["w_up_a"])[0].reshape(8, 128, 32, 128).transpose(2, 1, 0, 3))
    wub = np.ascontiguousarray(np.asarray(inp["w_up_b"])[0].reshape(16, 128, 32, 128).transpose(2, 1, 0, 3))
    wo = np.ascontiguousarray(np.asarray(inp["w_o"])[0].reshape(32, 128, 16, 256).transpose(2, 1, 0, 3))
    att_bias, tri, msk = _consts()
    w2 = [np.asarray(inp["gla_w2_f"])[0], np.asarray(inp["gla_w2_b"])[0]]
    bb = [np.asarray(inp["gla_b_f"])[0], np.asarray(inp["gla_b_b"])[0]]
    w2aug = np.zeros((2, 33, 1024), np.float32)
    for di in range(2):
        w2aug[di, 0:16] = w2[di]
        w2aug[di, 32] = bb[di]
    wlr = [np.ascontiguousarray(w_in[:, 16384:16400].reshape(32, 128, 16).transpose(1, 0, 2)),
           np.ascontiguousarray(w_in[:, 16400:16416].reshape(32, 128, 16).transpose(1, 0, 2))]
    sh = dict(
        w_blk=w_blk, wua=wua, wub=wub, wo=wo,
        ng_b=np.ascontiguousarray(np.broadcast_to(np.asarray(inp["norm_g"])[0][None, :], (128, D))).astype(np.float32),
        fng_b=np.ascontiguousarray(np.broadcast_to(np.asarray(inp["final_norm_g"])[None, :], (128, D))).astype(np.float32),
        gng_b=np.ascontiguousarray(np.broadcast_to(np.asarray(inp["gla_norm_g"])[0][None, :], (128, 512))).astype(np.float32),
        w2aug=w2aug, att_bias=att_bias, tri=tri, msk=msk, ident_in=np.eye(128, dtype=np.float32))
    return sh, w2aug, wlr


def _core_inputs(sh, w2aug, wlr, seq, seg, nseg):
    x_own = seq[seg * T:(seg + 1) * T]
    x_halo = np.zeros((T, D), np.float32)
    if seg > 0:
        x_halo[0:1024] = seq[seg * T - 1024:seg * T]
    if seg < nseg - 1:
        x_halo[1024:2048] = seq[(seg + 1) * T:(seg + 1) * T + 1024]
    x_ctx = np.zeros((NCTX, D), np.float32)
    cw2aug = np.zeros((3, 33, 1024), np.float32)
    w_clr = np.zeros((3, 128, 32, 16), np.float32)
    slots = []
    for j in range(0, seg):
        slots.append((j, 0))
    for j in range(nseg - 1, seg, -1):
        slots.append((j, 1))
    n_f = seg if nseg > 1 else 0
    for i, (j, di) in enumerate(slots):
        blk = seq[j * T:(j + 1) * T]
        x_ctx[i * T:(i + 1) * T] = blk if di == 0 else blk[::-1]
        cw2aug[i] = w2aug[di]
        w_clr[i] = wlr[di]
    for i in range(len(slots), 3):
        cw2aug[i] = w2aug[0]
        w_clr[i] = wlr[0]
    bnd = np.zeros((128, 8), np.float32)
    bnd[:, n_f] = 1.0
    bnd[:, 4:8] = 1.0 - bnd[:, 0:4]
    kb = np.zeros((128, 69), np.float32)
    col = 0
    for g, d in enumerate(DILS):
        hw = 64 * d
        L = T // d
        nq = L // 128
        for r in range(d):
            for jk in range(nq + 1):
                u = jk * 128 + np.arange(128)
                p = r + d * u
                tglob = seg * T + (p - hw)
                ok = (tglob >= 0) & (tglob < nseg * T)
                kb[:, col] = np.where(ok, 0.0, -30000.0)
                col += 1
    d_ = dict(sh)
    d_.update(x_own=np.ascontiguousarray(x_own), x_halo=x_halo, x_ctx=x_ctx, cw2aug=cw2aug, w_clr=w_clr,
              att_kb=kb, bnd=bnd)
    return d_


_NC_CACHE = {}


def kernel(x_prompt, x_sample, norm_g, w_in, gla_w2_f, gla_b_f, gla_w2_b, gla_b_b, gla_norm_g,
           w_up_a, w_up_b, w_o, final_norm_g):
    inp = dict(w_in=w_in, w_up_a=w_up_a, w_up_b=w_up_b, w_o=w_o, norm_g=norm_g, final_norm_g=final_norm_g,
               gla_norm_g=gla_norm_g, gla_w2_f=gla_w2_f, gla_w2_b=gla_w2_b, gla_b_f=gla_b_f, gla_b_b=gla_b_b)
    sh, w2aug, wlr = _prep_shared(inp)
    xp = np.asarray(x_prompt, dtype=np.float32)
    xs = np.asarray(x_sample, dtype=np.float32)
    in_maps = []
    for c in range(4):
        in_maps.append(_core_inputs(sh, w2aug, wlr, xp[c], 0, 1))
    for c in range(4):
        in_maps.append(_core_inputs(sh, w2aug, wlr, xs[0], c, 4))
    if "nc" not in _NC_CACHE:
        _NC_CACHE["nc"] = build_program()
    nc = _NC_CACHE["nc"]
    res = run_bass_kernel_spmd(nc, in_maps, core_ids=list(range(8)))
    outs = [np.asarray(r["y_out"], dtype=np.float32) for r in res.results]
    y_prompt = np.stack(outs[0:4], axis=0)
    y_sample = np.concatenate(outs[4:8], axis=0)[None]
    return (y_prompt, y_sample)
```

```python
import numpy as np
import concourse.bass as bass
import concourse.mybir as mybir
from concourse.bass_utils import run_bass_kernel_spmd

F32 = mybir.dt.float32
BF16 = mybir.dt.bfloat16
AF = mybir.ActivationFunctionType
ALU = mybir.AluOpType

D = 4096
T = 2048
NCTX = 6144
PAYW = 8256
EPS = 1e-6
NBLK = 193
DILS = (1, 4, 16)
ARENA0 = 16640
ARENA1 = 229376


def att_blk(hs, g, j):
    return (hs * 3 + g) * 3 + j


B_ATTZ = 72
B_GQ = 80
B_GK = 88
B_GV = 96
B_GZ = 112
B_LR = 128
B_MGA = 129
B_MGB = 161


def block_cols():
    cols = []
    for hs in range(8):
        for g in range(3):
            for j in range(3):
                cols.append((((g * 3 + j) * 8 + hs) * 128, 128))
    for hs in range(8):
        cols.append((9216 + hs * 128, 128))
    for i in range(8):
        cols.append((10240 + i * 128, 128))
    for i in range(8):
        cols.append((11264 + i * 128, 128))
    for i in range(16):
        cols.append((12288 + i * 128, 128))
    for i in range(16):
        cols.append((14336 + i * 128, 128))
    cols.append((16384, 32))
    for i in range(32):
        cols.append((16416 + i * 128, 128))
    for i in range(32):
        cols.append((20512 + i * 128, 128))
    assert len(cols) == NBLK
    return cols


class Sched:
    def __init__(self, nc, n_dma_slots=32, epoch=30000):
        self.nc = nc
        self.names = ["pe", "act", "dve", "pool", "sp"]
        self.prog = {k: [] for k in self.names}
        self.epoch = epoch
        self.esem = {}
        self.ecount = {}
        self.nsem = 0
        for k in self.names:
            self._new_esem(k)
        self.dslots = {"sp": [[self._sem("dsp%d" % i), 0] for i in range(n_dma_slots)],
                       "pool": [[self._sem("dpl%d" % i), 0] for i in range(16)]}
        self.dnext = {"sp": 0, "pool": 0}
        self.seen = {k: {} for k in self.names}
        self.lastw = {}
        self.readers = {}
        self.nops = 0

    def _sem(self, name):
        self.nsem += 1
        return self.nc.alloc_semaphore(name="s_%s_%d" % (name, self.nsem))

    def _new_esem(self, k):
        self.esem[k] = self._sem(k)
        self.ecount[k] = 0

    def _wait(self, e, tok):
        sem, val = tok
        sid = id(sem)
        if self.seen[e].get(sid, 0) >= val:
            return
        self.seen[e][sid] = val
        self.prog[e].append(("wait", sem, val))

    def _deps(self, e, reads, writes):
        for r in reads:
            t = self.lastw.get(r)
            if t is not None:
                self._wait(e, t)
        for w in writes:
            t = self.lastw.get(w)
            if t is not None:
                self._wait(e, t)
            for t in self.readers.get(w, ()):
                self._wait(e, t)

    def _commit(self, tok, reads, writes):
        for w in writes:
            self.lastw[w] = tok
            self.readers[w] = []
        for r in reads:
            if r in writes:
                continue
            self.readers.setdefault(r, []).append(tok)

    def rec_begin(self):
        self.rec = []

    def rec_end(self):
        r = self.rec
        self.rec = None
        return r

    def replay(self, items):
        for (kind, e, fn, reads, writes) in items:
            if kind == "op":
                self.op(e, fn, reads, writes)
            else:
                self.dma(e, fn, reads, writes)

    def merge(self, lists, weights=None):
        weights = weights or [1] * len(lists)
        pos = [0] * len(lists)
        alive = True
        while alive:
            alive = False
            for i, l in enumerate(lists):
                n = min(weights[i], len(l) - pos[i])
                if n > 0:
                    self.replay(l[pos[i]:pos[i] + n])
                    pos[i] += n
                    alive = True

    def op(self, e, fn, reads=(), writes=()):
        if getattr(self, "rec", None) is not None:
            self.rec.append(("op", e, fn, tuple(reads), tuple(writes)))
            return None
        self._deps(e, reads, writes)
        if self.ecount[e] >= self.epoch:
            self._new_esem(e)
        self.ecount[e] += 1
        tok = (self.esem[e], self.ecount[e])
        self.prog[e].append(("op", fn, tok[0], 1))
        self._commit(tok, reads, writes)
        self.nops += 1
        return tok

    def dma(self, e, fn, reads=(), writes=()):
        if getattr(self, "rec", None) is not None:
            self.rec.append(("dma", e, fn, tuple(reads), tuple(writes)))
            return None
        self._deps(e, reads, writes)
        ring = self.dslots[e]
        slot = ring[self.dnext[e]]
        self.dnext[e] = (self.dnext[e] + 1) % len(ring)
        if slot[1] > 0:
            self._wait(e, (slot[0], slot[1]))
        slot[1] += 16
        tok = (slot[0], slot[1])
        self.prog[e].append(("op", fn, tok[0], 16))
        self._commit(tok, reads, writes)
        self.nops += 1
        return tok

    def cc(self, e, fn, writes=()):
        sem = self._sem("cc")
        self.prog[e].append(("op", fn, sem, 1))
        tok = (sem, 1)
        self._wait(e, tok)
        self._commit(tok, (), writes)
        if not hasattr(self, "extra"):
            self.extra = []
        self.extra.append(tok)

    def barrier(self):
        for e in self.names:
            for tok in getattr(self, "extra", []):
                self._wait(e, tok)
            for ring in self.dslots.values():
                for s in ring:
                    if s[1] > 0:
                        self._wait(e, (s[0], s[1]))
            for k in self.names:
                if k != e and self.ecount[k] > 0:
                    self._wait(e, (self.esem[k], self.ecount[k]))
        for e in self.names:
            if self.ecount[e] > 0:
                self._wait(e, (self.esem[e], self.ecount[e]))
        self.lastw = {}
        self.readers = {}

    def emit(self):
        nc = self.nc
        with nc.Block() as block:
            def mk(k):
                def body(engine):
                    for item in self.prog[k]:
                        if item[0] == "wait":
                            engine.wait_ge(item[1], item[2])
                        else:
                            ins = item[1](engine)
                            ins.then_inc(item[2], item[3])
                return body
            block.tensor(mk("pe"))
            block.scalar(mk("act"))
            block.vector(mk("dve"))
            block.gpsimd(mk("pool"))
            block.sync(mk("sp"))


class Arena:
    def __init__(self, nc):
        self.nc = nc
        self.base = ARENA0
        self.ptr = ARENA0
        self.n = 0

    def mark(self):
        return self.ptr

    def reset(self, mark):
        self.ptr = mark

    def alloc(self, shape, dtype, name="t"):
        esz = 4 if dtype == F32 else 2
        per = esz
        for s in shape[1:]:
            per *= s
        off = (self.ptr + 63) // 64 * 64
        assert off + per <= ARENA1, ("SBUF arena overflow", name, off, per)
        self.ptr = off + per
        self.n += 1
        return self.nc.alloc_sbuf_tensor_at("%s_%d" % (name, self.n), list(shape), dtype, offset=off)


def build_program(dbg=False):
    nc = bass.Bass("TRN2", target_bir_lowering=False)
    S = Sched(nc)
    A = Arena(nc)

    def din(name, shape, dt=F32):
        return nc.dram_tensor(name, list(shape), dt, kind="ExternalInput").ap()

    def dscr(name, shape, dt):
        if dbg and name in ("yaT_d", "ybT_d", "gk_d", "gqT_d", "gv_d"):
            return nc.dram_tensor(name, list(shape), dt, kind="ExternalOutput").ap()
        return nc.dram_tensor(name, list(shape), dt).ap()

    x_own = din("x_own", [T, D])
    x_halo = din("x_halo", [T, D])
    w_blk = din("w_blk", [NBLK, 128, 32, 128])
    wua = din("wua", [32, 128, 8, 128])
    wub = din("wub", [32, 128, 16, 128])
    wo = din("wo", [16, 128, 32, 256])
    ng_b = din("ng_b", [128, D])
    fng_b = din("fng_b", [128, D])
    gng_b = din("gng_b", [128, 512])
    w2aug = din("w2aug", [2, 33, 1024])
    smask = din("smask", [128, 16])
    att_bias = din("att_bias", [3, 8, 128, 256])
    att_kb = din("att_kb", [128, 69])
    tri = din("tri", [4, 128, 128])
    msk = din("msk", [2, 128, 128])
    ident_in = din("ident_in", [128, 128])
    y_out = nc.dram_tensor("y_out", [T, D], F32, kind="ExternalOutput").ap()

    hT_d = dscr("hT_d", [8, 128, 32, 512], BF16)
    yaT_d = dscr("yaT_d", [1024, T], BF16)
    ybT_d = dscr("ybT_d", [2048, T], BF16)
    gqT_d = dscr("gqT_d", [1024, T], F32)
    gkT_d = dscr("gkT_d", [1024, T], F32)
    lrT_d = dscr("lrT_d", [32, T], F32)
    gk_d = dscr("gk_d", [T, 1024], F32)
    gv_d = dscr("gv_d", [T, 2048], BF16)
    gzs_d = dscr("gzs_d", [T, 2048], F32)
    cin_d = dscr("cin_d", [4 * 128, PAYW], F32)
    cout_d = dscr("cout_d", [4 * 128, PAYW], F32)
    sin_d = dscr("sin_d", [2, 128, 4096], F32)
    o_d = dscr("o_d", [2, T, 2048], F32)

    psb = [nc.alloc_psum_tensor("psb%d" % i, [128, 512], F32) for i in range(8)]
    ptb = [psb[7][:].bitcast(BF16)]
    rot = {"ps": 0, "pt": 0}

    def next_ps(lo=0, hi=7):
        i = lo + rot["ps"] % (hi - lo)
        rot["ps"] += 1
        return i

    ident = A.alloc([128, 128], BF16, "ident")
    ones_b = A.alloc([128, 128], BF16, "ones")
    smk_t = A.alloc([128, 16], F32, "smk")
    S.dma("pool", lambda e: e.dma_start(out=ident[:], in_=ident_in), writes=["ident"])
    S.dma("sp", lambda e: e.dma_start(out=smk_t[:], in_=smask), writes=["smk"])
    S.op("dve", lambda e: e.memset(ones_b[:], 1.0), writes=["ones"])
    base_mark = A.mark()

    gb = A.alloc([128, D], F32, "gb")
    S.dma("sp", lambda e: e.dma_start(out=gb[:], in_=ng_b), writes=["gb"])
    xt = [A.alloc([128, D], F32, "xt") for _ in range(2)]
    hb = [A.alloc([128, D], BF16, "hb") for _ in range(2)]
    junk = A.alloc([128, D], BF16, "junk")
    hTt = [A.alloc([128, 32, 512], BF16, "hTt") for _ in range(2)]
    stat = [A.alloc([128, 4], F32, "stat") for _ in range(2)]
    srcs = [(x_own, 4), (x_halo, 4)]
    tile_id = 0
    it = 0
    for src, nt in srcs:
        for t5 in range(nt):
            hp = tile_id % 2
            for sub in range(4):
                b = it % 2
                r0 = t5 * 512 + sub * 128
                S.dma("sp", lambda e, b=b, r0=r0, src=src: e.dma_start(out=xt[b][:], in_=src[r0:r0 + 128, :]),
                      writes=[("xt", b)])
                S.op("dve", lambda e, b=b: e.memset(stat[b][:, 0:1], 0.0), writes=[("st0", b)])
                S.op("act", lambda e, b=b: e.activation(out=junk[:], in_=xt[b][:], func=AF.Square,
                                                          accum_out=stat[b][:, 0:1]),
                     reads=[("xt", b)], writes=["junk", ("st0", b)])
                S.op("dve", lambda e, b=b: e.tensor_scalar(out=stat[b][:, 1:2], in0=stat[b][:, 0:1],
                                                            scalar1=1.0 / D, scalar2=EPS, op0=ALU.mult, op1=ALU.add),
                     reads=[("st0", b)], writes=[("st1", b)])
                S.op("act", lambda e, b=b: e.activation(out=stat[b][:, 2:3], in_=stat[b][:, 1:2], func=AF.Sqrt),
                     reads=[("st1", b)], writes=[("st2", b)])
                S.op("dve", lambda e, b=b: e.reciprocal(out=stat[b][:, 3:4], in_=stat[b][:, 2:3]),
                     reads=[("st2", b)], writes=[("st3", b)])
                S.op("dve", lambda e, b=b: e.scalar_tensor_tensor(out=hb[b][:], in0=xt[b][:], scalar=stat[b][:, 3:4],
                                                                   in1=gb[:], op0=ALU.mult, op1=ALU.mult),
                     reads=[("xt", b), ("st3", b), "gb"], writes=[("hb", b)])
                for q4 in range(4):
                    pb = 0
                    rot["pt"] += 1

                    def tr(e, b=b, q4=q4, pb=pb):
                        ins = None
                        for k in range(8):
                            kc = q4 * 8 + k
                            ins = e.transpose(out=ptb[pb][:, k * 128:(k + 1) * 128],
                                              in_=hb[b][:, kc * 128:(kc + 1) * 128], identity=ident[:])
                        return ins
                    S.op("pe", tr, reads=[("hb", b), "ident"], writes=[("pt", pb)])
                    eng = "act" if q4 % 2 == 0 else "dve"

                    def ev(e, hp=hp, q4=q4, pb=pb, sub=sub, eng=eng):
                        o = hTt[hp][:, q4 * 8:(q4 + 1) * 8, sub * 128:(sub + 1) * 128]
                        i = ptb[pb][:].rearrange("p (k t) -> p k t", k=8)
                        if eng == "act":
                            return e.copy(out=o, in_=i)
                        return e.tensor_copy(out=o, in_=i)
                    S.op(eng, ev, reads=[("pt", pb)], writes=[("hTt", hp, sub, q4)])
                it += 1
            S.dma("sp", lambda e, hp=hp, tile_id=tile_id: e.dma_start(out=hT_d[tile_id], in_=hTt[hp][:]),
                  reads=[("hTt", hp, s_, q_) for s_ in range(4) for q_ in range(4)], writes=[("hT_d", tile_id)])
            tile_id += 1
    S.barrier()
    A.reset(base_mark)

    def load_h(buf, key, pieces):
        c = 0
        for (tl, c0, n) in pieces:
            S.dma("sp", lambda e, tl=tl, c0=c0, n=n, c=c: e.dma_start(out=buf[:, :, c:c + n],
                                                                      in_=hT_d[tl, :, :, c0:c0 + n]),
                  reads=[("hT_d", tl)], writes=[key])
            c += n
        return c

    def mm_fm(ps_ap, wbuf, hbuf, n, nk=32, wsl=None):
        def f(e):
            ins = None
            for kc in range(nk):
                lw = wbuf[:, kc, :] if wsl is None else wbuf[:, kc, wsl[0]:wsl[1]]
                ins = e.matmul(ps_ap, lhsT=lw, rhs=hbuf[:, kc, 0:n], start=(kc == 0), stop=(kc == nk - 1))
            return ins
        return f

    def mm_tm(ps_ap, hbuf, t0, wbuf, ncols, nk=32):
        def f(e):
            ins = None
            for kc in range(nk):
                ins = e.matmul(ps_ap, lhsT=hbuf[:, kc, t0:t0 + 128], rhs=wbuf[:, kc, 0:ncols],
                               start=(kc == 0), stop=(kc == nk - 1))
            return ins
        return f

    hbuf = [A.alloc([128, 32, 512], BF16, "hbuf") for _ in range(2)]
    wfm = [A.alloc([128, 32, 128], BF16, "wfm") for _ in range(4)]
    wtm = [A.alloc([128, 32, 512], BF16, "wtm") for _ in range(2)]
    stg = [A.alloc([128, 512], F32, "stg") for _ in range(4)]
    stgb = [A.alloc([128, 512], BF16, "stgb") for _ in range(4)]
    sctr = 0
    hctr = 0
    fm_list = [(B_GQ + i, gqT_d, i * 128, 128, 256.0 ** -0.5) for i in range(8)]
    fm_list += [(B_GK + i, gkT_d, i * 128, 128, 1.0) for i in range(8)]
    fm_list += [(B_LR, lrT_d, 0, 32, 1.0)]
    for c4 in range(0, len(fm_list), 4):
        grp = fm_list[c4:c4 + 4]
        for wi, (blk, dst, r0, m, sc) in enumerate(grp):
            S.dma("pool", lambda e, wi=wi, blk=blk: e.dma_start(out=wfm[wi][:], in_=w_blk[blk]), writes=[("wfm", wi)])
        for i in range(4):
            hb_i = hctr % 2
            hctr += 1
            load_h(hbuf[hb_i], ("h", hb_i), [(i, 0, 512)])
            for wi, (blk, dst, r0, m, sc) in enumerate(grp):
                pi = next_ps()
                S.op("pe", mm_fm(psb[pi][0:m, 0:512], wfm[wi], hbuf[hb_i], 512, wsl=(0, m)),
                     reads=[("wfm", wi), ("h", hb_i)], writes=[("ps", pi)])
                sb_i = sctr % 4
                sctr += 1
                S.op("act", lambda e, sb_i=sb_i, pi=pi, m=m, sc=sc: e.activation(
                    out=stg[sb_i][0:m, :], in_=psb[pi][0:m, 0:512], func=AF.Copy, scale=sc),
                    reads=[("ps", pi)], writes=[("stg", sb_i)])
                S.dma("sp", lambda e, sb_i=sb_i, dst=dst, r0=r0, m=m, i=i: e.dma_start(
                    out=dst[r0:r0 + m, i * 512:(i + 1) * 512], in_=stg[sb_i][0:m, :]),
                    reads=[("stg", sb_i)], writes=[("gfm", id(dst), r0, i)])
    tm_list = []
    for cg in range(2):
        tm_list.append((B_GK + 4 * cg, "k", cg))
    for cg in range(4):
        tm_list.append((B_GV + 4 * cg, "v", cg))
    for cg in range(4):
        tm_list.append((B_GZ + 4 * cg, "z", cg))
    wctr = 0
    for (blk0, kind, cg) in tm_list:
        wi = wctr % 2
        wctr += 1
        for b4 in range(4):
            S.dma("pool", lambda e, wi=wi, blk0=blk0, b4=b4: e.dma_start(
                out=wtm[wi][:, :, b4 * 128:(b4 + 1) * 128], in_=w_blk[blk0 + b4]), writes=[("wtm", wi)])
        tiles = list(range(4))
        for tl in tiles:
            hb_i = hctr % 2
            hctr += 1
            load_h(hbuf[hb_i], ("h", hb_i), [(tl, 0, 512)])
            for sub in range(4):
                pi = next_ps()
                S.op("pe", mm_tm(psb[pi][:, 0:512], hbuf[hb_i], sub * 128, wtm[wi], 512),
                     reads=[("wtm", wi), ("h", hb_i)], writes=[("ps", pi)])
                sb_i = sctr % 4
                sctr += 1
                row0 = tl * 512 + sub * 128
                dk, dv_, dz = gk_d, gv_d, gzs_d
                if kind == "k":
                    S.op("act", lambda e, sb_i=sb_i, pi=pi: e.copy(out=stg[sb_i][:], in_=psb[pi][:, 0:512]),
                         reads=[("ps", pi)], writes=[("stg", sb_i)])
                    S.dma("sp", lambda e, sb_i=sb_i, dk=dk, row0=row0, cg=cg: e.dma_start(
                        out=dk[row0:row0 + 128, cg * 512:(cg + 1) * 512], in_=stg[sb_i][:]),
                        reads=[("stg", sb_i)], writes=[("gtm", kind, cg, tl, sub)])
                elif kind == "v":
                    S.op("dve", lambda e, sb_i=sb_i, pi=pi: e.tensor_copy(out=stgb[sb_i][:], in_=psb[pi][:, 0:512]),
                         reads=[("ps", pi)], writes=[("stgb", sb_i)])
                    S.dma("sp", lambda e, sb_i=sb_i, dv_=dv_, row0=row0, cg=cg: e.dma_start(
                        out=dv_[row0:row0 + 128, cg * 512:(cg + 1) * 512], in_=stgb[sb_i][:]),
                        reads=[("stgb", sb_i)], writes=[("gtm", kind, cg, tl, sub)])
                else:
                    S.op("act", lambda e, sb_i=sb_i, pi=pi: e.activation(out=stg[sb_i][:], in_=psb[pi][:, 0:512],
                                                                         func=AF.Silu),
                         reads=[("ps", pi)], writes=[("stg", sb_i)])
                    S.dma("sp", lambda e, sb_i=sb_i, dz=dz, row0=row0, cg=cg: e.dma_start(
                        out=dz[row0:row0 + 128, cg * 512:(cg + 1) * 512], in_=stg[sb_i][:]),
                        reads=[("stg", sb_i)], writes=[("gtm", kind, cg, tl, sub)])
    S.barrier()
    A.reset(base_mark)

    tri_t = A.alloc([128, 4, 128], F32, "tri")
    for i4 in range(4):
        S.dma("sp", lambda e, i4=i4: e.dma_start(out=tri_t[:, i4, :], in_=tri[i4]), writes=["tri"])
    pay = A.alloc([128, PAYW], F32, "pay")
    S.op("dve", lambda e: e.memset(pay[:, 0:8192], 0.0), writes=[("pay", hh, dd, kk) for hh in range(4) for dd in range(2) for kk in range(2)])
    S.op("dve", lambda e: e.memset(pay[:, 8192:PAYW], 1.0), writes=[("payA", hh, dd) for hh in range(4) for dd in range(2)])

    def pre_thread(h):
        X = psb[2 * h]
        Y = psb[2 * h + 1]
        kx = ("ps", 2 * h)
        ky = ("ps", 2 * h + 1)
        K_ = lambda name, *a: (name, h) + a
        lra = [A.alloc([64, 128], F32, "lra") for _ in range(3)]
        w2a = A.alloc([64, 256], F32, "w2a")
        e_t = [A.alloc([128, 256], F32, "e_t") for _ in range(2)]
        sp_t = [A.alloc([128, 256], F32, "sp_t") for _ in range(2)]
        eb_t = [A.alloc([128, 2, 128], F32, "eb_t") for _ in range(2)]
        ekd_t = [A.alloc([128, 256], F32, "ekd_t") for _ in range(2)]
        kd_t = [A.alloc([128, 256], BF16, "kd_t") for _ in range(2)]
        ksc = [A.alloc([128, 256], F32, "ksc") for _ in range(3)]
        vsc = [A.alloc([128, 512], BF16, "vsc") for _ in range(3)]
        cc = {"n": 0, "sc": 0}
        for b in range(3):
            S.op("dve", lambda e, b=b: e.memset(lra[b][:], 0.0), writes=[K_("lra", b)])
            S.op("dve", lambda e, b=b: e.memset(lra[b][32:33, :], 1.0), writes=[K_("lra", b)])
        S.op("dve", lambda e: e.memset(w2a[:], 0.0), writes=[K_("w2a")])
        for di in range(2):
            St = pay[:, di * 4096 + h * 1024:di * 4096 + (h + 1) * 1024].rearrange("p (k d) -> p k d", k=2)
            Pv = pay[:, 8192 + di * 8 + h * 2:8192 + di * 8 + h * 2 + 2]
            last = 127 if di == 0 else 0
            S.dma("sp", lambda e, di=di: e.dma_start(out=w2a[0:33, :], in_=w2aug[di][:, h * 256:(h + 1) * 256]),
                  writes=[K_("w2a")])
            for s1 in (range(16) if di == 0 else range(15, -1, -1)):
                sl = cc["sc"] % 3
                cc["sc"] += 1
                tok0 = s1 * 128
                b = cc["n"] % 2
                cc["n"] += 1
                S.dma("sp", lambda e, sl=sl, tok0=tok0, di=di: e.dma_start(
                    out=lra[sl][0:16, :], in_=lrT_d[16 * di:16 * di + 16, tok0:tok0 + 128]), writes=[K_("lra", sl)])
                S.dma("sp", lambda e, sl=sl, tok0=tok0: e.dma_start(
                    out=ksc[sl][:], in_=gk_d[tok0:tok0 + 128, h * 256:(h + 1) * 256]), writes=[K_("ksc", sl)])
                S.dma("sp", lambda e, sl=sl, tok0=tok0: e.dma_start(
                    out=vsc[sl][:], in_=gv_d[tok0:tok0 + 128, h * 512:(h + 1) * 512]), writes=[K_("vsc", sl)])
                S.op("pe", lambda e, sl=sl: e.matmul(X[:, 0:256], lhsT=lra[sl][0:33, :], rhs=w2a[0:33, :],
                                                      start=True, stop=True), reads=[K_("lra", sl), K_("w2a")], writes=[kx])
                S.op("act", lambda e, b=b: e.activation(out=e_t[b][:], in_=X[:, 0:256], func=AF.Exp, scale=-1.0),
                     reads=[kx], writes=[K_("e_t", b)])
                S.op("act", lambda e, b=b: e.activation(out=sp_t[b][:], in_=e_t[b][:], func=AF.Ln, bias=1.0, scale=1.0),
                     reads=[K_("e_t", b)], writes=[K_("sp_t", b)])

                def bfm(e, b=b, di=di):
                    ins = None
                    for kk in range(2):
                        ins = e.matmul(X[:, kk * 128:(kk + 1) * 128], lhsT=sp_t[b][:, kk * 128:(kk + 1) * 128],
                                       rhs=tri_t[:, 2 * di, :], start=True, stop=True)
                    return ins
                S.op("pe", bfm, reads=[K_("sp_t", b), "tri"], writes=[kx])
                S.op("act", lambda e, b=b: e.activation(out=eb_t[b][:], in_=X[:, 0:256].rearrange("p (k t) -> p k t", k=2),
                                                        func=AF.Exp), reads=[kx], writes=[K_("eb_t", b)])
                S.op("pe", lambda e, b=b, di=di: e.matmul(X[:, 256:512], lhsT=tri_t[:, 2 * di + 1, :], rhs=sp_t[b][:],
                                                          start=True, stop=True), reads=[K_("sp_t", b), "tri"], writes=[kx])
                S.op("act", lambda e, b=b: e.activation(out=ekd_t[b][:], in_=X[:, 256:512], func=AF.Exp),
                     reads=[kx], writes=[K_("ekd_t", b)])
                S.op("dve", lambda e, b=b, sl=sl: e.tensor_tensor(out=kd_t[b][:], in0=ksc[sl][:], in1=ekd_t[b][:],
                                                                  op=ALU.mult),
                     reads=[K_("ksc", sl), K_("ekd_t", b)], writes=[K_("kd_t", b)])
                S.op("dve", lambda e, b=b, Pv=Pv, last=last: e.tensor_tensor(out=Pv, in0=Pv, in1=eb_t[b][:, :, last],
                                                                             op=ALU.mult),
                     reads=[K_("eb_t", b), ("payA", h, di)], writes=[("payA", h, di)])
                for kk in range(2):
                    S.op("pe", lambda e, kk=kk, b=b, sl=sl: e.matmul(Y[:, 0:512], lhsT=kd_t[b][:, kk * 128:(kk + 1) * 128],
                                                                      rhs=vsc[sl][:], start=True, stop=True),
                         reads=[K_("kd_t", b), K_("vsc", sl)], writes=[ky])
                    S.op("dve", lambda e, kk=kk, b=b, St=St, last=last: e.scalar_tensor_tensor(
                        out=St[:, kk, :], in0=St[:, kk, :], scalar=eb_t[b][:, kk, last:last + 1], in1=Y[:, 0:512],
                        op0=ALU.mult, op1=ALU.add), reads=[ky, K_("eb_t", b), ("pay", h, di, kk)],
                        writes=[("pay", h, di, kk)])

    threads = []
    for h in range(4):
        S.rec_begin()
        pre_thread(h)
        threads.append(S.rec_end())
    S.merge(threads)
    stage = [A.alloc([128, PAYW], F32, "stage") for _ in range(2)]
    allpay = [("pay", hh, dd, kk) for hh in range(4) for dd in range(2) for kk in range(2)] + \
             [("payA", hh, dd) for hh in range(4) for dd in range(2)]
    for r_ in range(4):
        sb_ = r_ % 2
        S.op("act", lambda e, sb_=sb_, r_=r_: e.activation(out=stage[sb_][:, 0:4096], in_=pay[:, 0:4096], func=AF.Copy,
                                                            scale=smk_t[:, r_:r_ + 1]),
             reads=allpay + ["smk"], writes=[("stage", sb_, 0)])
        S.op("dve", lambda e, sb_=sb_, r_=r_: e.tensor_scalar(out=stage[sb_][:, 4096:PAYW], in0=pay[:, 4096:PAYW],
                                                               scalar1=smk_t[:, r_:r_ + 1], scalar2=None, op0=ALU.mult),
             reads=allpay + ["smk"], writes=[("stage", sb_, 1)])
        S.dma("sp", lambda e, sb_=sb_, r_=r_: e.dma_start(out=cin_d[r_ * 128:(r_ + 1) * 128, :], in_=stage[sb_][:]),
              reads=[("stage", sb_, 0), ("stage", sb_, 1)], writes=[("cin", r_)])
    S.barrier()
    A.reset(base_mark)
    S.cc("pool", lambda e: e.collective_compute("AllReduce", ALU.add, replica_groups=[list(range(8))],
                                                ins=[cin_d.opt()], outs=[cout_d.opt()]), writes=["cout"])

    hbuf = [A.alloc([128, 32, 512], BF16, "hbuf") for _ in range(2)]
    wbuf = [[A.alloc([128, 32, 128], BF16, "wbuf") for _ in range(3)] for _ in range(2)]
    wz = A.alloc([128, 32, 128], BF16, "wz")
    qT2 = [A.alloc([128, T], BF16, "qT") for _ in range(2)]
    kT2 = [A.alloc([128, 4096], BF16, "kT") for _ in range(2)]
    vT2 = [A.alloc([128, 4096], BF16, "vT") for _ in range(2)]
    zs2 = [A.alloc([128, T], F32, "zs") for _ in range(2)]
    accO = A.alloc([128, T], F32, "accO")
    accR = A.alloc([128, T], F32, "accR")
    tbias = [A.alloc([128, 256], F32, "tbias") for _ in range(2)]
    kb_t = A.alloc([128, 69], F32, "kb")
    sbuf_s = [A.alloc([128, 256], F32, "sb") for _ in range(2)]
    pT = [A.alloc([128, 256], BF16, "pT") for _ in range(2)]
    vt = [A.alloc([128, 128], BF16, "vt") for _ in range(2)]
    rr = [A.alloc([128, 512], F32, "rr") for _ in range(2)]
    yst = [A.alloc([128, 512], BF16, "yst") for _ in range(2)]
    S.dma("sp", lambda e: e.dma_start(out=kb_t[:], in_=att_kb), writes=["kb"])
    SCALE = 128.0 ** -0.5
    kbcol0 = [0, 17, 37]
    gst = {"h": 0, "ps": 0}

    def g_ps():
        i = gst["ps"] % 2
        gst["ps"] += 1
        return i

    def gemm_group(hs, g, par):
        d = DILS[g]
        hw = 64 * d
        qT, kT, vT = qT2[par], kT2[par], vT2[par]
        keys = []
        if g == 0:
            zp = hs % 2
            S.dma("pool", lambda e: e.dma_start(out=wz[:], in_=w_blk[B_ATTZ + hs]), writes=["wz"])
            for i in range(4):
                hb_i = gst["h"] % 2
                gst["h"] += 1
                load_h(hbuf[hb_i], ("h", hb_i), [(i, 0, 512)])
                pi = g_ps()
                S.op("pe", mm_fm(psb[pi][:, 0:512], wz, hbuf[hb_i], 512), reads=["wz", ("h", hb_i)],
                     writes=[("ps", pi)])
                S.op("act", lambda e, pi=pi, i=i, zp=zp: e.activation(out=zs2[zp][:, i * 512:(i + 1) * 512],
                                                                      in_=psb[pi][:, 0:512], func=AF.Silu),
                     reads=[("ps", pi)], writes=[("zs", zp, i)])
        ws = par
        for j in range(3):
            S.dma("pool", lambda e, j=j: e.dma_start(out=wbuf[ws][j][:], in_=w_blk[att_blk(hs, g, j)]),
                  writes=[("w", ws, j)])
        jobs = []
        if hw == 1024:
            jobs.append(([(4, 0, 512)], 0, 512, False, 0))
            jobs.append(([(5, 0, 512)], 512, 512, False, 0))
        else:
            jobs.append(([(5, 512 - hw, hw)], 0, hw, False, 0))
        for i in range(4):
            jobs.append(([(i, 0, 512)], hw + 512 * i, 512, True, 512 * i))
        if hw == 1024:
            jobs.append(([(6, 0, 512)], hw + T, 512, False, 0))
            jobs.append(([(7, 0, 512)], hw + T + 512, 512, False, 0))
        else:
            jobs.append(([(6, 0, hw)], hw + T, hw, False, 0))
        for (pieces, edst, n, has_q, t0) in jobs:
            hb_i = gst["h"] % 2
            gst["h"] += 1
            load_h(hbuf[hb_i], ("h", hb_i), pieces)
            for j in ([0, 1, 2] if has_q else [1, 2]):
                pi = g_ps()
                S.op("pe", mm_fm(psb[pi][:, 0:n], wbuf[ws][j], hbuf[hb_i], n),
                     reads=[("w", ws, j), ("h", hb_i)], writes=[("ps", pi)])
                if j == 0:
                    dst, dk_ = qT[:, t0:t0 + n], ("qT", par, t0)
                elif j == 1:
                    dst, dk_ = kT[:, edst:edst + n], ("kT", par, edst)
                else:
                    dst, dk_ = vT[:, edst:edst + n], ("vT", par, edst)
                keys.append(dk_)
                if j != 1:
                    S.op("act", lambda e, dst=dst, pi=pi, n=n: e.copy(out=dst, in_=psb[pi][:, 0:n]),
                         reads=[("ps", pi)], writes=[dk_])
                else:
                    S.op("dve", lambda e, dst=dst, pi=pi, n=n: e.tensor_copy(out=dst, in_=psb[pi][:, 0:n]),
                         reads=[("ps", pi)], writes=[dk_])
        return keys

    def attn_group(hs, g, par, qkv_keys):
        d = DILS[g]
        hw = 64 * d
        NT = T + 2 * hw
        qT, kT, vT = qT2[par], kT2[par], vT2[par]
        if g == 0:
            S.op("pool", lambda e: e.memset(accO[:], 0.0), writes=["accO"])
            S.op("pool", lambda e: e.memset(accR[:], 0.0), writes=["accR"])
        tb = tbias[par]
        tbk = ("tbias", par)
        S.dma("sp", lambda e: e.dma_start(out=tb[:], in_=att_bias[g, hs]), writes=[tbk])
        L = T // d
        nq = L // 128
        kTv = kT[:, 0:NT].rearrange("p (u r) -> p u r", r=d)
        vTv = vT[:, 0:NT].rearrange("p (u r) -> p u r", r=d)
        qTv = qT[:].rearrange("p (u r) -> p u r", r=d)
        aOv = accO[:].rearrange("p (u r) -> p u r", r=d)
        aRv = accR[:].rearrange("p (u r) -> p u r", r=d)
        ctr = 0
        for r in range(d):
            for jk in range(nq + 1):
                b2 = ctr % 2
                ctr += 1
                qlo = max(jk - 1, 0)
                qhi = min(jk + 1, nq)
                nqq = (qhi - qlo) * 128
                c0 = 0 if jk >= 1 else 128
                S.op("pe", lambda e, jk=jk, r=r, qlo=qlo, nqq=nqq: e.matmul(
                    psb[2][:, 0:nqq], lhsT=kTv[:, jk * 128:(jk + 1) * 128, r],
                    rhs=qTv[:, qlo * 128:qlo * 128 + nqq, r], start=True, stop=True),
                    reads=qkv_keys, writes=[("ps", 2)])
                S.op("dve", lambda e, b2=b2, nqq=nqq, c0=c0: e.scalar_tensor_tensor(
                    out=sbuf_s[b2][:, 0:nqq], in0=psb[2][:, 0:nqq], scalar=SCALE, in1=tb[:, c0:c0 + nqq],
                    op0=ALU.mult, op1=ALU.add), reads=[("ps", 2), tbk], writes=[("sb", b2)])
                kc_ = kbcol0[g] + r * (nq + 1) + jk
                S.op("act", lambda e, b2=b2, nqq=nqq, kc_=kc_: e.activation(
                    out=pT[b2][:, 0:nqq], in_=sbuf_s[b2][:, 0:nqq], func=AF.Exp, bias=kb_t[:, kc_:kc_ + 1],
                    scale=1.0), reads=[("sb", b2), "kb"], writes=[("pT", b2)])
                S.op("pe", lambda e, jk=jk, r=r: e.transpose(
                    out=ptb[0][:, 0:128], in_=vTv[:, jk * 128:(jk + 1) * 128, r], identity=ident[:]),
                    reads=qkv_keys + ["ident"], writes=[("pt", 0)])
                S.op("act", lambda e, b2=b2: e.copy(out=vt[b2][:], in_=ptb[0][:, 0:128]),
                     reads=[("pt", 0)], writes=[("vt", b2)])
                for qi in range(qlo, qhi):
                    col = (qi - qlo) * 128
                    first = (qi == jk)
                    last = (qi == jk - 1)
                    ob = 3 + qi % 2
                    rb = 5 + qi % 2

                    def pv(e, ob=ob, rb=rb, b2=b2, col=col, first=first, last=last):
                        e.matmul(psb[ob][:, 0:128], lhsT=vt[b2][:], rhs=pT[b2][:, col:col + 128],
                                 start=first, stop=last)
                        return e.matmul(psb[rb][:, 0:128], lhsT=ones_b[:], rhs=pT[b2][:, col:col + 128],
                                        start=first, stop=last)
                    S.op("pe", pv, reads=[("vt", b2), ("pT", b2), "ones"], writes=[("ps", ob), ("ps", rb)])
                    if last:
                        S.op("dve", lambda e, ob=ob, qi=qi, r=r: e.tensor_tensor(
                            out=aOv[:, qi * 128:(qi + 1) * 128, r], in0=aOv[:, qi * 128:(qi + 1) * 128, r],
                            in1=psb[ob][:, 0:128], op=ALU.add), reads=[("ps", ob), "accO"], writes=["accO"])
                        S.op("dve", lambda e, rb=rb, qi=qi, r=r: e.tensor_tensor(
                            out=aRv[:, qi * 128:(qi + 1) * 128, r], in0=aRv[:, qi * 128:(qi + 1) * 128, r],
                            in1=psb[rb][:, 0:128], op=ALU.add), reads=[("ps", rb), "accR"], writes=["accR"])
        if g == 2:
            zp = hs % 2
            for i in range(4):
                b2 = i % 2
                S.op("dve", lambda e, b2=b2, i=i: e.reciprocal(out=rr[b2][:], in_=accR[:, i * 512:(i + 1) * 512]),
                     reads=["accR"], writes=[("rr", b2)])
                S.op("dve", lambda e, b2=b2, i=i: e.tensor_tensor(out=rr[b2][:], in0=rr[b2][:],
                                                                 in1=accO[:, i * 512:(i + 1) * 512], op=ALU.mult),
                     reads=["accO", ("rr", b2)], writes=[("rr", b2)])
                S.op("dve", lambda e, b2=b2, i=i, zp=zp: e.tensor_tensor(
                    out=yst[b2][:], in0=rr[b2][:], in1=zs2[zp][:, i * 512:(i + 1) * 512], op=ALU.mult),
                    reads=[("rr", b2), ("zs", zp, i)], writes=[("yst", b2)])
                S.dma("sp", lambda e, b2=b2, i=i: e.dma_start(
                    out=yaT_d[hs * 128:(hs + 1) * 128, i * 512:(i + 1) * 512], in_=yst[b2][:]),
                    reads=[("yst", b2)], writes=[("yaT_d", hs, i)])

    def merge_prop(a, b):
        la, lb = len(a), len(b)
        ia = ib = 0
        while ia < la or ib < lb:
            if ib >= lb or (ia < la and ia * lb <= ib * la):
                S.replay(a[ia:ia + 1])
                ia += 1
            else:
                S.replay(b[ib:ib + 1])
                ib += 1

    groups = [(hs, g) for hs in range(8) for g in range(3)]
    g_lists, a_lists, g_keys = [], [], []
    for gi, (hs, g) in enumerate(groups):
        S.rec_begin()
        g_keys.append(gemm_group(hs, g, gi % 2))
        g_lists.append(S.rec_end())
    for gi, (hs, g) in enumerate(groups):
        S.rec_begin()
        attn_group(hs, g, gi % 2, g_keys[gi])
        a_lists.append(S.rec_end())
    for step in range(len(groups) + 1):
        ga = g_lists[step] if step < len(groups) else []
        aa = a_lists[step - 1] if step >= 1 else []
        merge_prop(ga, aa)
    S.barrier()
    A.reset(base_mark)

    slb = [A.alloc([128, PAYW], F32, "slb") for _ in range(2)]
    Sin = [A.alloc([128, 4096], F32, "Sin") for _ in range(2)]
    U_t = A.alloc([128, 4096], F32, "U_t")
    lc = 0
    for di in range(2):
        S.op("dve", lambda e, di=di: e.memset(Sin[di][:], 0.0), writes=[("Sin", di)])
        for j in (range(4) if di == 0 else range(3, -1, -1)):
            sb_ = lc % 2
            lc += 1
            S.dma("sp", lambda e, sb_=sb_, j=j: e.dma_start(out=slb[sb_][:], in_=cout_d[j * 128:(j + 1) * 128, :]),
                  reads=["cout"], writes=[("slb", sb_)])
            for hk in range(8):
                S.op("dve", lambda e, sb_=sb_, di=di, hk=hk: e.scalar_tensor_tensor(
                    out=U_t[:, hk * 512:(hk + 1) * 512], in0=Sin[di][:, hk * 512:(hk + 1) * 512],
                    scalar=slb[sb_][:, 8192 + di * 8 + hk:8192 + di * 8 + hk + 1],
                    in1=slb[sb_][:, di * 4096 + hk * 512:di * 4096 + (hk + 1) * 512], op0=ALU.mult, op1=ALU.add),
                    reads=[("slb", sb_), ("Sin", di)], writes=[("U", hk)])
            S.op("pool", lambda e, di=di: e.tensor_tensor(out=U_t[:], in0=U_t[:], in1=Sin[di][:], op=ALU.subtract),
                 reads=[("U", hk) for hk in range(8)] + [("Sin", di)], writes=[("U", hk) for hk in range(8)])
            mc = 4 + di * 4 + j
            S.op("dve", lambda e, di=di, mc=mc: e.scalar_tensor_tensor(
                out=Sin[di][:], in0=U_t[:], scalar=smk_t[:, mc:mc + 1], in1=Sin[di][:], op0=ALU.mult, op1=ALU.add),
                reads=[("U", hk) for hk in range(8)] + [("Sin", di), "smk"], writes=[("Sin", di)])
        S.dma("sp", lambda e, di=di: e.dma_start(out=sin_d[di], in_=Sin[di][:]), reads=[("Sin", di)],
              writes=[("sin_d", di)])
    S.barrier()
    A.reset(base_mark)

    tri_t = A.alloc([128, 4, 128], F32, "tri")
    msk_t = A.alloc([128, 2, 128], F32, "msk")
    for i4 in range(4):
        S.dma("sp", lambda e, i4=i4: e.dma_start(out=tri_t[:, i4, :], in_=tri[i4]), writes=["tri"])
    for i2 in range(2):
        S.dma("sp", lambda e, i2=i2: e.dma_start(out=msk_t[:, i2, :], in_=msk[i2]), writes=["msk"])

    def gla_thread(h):
        X = psb[2 * h]
        Y = psb[2 * h + 1]
        kx = ("ps", 2 * h)
        ky = ("ps", 2 * h + 1)
        K_ = lambda name, *a: (name, h) + a
        lra = [A.alloc([64, 128], F32, "lra") for _ in range(3)]
        w2a = A.alloc([64, 256], F32, "w2a")
        SA = A.alloc([128, 2, 512], F32, "SA")
        SB = A.alloc([128, 2, 512], F32, "SB")
        Sbf = A.alloc([128, 2, 512], BF16, "Sbf")
        e_t = [A.alloc([128, 256], F32, "e_t") for _ in range(2)]
        sp_t = [A.alloc([128, 256], F32, "sp_t") for _ in range(2)]
        eb_t = [A.alloc([128, 2, 128], F32, "eb_t") for _ in range(2)]
        enb_t = [A.alloc([128, 2, 128], F32, "enb_t") for _ in range(2)]
        ekd_t = [A.alloc([128, 256], F32, "ekd_t") for _ in range(2)]
        kd_t = [A.alloc([128, 256], BF16, "kd_t") for _ in range(2)]
        qe_t = [A.alloc([128, 2, 128], BF16, "qe_t") for _ in range(2)]
        ke_t = [A.alloc([128, 2, 128], BF16, "ke_t") for _ in range(2)]
        AT_t = [A.alloc([128, 128], BF16, "AT_t") for _ in range(2)]
        ksc = [A.alloc([128, 256], F32, "ksc") for _ in range(3)]
        vsc = [A.alloc([128, 512], BF16, "vsc") for _ in range(3)]
        qsc = [A.alloc([128, 2, 128], F32, "qsc") for _ in range(3)]
        ktsc = [A.alloc([128, 2, 128], F32, "ktsc") for _ in range(3)]
        o_s = [A.alloc([128, 512], F32, "o_s") for _ in range(3)]
        cc = {"n": 0, "sc": 0, "o": 0}

        for b in range(3):
            S.op("dve", lambda e, b=b: e.memset(lra[b][:], 0.0), writes=[K_("lra", b)])
            S.op("dve", lambda e, b=b: e.memset(lra[b][32:33, :], 1.0), writes=[K_("lra", b)])
        S.op("dve", lambda e: e.memset(w2a[:], 0.0), writes=[K_("w2a")])

        def set_w2(w2src):
            S.dma("sp", lambda e: e.dma_start(out=w2a[0:33, :], in_=w2src[:, h * 256:(h + 1) * 256]),
                  writes=[K_("w2a")])

        def load_sc(lr_rows, kd_src, vd_src, tok0, sl):
            S.dma("sp", lambda e: e.dma_start(out=lra[sl][0:16, :], in_=lr_rows), writes=[K_("lra", sl)])
            S.dma("sp", lambda e: e.dma_start(out=ksc[sl][:], in_=kd_src[tok0:tok0 + 128, h * 256:(h + 1) * 256]),
                  writes=[K_("ksc", sl)])
            S.dma("sp", lambda e: e.dma_start(out=vsc[sl][:], in_=vd_src[tok0:tok0 + 128, h * 512:(h + 1) * 512]),
                  writes=[K_("vsc", sl)])

        def decays(sl, di, need_fm, b):
            S.op("pe", lambda e: e.matmul(X[:, 0:256], lhsT=lra[sl][0:33, :], rhs=w2a[0:33, :],
                                          start=True, stop=True), reads=[K_("lra", sl), K_("w2a")], writes=[kx])
            S.op("act", lambda e: e.activation(out=e_t[b][:], in_=X[:, 0:256], func=AF.Exp, scale=-1.0),
                 reads=[kx], writes=[K_("e_t", b)])
            S.op("act", lambda e: e.activation(out=sp_t[b][:], in_=e_t[b][:], func=AF.Ln, bias=1.0, scale=1.0),
                 reads=[K_("e_t", b)], writes=[K_("sp_t", b)])

            def bfm(e):
                ins = None
                for kk in range(2):
                    ins = e.matmul(X[:, kk * 128:(kk + 1) * 128], lhsT=sp_t[b][:, kk * 128:(kk + 1) * 128],
                                   rhs=tri_t[:, 2 * di, :], start=True, stop=True)
                return ins
            S.op("pe", bfm, reads=[K_("sp_t", b), "tri"], writes=[kx])
            S.op("act", lambda e: e.activation(out=eb_t[b][:], in_=X[:, 0:256].rearrange("p (k t) -> p k t", k=2),
                                               func=AF.Exp), reads=[kx], writes=[K_("eb_t", b)])
            if need_fm:
                S.op("act", lambda e: e.activation(out=enb_t[b][:], in_=X[:, 0:256].rearrange("p (k t) -> p k t", k=2),
                                                   func=AF.Exp, scale=-1.0), reads=[kx], writes=[K_("enb_t", b)])
            S.op("pe", lambda e: e.matmul(X[:, 256:512], lhsT=tri_t[:, 2 * di + 1, :], rhs=sp_t[b][:],
                                          start=True, stop=True), reads=[K_("sp_t", b), "tri"], writes=[kx])
            S.op("act", lambda e: e.activation(out=ekd_t[b][:], in_=X[:, 256:512], func=AF.Exp),
                 reads=[kx], writes=[K_("ekd_t", b)])
            S.op("dve", lambda e: e.tensor_tensor(out=kd_t[b][:], in0=ksc[sl][:], in1=ekd_t[b][:], op=ALU.mult),
                 reads=[K_("ksc", sl), K_("ekd_t", b)], writes=[K_("kd_t", b)])

        def state_update(St, skey, sl, di, b, cast):
            last = 127 if di == 0 else 0
            for kk in range(2):
                S.op("pe", lambda e, kk=kk: e.matmul(Y[:, 0:512], lhsT=kd_t[b][:, kk * 128:(kk + 1) * 128],
                                                      rhs=vsc[sl][:], start=True, stop=True),
                     reads=[K_("kd_t", b), K_("vsc", sl)], writes=[ky])
                S.op("dve", lambda e, kk=kk: e.scalar_tensor_tensor(
                    out=St[:, kk, :], in0=St[:, kk, :], scalar=eb_t[b][:, kk, last:last + 1], in1=Y[:, 0:512],
                    op0=ALU.mult, op1=ALU.add), reads=[ky, K_("eb_t", b), K_(skey, kk)], writes=[K_(skey, kk)])
            if cast:
                S.op("act", lambda e: e.copy(out=Sbf[:], in_=St[:]), reads=[K_(skey, 0), K_(skey, 1)],
                     writes=[K_("Sbf")])

        S.dma("sp", lambda e: e.dma_start(out=SA[:], in_=sin_d[0][:, h * 1024:(h + 1) * 1024].rearrange("p (k d) -> p k d", k=2)),
              writes=[K_("SA", 0), K_("SA", 1)])
        S.dma("sp", lambda e: e.dma_start(out=SB[:], in_=sin_d[1][:, h * 1024:(h + 1) * 1024].rearrange("p (k d) -> p k d", k=2)),
              writes=[K_("SB", 0), K_("SB", 1)])
        for di in range(2):
            St = SA if di == 0 else SB
            skey = "SA" if di == 0 else "SB"
            S.op("act", lambda e, St=St: e.copy(out=Sbf[:], in_=St[:]), reads=[K_(skey, 0), K_(skey, 1)],
                 writes=[K_("Sbf")])
            set_w2(w2aug[di])
            for s1 in (range(16) if di == 0 else range(15, -1, -1)):
                sl = cc["sc"] % 3
                cc["sc"] += 1
                tok0 = s1 * 128
                load_sc(lrT_d[16 * di:16 * di + 16, tok0:tok0 + 128], gk_d, gv_d, tok0, sl)
                S.dma("sp", lambda e, sl=sl, tok0=tok0: e.dma_start(
                    out=qsc[sl][:], in_=gqT_d[h * 256:(h + 1) * 256, tok0:tok0 + 128].rearrange("(k p) t -> p k t", p=128)),
                    writes=[K_("qsc", sl)])
                S.dma("sp", lambda e, sl=sl, tok0=tok0: e.dma_start(
                    out=ktsc[sl][:], in_=gkT_d[h * 256:(h + 1) * 256, tok0:tok0 + 128].rearrange("(k p) t -> p k t", p=128)),
                    writes=[K_("ktsc", sl)])
                b = cc["n"] % 2
                cc["n"] += 1
                decays(sl, di, True, b)
                S.op("dve", lambda e, b=b, sl=sl: e.tensor_tensor(out=qe_t[b][:], in0=qsc[sl][:], in1=eb_t[b][:],
                                                                  op=ALU.mult),
                     reads=[K_("qsc", sl), K_("eb_t", b)], writes=[K_("qe_t", b)])
                S.op("dve", lambda e, b=b, sl=sl: e.tensor_tensor(out=ke_t[b][:], in0=ktsc[sl][:], in1=enb_t[b][:],
                                                                  op=ALU.mult),
                     reads=[K_("ktsc", sl), K_("enb_t", b)], writes=[K_("ke_t", b)])

                def amm(e, b=b):
                    ins = None
                    for kk in range(2):
                        ins = e.matmul(X[:, 0:128], lhsT=ke_t[b][:, kk, :], rhs=qe_t[b][:, kk, :],
                                       start=(kk == 0), stop=(kk == 1))
                    return ins
                S.op("pe", amm, reads=[K_("qe_t", b), K_("ke_t", b)], writes=[kx])
                S.op("dve", lambda e, b=b, di=di: e.tensor_tensor(
                    out=AT_t[b][:], in0=X[:, 0:128], in1=msk_t[:, di, :], op=ALU.mult),
                    reads=[kx, "msk"], writes=[K_("AT_t", b)])

                def omm(e, b=b, sl=sl):
                    e.matmul(Y[:, 0:512], lhsT=AT_t[b][:], rhs=vsc[sl][:], start=True, stop=False)
                    e.matmul(Y[:, 0:512], lhsT=qe_t[b][:, 0, :], rhs=Sbf[:, 0, :], start=False, stop=False)
                    return e.matmul(Y[:, 0:512], lhsT=qe_t[b][:, 1, :], rhs=Sbf[:, 1, :], start=False, stop=True)
                S.op("pe", omm, reads=[K_("AT_t", b), K_("vsc", sl), K_("qe_t", b), K_("Sbf")], writes=[ky])
                ob_ = cc["o"] % 3
                cc["o"] += 1
                S.op("act", lambda e, ob_=ob_: e.copy(out=o_s[ob_][:], in_=Y[:, 0:512]), reads=[ky],
                     writes=[K_("o_s", ob_)])
                S.dma("pool", lambda e, ob_=ob_, tok0=tok0, di=di: e.dma_start(
                    out=o_d[di, tok0:tok0 + 128, h * 512:(h + 1) * 512], in_=o_s[ob_][:]),
                    reads=[K_("o_s", ob_)], writes=[("o_d", di, h, tok0)])
                state_update(St, skey, sl, di, b, True)

    threads = []
    for h in range(4):
        S.rec_begin()
        gla_thread(h)
        threads.append(S.rec_end())
    S.merge(threads)
    S.barrier()
    A.reset(base_mark)
    gn_t = A.alloc([128, 512], F32, "gn")
    S.dma("sp", lambda e: e.dma_start(out=gn_t[:], in_=gng_b), writes=["gn"])
    of_t = [A.alloc([128, 2048], F32, "of_t") for _ in range(2)]
    ob_t = [A.alloc([128, 2048], F32, "ob_t") for _ in range(2)]
    gz_t = [A.alloc([128, 2048], F32, "gz_t") for _ in range(2)]
    yb_t = [A.alloc([128, 2048], BF16, "yb_t") for _ in range(2)]
    junk2 = A.alloc([128, 512], BF16, "junk2")
    st2 = [A.alloc([128, 16], F32, "st2") for _ in range(2)]
    ybT_s = [A.alloc([128, 16, 512], BF16, "ybT_s") for _ in range(2)]
    for gi in range(4):
        ys = gi % 2
        for sub in range(4):
            b = (gi * 4 + sub) % 2
            r0 = gi * 512 + sub * 128
            S.dma("sp", lambda e, b=b, r0=r0: e.dma_start(out=of_t[b][:], in_=o_d[0, r0:r0 + 128, :]), writes=[("of", b)])
            S.dma("sp", lambda e, b=b, r0=r0: e.dma_start(out=ob_t[b][:], in_=o_d[1, r0:r0 + 128, :]), writes=[("ob", b)])
            S.dma("sp", lambda e, b=b, r0=r0: e.dma_start(out=gz_t[b][:], in_=gzs_d[r0:r0 + 128, :]), writes=[("gz", b)])
            S.op("pool", lambda e, b=b: e.tensor_tensor(out=of_t[b][:], in0=of_t[b][:], in1=ob_t[b][:], op=ALU.add),
                 reads=[("of", b), ("ob", b)], writes=[("of", b)])
            S.op("dve", lambda e, b=b: e.memset(st2[b][:], 0.0), writes=[("st2", b)])
            for hh in range(4):
                S.op("act", lambda e, b=b, hh=hh: e.activation(out=junk2[:], in_=of_t[b][:, hh * 512:(hh + 1) * 512],
                                                                func=AF.Square, accum_out=st2[b][:, hh:hh + 1]),
                     reads=[("of", b), ("st2", b)], writes=["junk2", ("st2", b)])
            S.op("dve", lambda e, b=b: e.tensor_scalar(out=st2[b][:, 4:8], in0=st2[b][:, 0:4], scalar1=1.0 / 512,
                                                        scalar2=EPS, op0=ALU.mult, op1=ALU.add),
                 reads=[("st2", b)], writes=[("st2", b)])
            S.op("act", lambda e, b=b: e.activation(out=st2[b][:, 8:12], in_=st2[b][:, 4:8], func=AF.Sqrt),
                 reads=[("st2", b)], writes=[("st2", b)])
            S.op("dve", lambda e, b=b: e.reciprocal(out=st2[b][:, 12:16], in_=st2[b][:, 8:12]),
                 reads=[("st2", b)], writes=[("st2", b)])
            for hh in range(4):
                S.op("dve", lambda e, b=b, hh=hh: e.scalar_tensor_tensor(
                    out=of_t[b][:, hh * 512:(hh + 1) * 512], in0=of_t[b][:, hh * 512:(hh + 1) * 512],
                    scalar=st2[b][:, 12 + hh:13 + hh], in1=gn_t[:], op0=ALU.mult, op1=ALU.mult),
                    reads=[("of", b), ("st2", b), "gn"], writes=[("of", b)])
            S.op("dve", lambda e, b=b: e.tensor_tensor(out=yb_t[b][:], in0=of_t[b][:], in1=gz_t[b][:], op=ALU.mult),
                 reads=[("of", b), ("gz", b)], writes=[("yb", b)])
            for q2 in range(2):
                def trf(e, b=b, q2=q2):
                    ins = None
                    for k in range(8):
                        kc = q2 * 8 + k
                        ins = e.transpose(out=ptb[0][:, k * 128:(k + 1) * 128], in_=yb_t[b][:, kc * 128:(kc + 1) * 128],
                                          identity=ident[:])
                    return ins
                S.op("pe", trf, reads=[("yb", b), "ident"], writes=[("pt", 0)])
                S.op("act", lambda e, ys=ys, q2=q2, sub=sub: e.copy(
                    out=ybT_s[ys][:, q2 * 8:(q2 + 1) * 8, sub * 128:(sub + 1) * 128],
                    in_=ptb[0][:].rearrange("p (k t) -> p k t", k=8)), reads=[("pt", 0)], writes=[("ybT_s", ys, sub, q2)])
        S.dma("sp", lambda e, ys=ys, gi=gi: e.dma_start(
            out=ybT_d[:, gi * 512:(gi + 1) * 512].rearrange("(k p) t -> p k t", p=128), in_=ybT_s[ys][:]),
            reads=[("ybT_s", ys, s_, q_) for s_ in range(4) for q_ in range(2)], writes=[("ybT_d", gi)])
    S.barrier()
    A.reset(base_mark)

    mergedT = A.alloc([128, 32, 512], BF16, "mergedT")
    ov_mark = A.mark()
    for tt in range(4):
        A.reset(ov_mark)
        hT1 = A.alloc([128, 32, 512], BF16, "hT1")
        yaT_t = A.alloc([128, 8, 512], BF16, "yaT_t")
        ybT_t = A.alloc([128, 16, 512], BF16, "ybT_t")
        wg = [[A.alloc([128, 32, 128], BF16, "wg") for _ in range(2)] for _ in range(2)]
        wa = [A.alloc([128, 8, 128], BF16, "wa") for _ in range(2)]
        wb_ = [A.alloc([128, 16, 128], BF16, "wb") for _ in range(2)]
        sg = [[A.alloc([128, 512], F32, "sg") for _ in range(2)] for _ in range(2)]
        mt = [A.alloc([128, 512], F32, "mt") for _ in range(2)]
        load_h(hT1, "hT1", [(tt, 0, 512)])
        S.dma("sp", lambda e, tt=tt: e.dma_start(
            out=yaT_t[:], in_=yaT_d[:, tt * 512:(tt + 1) * 512].rearrange("(k p) t -> p k t", p=128)), writes=["yaT_t"])
        S.dma("sp", lambda e, tt=tt: e.dma_start(
            out=ybT_t[:], in_=ybT_d[:, tt * 512:(tt + 1) * 512].rearrange("(k p) t -> p k t", p=128)), writes=["ybT_t"])
        for blk in range(32):
            b = blk % 2
            S.dma("pool", lambda e, b=b, blk=blk: e.dma_start(out=wg[b][0][:], in_=w_blk[B_MGA + blk]),
                  writes=[("wg", b, 0)])
            S.dma("pool", lambda e, b=b, blk=blk: e.dma_start(out=wg[b][1][:], in_=w_blk[B_MGB + blk]),
                  writes=[("wg", b, 1)])
            S.dma("pool", lambda e, b=b, blk=blk: e.dma_start(out=wa[b][:], in_=wua[blk]), writes=[("wa", b)])
            S.dma("pool", lambda e, b=b, blk=blk: e.dma_start(out=wb_[b][:], in_=wub[blk]), writes=[("wb", b)])
            for ab in range(2):
                pi = next_ps()
                S.op("pe", mm_fm(psb[pi][:, 0:512], wg[b][ab], hT1, 512), reads=[("wg", b, ab), "hT1"],
                     writes=[("ps", pi)])
                S.op("act", lambda e, b=b, ab=ab, pi=pi: e.activation(out=sg[b][ab][:], in_=psb[pi][:, 0:512],
                                                                      func=AF.Sigmoid),
                     reads=[("ps", pi)], writes=[("sg", b, ab)])
            pa = next_ps()
            S.op("pe", mm_fm(psb[pa][:, 0:512], wa[b], yaT_t, 512, nk=8), reads=[("wa", b), "yaT_t"],
                 writes=[("ps", pa)])
            S.op("dve", lambda e, b=b, pa=pa: e.tensor_tensor(out=mt[b][:], in0=sg[b][0][:], in1=psb[pa][:, 0:512],
                                                              op=ALU.mult),
                 reads=[("ps", pa), ("sg", b, 0)], writes=[("mt", b)])
            pbb = next_ps()
            S.op("pe", mm_fm(psb[pbb][:, 0:512], wb_[b], ybT_t, 512, nk=16), reads=[("wb", b), "ybT_t"],
                 writes=[("ps", pbb)])
            S.op("dve", lambda e, b=b, pbb=pbb: e.tensor_tensor(out=sg[b][1][:], in0=sg[b][1][:],
                                                                in1=psb[pbb][:, 0:512], op=ALU.mult),
                 reads=[("ps", pbb), ("sg", b, 1)], writes=[("sg", b, 1)])
            S.op("dve", lambda e, b=b, blk=blk: e.tensor_tensor(out=mergedT[:, blk, :], in0=mt[b][:],
                                                                in1=sg[b][1][:], op=ALU.add),
                 reads=[("mt", b), ("sg", b, 1)], writes=[("merged", blk)])
        S.barrier()
        A.reset(ov_mark)
        wo_t = [A.alloc([128, 32, 256], BF16, "wo_t") for _ in range(2)]
        ypre = [A.alloc([128, D], F32, "ypre") for _ in range(4)]
        xs = [A.alloc([128, 256], F32, "xs") for _ in range(4)]
        gs = [A.alloc([128, 512], F32, "gs") for _ in range(2)]
        st3 = A.alloc([128, 8], F32, "st3")
        st4 = A.alloc([128, 16], F32, "st4")
        junk3 = A.alloc([128, 512], BF16, "junk3")
        xc = 0
        for cb in range(16):
            b = cb % 2
            S.dma("pool", lambda e, b=b, cb=cb: e.dma_start(out=wo_t[b][:], in_=wo[cb]), writes=[("wo_t", b)])
            for sub in range(4):
                xi = xc % 4
                xc += 1
                r0 = tt * 512 + sub * 128
                S.dma("sp", lambda e, xi=xi, r0=r0, cb=cb: e.dma_start(out=xs[xi][:],
                                                                      in_=x_own[r0:r0 + 128, cb * 256:(cb + 1) * 256]),
                      writes=[("xs", xi)])
                pi = next_ps()

                def omm2(e, pi=pi, sub=sub, b=b):
                    ins = None
                    for kc in range(32):
                        ins = e.matmul(psb[pi][:, 0:256], lhsT=mergedT[:, kc, sub * 128:(sub + 1) * 128],
                                       rhs=wo_t[b][:, kc, :], start=(kc == 0), stop=(kc == 31))
                    return ins
                S.op("pe", omm2, reads=[("wo_t", b)] + [("merged", k_) for k_ in range(32)], writes=[("ps", pi)])
                S.op("dve", lambda e, pi=pi, sub=sub, cb=cb, xi=xi: e.tensor_tensor(
                    out=ypre[sub][:, cb * 256:(cb + 1) * 256], in0=psb[pi][:, 0:256], in1=xs[xi][:], op=ALU.add),
                    reads=[("ps", pi), ("xs", xi)], writes=[("ypre", sub, cb)])
        for sub in range(4):
            S.op("dve", lambda e: e.memset(st3[:], 0.0), writes=[("st3", c_) for c_ in range(8)])
            for c8 in range(8):
                S.op("act", lambda e, sub=sub, c8=c8: e.activation(
                    out=junk3[:], in_=ypre[sub][:, c8 * 512:(c8 + 1) * 512], func=AF.Square,
                    accum_out=st4[:, sub * 4 + 0:sub * 4 + 1] if False else st3[:, c8:c8 + 1]),
                    reads=[("ypre", sub, 2 * c8), ("ypre", sub, 2 * c8 + 1)], writes=["junk3", ("st3", c8)])
            S.op("dve", lambda e, sub=sub: e.tensor_reduce(out=st4[:, sub * 4:sub * 4 + 1], in_=st3[:, 0:8],
                                                           axis=mybir.AxisListType.X, op=ALU.add),
                 reads=[("st3", c_) for c_ in range(8)], writes=[("st4", sub, 0)])
            S.op("dve", lambda e, sub=sub: e.tensor_scalar(out=st4[:, sub * 4 + 1:sub * 4 + 2],
                                                            in0=st4[:, sub * 4:sub * 4 + 1], scalar1=1.0 / D,
                                                            scalar2=EPS, op0=ALU.mult, op1=ALU.add),
                 reads=[("st4", sub, 0)], writes=[("st4", sub, 1)])
            S.op("act", lambda e, sub=sub: e.activation(out=st4[:, sub * 4 + 2:sub * 4 + 3],
                                                        in_=st4[:, sub * 4 + 1:sub * 4 + 2], func=AF.Sqrt),
                 reads=[("st4", sub, 1)], writes=[("st4", sub, 2)])
            S.op("dve", lambda e, sub=sub: e.reciprocal(out=st4[:, sub * 4 + 3:sub * 4 + 4],
                                                        in_=st4[:, sub * 4 + 2:sub * 4 + 3]),
                 reads=[("st4", sub, 2)], writes=[("st4", sub, 3)])
            for c8 in range(8):
                gi = c8 % 2
                S.dma("sp", lambda e, gi=gi, c8=c8: e.dma_start(out=gs[gi][:], in_=fng_b[:, c8 * 512:(c8 + 1) * 512]),
                      writes=[("gs", gi)])
                S.op("dve", lambda e, sub=sub, c8=c8, gi=gi: e.scalar_tensor_tensor(
                    out=ypre[sub][:, c8 * 512:(c8 + 1) * 512], in0=ypre[sub][:, c8 * 512:(c8 + 1) * 512],
                    scalar=st4[:, sub * 4 + 3:sub * 4 + 4], in1=gs[gi][:], op0=ALU.mult, op1=ALU.mult),
                    reads=[("ypre", sub, 2 * c8), ("ypre", sub, 2 * c8 + 1), ("st4", sub, 3), ("gs", gi)],
                    writes=[("ypre", sub, 2 * c8), ("ypre", sub, 2 * c8 + 1)])
            r0 = tt * 512 + sub * 128
            S.dma("sp", lambda e, sub=sub, r0=r0: e.dma_start(out=y_out[r0:r0 + 128, :], in_=ypre[sub][:]),
                  reads=[("ypre", sub, c_) for c_ in range(16)], writes=[("y_out", r0)])
        S.barrier()
    S.emit()
    return nc


def _consts():
    slopes = np.exp2(-8.0 * (np.arange(8, dtype=np.float32) + 1.0) / 8).astype(np.float32)
    i = np.arange(128)[:, None]
    c = np.arange(256)[None, :]
    delta = i - c + 64
    att_bias = np.zeros((3, 8, 128, 256), np.float32)
    for g, d in enumerate(DILS):
        dist = (np.abs(delta) * d).astype(np.float32)
        for h in range(8):
            b = -slopes[h] * dist
            att_bias[g, h] = np.where(np.abs(delta) <= 64, b, np.float32(-30000.0))
    s = np.arange(128)[:, None]
    t = np.arange(128)[None, :]
    k = np.float32(-1.0 / 16.0)
    tri = np.zeros((4, 128, 128), np.float32)
    tri[0] = np.where(s <= t, k, 0)
    tri[1] = np.where(s > t, k, 0)
    tri[2] = np.where(s >= t, k, 0)
    tri[3] = np.where(s < t, k, 0)
    msk = np.zeros((2, 128, 128), np.float32)
    msk[0] = (s <= t)
    msk[1] = (s > t)
    return att_bias, tri, msk


def _prep_shared(inp):
    w_in = np.asarray(inp["w_in"])[0]
    cols = block_cols()
    idx = np.zeros(NBLK * 128, np.int64)
    valid = np.zeros(NBLK * 128, bool)
    for b, (c0, n) in enumerate(cols):
        idx[b * 128:b * 128 + n] = np.arange(c0, c0 + n)
        valid[b * 128:b * 128 + n] = True
    wsel = np.take(w_in, idx, axis=1)
    wsel[:, ~valid] = 0.0
    w_blk = np.ascontiguousarray(wsel.reshape(32, 128, NBLK, 128).transpose(2, 1, 0, 3))
    wua = np.ascontiguousarray(np.asarray(inp["w_up_a"])[0].reshape(8, 128, 32, 128).transpose(2, 1, 0, 3))
    wub = np.ascontiguousarray(np.asarray(inp["w_up_b"])[0].reshape(16, 128, 32, 128).transpose(2, 1, 0, 3))
    wo = np.ascontiguousarray(np.asarray(inp["w_o"])[0].reshape(32, 128, 16, 256).transpose(2, 1, 0, 3))
    att_bias, tri, msk = _consts()
    w2 = [np.asarray(inp["gla_w2_f"])[0], np.asarray(inp["gla_w2_b"])[0]]
    bb = [np.asarray(inp["gla_b_f"])[0], np.asarray(inp["gla_b_b"])[0]]
    w2aug = np.zeros((2, 33, 1024), np.float32)
    for di in range(2):
        w2aug[di, 0:16] = w2[di]
        w2aug[di, 32] = bb[di]
    wlr = [np.ascontiguousarray(w_in[:, 16384:16400].reshape(32, 128, 16).transpose(1, 0, 2)),
           np.ascontiguousarray(w_in[:, 16400:16416].reshape(32, 128, 16).transpose(1, 0, 2))]
    sh = dict(
        w_blk=w_blk, wua=wua, wub=wub, wo=wo,
        ng_b=np.ascontiguousarray(np.broadcast_to(np.asarray(inp["norm_g"])[0][None, :], (128, D))).astype(np.float32),
        fng_b=np.ascontiguousarray(np.broadcast_to(np.asarray(inp["final_norm_g"])[None, :], (128, D))).astype(np.float32),
        gng_b=np.ascontiguousarray(np.broadcast_to(np.asarray(inp["gla_norm_g"])[0][None, :], (128, 512))).astype(np.float32),
        w2aug=w2aug, att_bias=att_bias, tri=tri, msk=msk, ident_in=np.eye(128, dtype=np.float32))
    return sh, w2aug, wlr


def _core_inputs(sh, w2aug, wlr, seq, seg, nseg):
    x_own = seq[seg * T:(seg + 1) * T]
    x_halo = np.zeros((T, D), np.float32)
    if seg > 0:
        x_halo[0:1024] = seq[seg * T - 1024:seg * T]
    if seg < nseg - 1:
        x_halo[1024:2048] = seq[(seg + 1) * T:(seg + 1) * T + 1024]
    smask = np.zeros((128, 16), np.float32)
    if nseg > 1:
        smask[:, seg] = 1.0
        for j in range(4):
            if j < seg:
                smask[:, 4 + j] = 1.0
            if j > seg:
                smask[:, 8 + j] = 1.0
    kb = np.zeros((128, 69), np.float32)
    col = 0
    for g, d in enumerate(DILS):
        hw = 64 * d
        L = T // d
        nq = L // 128
        for r in range(d):
            for jk in range(nq + 1):
                u = jk * 128 + np.arange(128)
                p = r + d * u
                tglob = seg * T + (p - hw)
                ok = (tglob >= 0) & (tglob < nseg * T)
                kb[:, col] = np.where(ok, 0.0, -30000.0)
                col += 1
    d_ = dict(sh)
    d_.update(x_own=np.ascontiguousarray(x_own), x_halo=x_halo, att_kb=kb, smask=smask)
    return d_


_NC_CACHE = {}


def kernel(x_prompt, x_sample, norm_g, w_in, gla_w2_f, gla_b_f, gla_w2_b, gla_b_b, gla_norm_g,
           w_up_a, w_up_b, w_o, final_norm_g):
    inp = dict(w_in=w_in, w_up_a=w_up_a, w_up_b=w_up_b, w_o=w_o, norm_g=norm_g, final_norm_g=final_norm_g,
               gla_norm_g=gla_norm_g, gla_w2_f=gla_w2_f, gla_w2_b=gla_w2_b, gla_b_f=gla_b_f, gla_b_b=gla_b_b)
    sh, w2aug, wlr = _prep_shared(inp)
    xp = np.asarray(x_prompt, dtype=np.float32)
    xs = np.asarray(x_sample, dtype=np.float32)
    in_maps = []
    for c in range(4):
        in_maps.append(_core_inputs(sh, w2aug, wlr, xp[c], 0, 1))
    for c in range(4):
        in_maps.append(_core_inputs(sh, w2aug, wlr, xs[0], c, 4))
    if "nc" not in _NC_CACHE:
        _NC_CACHE["nc"] = build_program()
    nc = _NC_CACHE["nc"]
    res = run_bass_kernel_spmd(nc, in_maps, core_ids=list(range(8)))
    outs = [np.asarray(r["y_out"], dtype=np.float32) for r in res.results]
    y_prompt = np.stack(outs[0:4], axis=0)
    y_sample = np.concatenate(outs[4:8], axis=0)[None]
    return (y_prompt, y_sample)
```
